# Optimizing a Trainium2 kernel written in Bass

```python
import jax, jax.numpy as jnp
from jax import lax
import numpy as np

D_MODEL = 1024
BATCH = 8
SEQ = 2048
DEPTH = 1

D_MIX = D_MODEL
HG_HEADS = 4
HG_KEY = 128
HG_VAL = 128
HG_FDIM = HG_HEADS * HG_KEY
HG_VDIM = HG_HEADS * HG_VAL
HG_CHUNK = 64
MLA_HEADS = 4
MLA_Q_RANK = 384
MLA_KV_RANK = 128
MLA_NOPE = 128
MLA_ROPE = 64
MLA_VHEAD = 128
MLA_VDIM = MLA_HEADS * MLA_VHEAD
ROPE_THETA = 10000.0
Q_BLOCK = 128
IN_COLS = 3 * HG_FDIM + 2 * HG_VDIM + MLA_Q_RANK + MLA_KV_RANK + MLA_ROPE
D_FF = 2816
CONV_W = 3
DN_ALPHA = (2.0 * DEPTH) ** 0.25
DN_BETA = (8.0 * DEPTH) ** -0.25
NORM_EPS = 1e-5

kernel_name = "hymba_hgrn2_mla_convglu_deepnorm_encoder"


def _layernorm(x, g, b):
    xf = x.astype(jnp.float32)
    mu = jnp.mean(xf, -1, keepdims=True)
    var = jnp.mean(jnp.square(xf - mu), -1, keepdims=True)
    return ((xf - mu) * lax.rsqrt(var + NORM_EPS) * g + b).astype(x.dtype)


def _rmsnorm(x, g):
    xf = x.astype(jnp.float32)
    y = xf * lax.rsqrt(jnp.mean(xf * xf, -1, keepdims=True) + NORM_EPS)
    return (y * g).astype(x.dtype)


def _rope(x, cos, sin):
    half = MLA_ROPE // 2
    xf = x.astype(jnp.float32)
    x1, x2 = xf[..., :half], xf[..., half:]
    return jnp.concatenate([x1 * cos - x2 * sin, x2 * cos + x1 * sin], -1).astype(x.dtype)


def _gla_chunk_scan(q, k, v, g):
    B, H, S, K = q.shape
    V = v.shape[-1]
    n = S // HG_CHUNK

    def to_chunks(t):
        return t.astype(jnp.float32).reshape(B, H, n, HG_CHUNK, t.shape[-1]).transpose(2, 0, 1, 3, 4)

    qc, kc, vc, gc = to_chunks(q), to_chunks(k), to_chunks(v), to_chunks(g)
    mask = jnp.tril(jnp.ones((HG_CHUNK, HG_CHUNK), dtype=bool))[:, :, None]

    def step(state, inp):
        qb, kb, vb, gb = inp
        G = jnp.cumsum(gb, axis=2)
        o_inter = jnp.einsum('bhck,bhkv->bhcv', qb * jnp.exp(G), state)
        diff = G[:, :, :, None, :] - G[:, :, None, :, :]
        decay = jnp.exp(jnp.where(mask, diff, -jnp.inf))
        A = jnp.einsum('bhtk,bhsk,bhtsk->bhts', qb, kb, decay)
        o = o_inter + jnp.einsum('bhts,bhsv->bhtv', A, vb)
        G_last = G[:, :, -1:, :]
        new_state = jnp.exp(G_last[:, :, 0, :])[..., None] * state + jnp.einsum(
            'bhck,bhcv->bhkv', kb * jnp.exp(G_last - G), vb)
        return new_state, o

    s0 = jnp.zeros((B, H, K, V), jnp.float32)
    _, o = lax.scan(step, s0, (qc, kc, vc, gc))
    return o.transpose(1, 2, 0, 3, 4).reshape(B, H, S, V)


def _hgrn2_group(q_raw, i_raw, ff_raw, fb_raw, g_raw, lb_f, lb_b, norm_g):
    B, S, _ = q_raw.shape

    def heads(t, d):
        return t.reshape(B, S, HG_HEADS, d).transpose(0, 2, 1, 3)

    q = heads(jax.nn.silu(q_raw.astype(jnp.float32)), HG_KEY)
    v = heads(i_raw.astype(jnp.float32), HG_VAL)

    def gates(f_raw, lb):
        f = lb + (1.0 - lb) * jax.nn.sigmoid(f_raw.astype(jnp.float32))
        return heads(1.0 - f, HG_KEY), heads(jnp.log(f), HG_KEY)

    k_f, g_f = gates(ff_raw, lb_f)
    k_b, g_b = gates(fb_raw, lb_b)
    o_f = _gla_chunk_scan(q, k_f, v, g_f)
    flip = lambda t: jnp.flip(t, axis=2)
    o_b = flip(_gla_chunk_scan(flip(q), flip(k_b), flip(v), flip(g_b)))
    o = (o_f + o_b).transpose(0, 2, 1, 3)
    gate = g_raw.astype(jnp.float32).reshape(B, S, HG_HEADS, HG_VAL)
    o = _rmsnorm(o, norm_g) * jax.nn.silu(gate)
    return o.reshape(B, S, HG_VDIM).astype(q_raw.dtype)


def _block_attention(q_nope, q_pe, k_nope, k_pe, v):
    B, H, S, N = q_nope.shape
    nb = S // Q_BLOCK
    scale = (MLA_NOPE + MLA_ROPE) ** -0.5

    def blocks(t):
        return t.reshape(B, H, nb, Q_BLOCK, t.shape[-1]).transpose(2, 0, 1, 3, 4)

    def one(blk):
        qn, qr = blk
        s = (jnp.einsum('bhqn,bhkn->bhqk', qn, k_nope)
             + jnp.einsum('bhqr,bkr->bhqk', qr, k_pe)).astype(jnp.float32) * scale
        p = jax.nn.softmax(s, axis=-1)
        return jnp.einsum('bhqk,bhkd->bhqd', p.astype(v.dtype), v)

    o = lax.map(one, (blocks(q_nope), blocks(q_pe)))
    return o.transpose(1, 2, 0, 3, 4).reshape(B, H, S, v.shape[-1])


def _mla_group(qa, kva, kr, cos, sin, q_a_norm_g, w_q_b, kv_a_norm_g, w_kv_b, attn_norm_g):
    B, S, _ = qa.shape
    qh = (_rmsnorm(qa, q_a_norm_g) @ w_q_b).reshape(B, S, MLA_HEADS, MLA_NOPE + MLA_ROPE)
    q_nope = qh[..., :MLA_NOPE]
    q_pe = _rope(qh[..., MLA_NOPE:], cos[:, :, None, :], sin[:, :, None, :])
    kvh = (_rmsnorm(kva, kv_a_norm_g) @ w_kv_b).reshape(B, S, MLA_HEADS, MLA_NOPE + MLA_VHEAD)
    k_nope, v = kvh[..., :MLA_NOPE], kvh[..., MLA_NOPE:]
    k_pe = _rope(kr, cos, sin)
    t = lambda a: a.transpose(0, 2, 1, 3)
    o = _block_attention(t(q_nope), t(q_pe), t(k_nope), k_pe, t(v))
    o = _rmsnorm(t(o), attn_norm_g)
    return o.reshape(B, S, MLA_VDIM)


def _conv_glu_ffn(h, w_up, conv_w, conv_b, w_down):
    u = h @ w_up
    up = jnp.pad(u, ((0, 0), (1, 1), (0, 0)))
    c = up[:, :-2] * conv_w[0] + up[:, 1:-1] * conv_w[1] + up[:, 2:] * conv_w[2] + conv_b
    a, b = c[..., :D_FF], c[..., D_FF:]
    return (jax.nn.gelu(a) * b) @ w_down


def setup_inputs(seed: int = 0) -> dict:
    key = jax.random.key(seed)
    ks = jax.random.split(key, 24)
    f32 = jnp.float32
    nrm = lambda k, shape, s: jax.random.normal(k, shape, f32) * s
    gain = lambda k, shape: 1.0 + 0.02 * jax.random.normal(k, shape, f32)
    return {
        "x": jax.random.normal(ks[0], (BATCH, SEQ, D_MODEL), f32),
        "positions": jnp.broadcast_to(jnp.arange(SEQ, dtype=jnp.int32), (BATCH, SEQ)),
        "ln_in_g": gain(ks[1], (D_MODEL,)),
        "ln_in_b": nrm(ks[2], (D_MODEL,), 0.02),
        "w_in": nrm(ks[3], (DEPTH, D_MODEL, IN_COLS), D_MODEL ** -0.5),
        "lb_fwd": nrm(ks[4], (DEPTH + 1, HG_FDIM), 0.1),
        "lb_bwd": nrm(ks[5], (DEPTH + 1, HG_FDIM), 0.1),
        "hg_norm_g": gain(ks[6], (DEPTH, HG_VAL)),
        "q_a_norm_g": gain(ks[7], (DEPTH, MLA_Q_RANK)),
        "w_q_b": nrm(ks[8], (DEPTH, MLA_Q_RANK, MLA_HEADS * (MLA_NOPE + MLA_ROPE)), MLA_Q_RANK ** -0.5),
        "kv_a_norm_g": gain(ks[9], (DEPTH, MLA_KV_RANK)),
        "w_kv_b": nrm(ks[10], (DEPTH, MLA_KV_RANK, MLA_HEADS * (MLA_NOPE + MLA_VHEAD)), MLA_KV_RANK ** -0.5),
        "attn_norm_g": gain(ks[11], (DEPTH, MLA_VHEAD)),
        "w_out": nrm(ks[12], (DEPTH, D_MIX, D_MODEL), DN_BETA * D_MIX ** -0.5),
        "ln1_g": gain(ks[13], (DEPTH, D_MODEL)),
        "ln1_b": nrm(ks[14], (DEPTH, D_MODEL), 0.02),
        "w_up": nrm(ks[15], (DEPTH, D_MODEL, 2 * D_FF), D_MODEL ** -0.5),
        "conv_w": nrm(ks[16], (DEPTH, CONV_W, 2 * D_FF), CONV_W ** -0.5),
        "conv_b": nrm(ks[17], (DEPTH, 2 * D_FF), 0.02),
        "w_down": nrm(ks[18], (DEPTH, D_FF, D_MODEL), DN_BETA * D_FF ** -0.5),
        "ln2_g": gain(ks[19], (DEPTH, D_MODEL)),
        "ln2_b": nrm(ks[20], (DEPTH, D_MODEL), 0.02),
    }


def reference(x, positions, ln_in_g, ln_in_b, w_in, lb_fwd, lb_bwd, hg_norm_g,
              q_a_norm_g, w_q_b, kv_a_norm_g, w_kv_b, attn_norm_g, w_out, ln1_g, ln1_b,
              w_up, conv_w, conv_b, w_down, ln2_g, ln2_b):
    half = MLA_ROPE // 2
    inv_freq = 1.0 / (ROPE_THETA ** (jnp.arange(half, dtype=jnp.float32) / half))
    ang = positions.astype(jnp.float32)[..., None] * inv_freq
    cos, sin = jnp.cos(ang), jnp.sin(ang)
    lbs_f = jnp.cumsum(jax.nn.softmax(lb_fwd.astype(jnp.float32), axis=0), axis=0)
    lbs_b = jnp.cumsum(jax.nn.softmax(lb_bwd.astype(jnp.float32), axis=0), axis=0)
    split_at = np.cumsum([HG_FDIM, HG_VDIM, HG_FDIM, HG_FDIM, HG_VDIM,
                          MLA_Q_RANK, MLA_KV_RANK]).tolist()

    h = _layernorm(x, ln_in_g, ln_in_b)
    for l in range(DEPTH):
        proj = h @ w_in[l]
        q_raw, i_raw, ff_raw, fb_raw, g_raw, qa, kva, kr = jnp.split(proj, split_at, axis=-1)
        hg = _hgrn2_group(q_raw, i_raw, ff_raw, fb_raw, g_raw, lbs_f[l], lbs_b[l], hg_norm_g[l])
        at = _mla_group(qa, kva, kr, cos, sin, q_a_norm_g[l], w_q_b[l], kv_a_norm_g[l],
                        w_kv_b[l], attn_norm_g[l])
        mix = jnp.concatenate([hg, at.astype(hg.dtype)], axis=-1) @ w_out[l]
        h = _layernorm(DN_ALPHA * h + mix, ln1_g[l], ln1_b[l])
        ffn = _conv_glu_ffn(h, w_up[l], conv_w[l], conv_b[l], w_down[l])
        h = _layernorm(DN_ALPHA * h + ffn, ln2_g[l], ln2_b[l])
    return h
```

```python
import math
from contextlib import ExitStack

import numpy as np
import concourse.bass as bass
import concourse.mybir as mybir
from concourse.bass_utils import run_bass_kernel_spmd

F32 = mybir.dt.float32
BF16 = mybir.dt.bfloat16
I32 = mybir.dt.int32
AF = mybir.ActivationFunctionType
ALU = mybir.AluOpType

P = 128
S = 2048
NT = 16
D = 1024
KC = 8
DFF = 2816
NJ = 22
EPS = 1e-5
ALPHA = 2.0 ** 0.25
SCALE = 192.0 ** -0.5
TWO_PI = 2.0 * math.pi
NCOL = 16


class Src:
    def __init__(self, sem, name):
        self.sem = sem
        self.cnt = 0
        self.name = name


class Eng:
    def __init__(self, name, e, src):
        self.name = name
        self.e = e
        self.src = src
        self.waited = {}


class Trk:
    def __init__(self, nc, es):
        self.nc = nc
        self.es = es
        self.lastw = {}
        self.readers = {}
        self.srcs = []
        self.nsem = 0

    def new_src(self, name):
        sem = self.es.enter_context(self.nc.semaphore(name))
        s = Src(sem, name)
        self.srcs.append(s)
        return s

    def _wait(self, eng, src, c):
        if c <= 0:
            return
        if eng.waited.get(src, 0) >= c:
            return
        assert src.cnt >= c, (eng.name, src.name, src.cnt, c)
        eng.e.wait_ge(src.sem, c)
        eng.waited[src] = c

    def _deps(self, eng, reads, writes, own):
        deps = {}

        def add(s, c):
            if deps.get(s, 0) < c:
                deps[s] = c

        for k in reads:
            w = self.lastw.get(k)
            if w is not None:
                add(*w)
        for k in writes:
            w = self.lastw.get(k)
            if w is not None and w[0] is not own:
                add(*w)
            for s, c in self.readers.get(k, {}).items():
                if s is not own:
                    add(s, c)
        for s, c in deps.items():
            self._wait(eng, s, c)

    def _record(self, src, c, reads, writes):
        for k in reads:
            d = self.readers.setdefault(k, {})
            if d.get(src, 0) < c:
                d[src] = c
        for k in writes:
            self.lastw[k] = (src, c)
            self.readers[k] = {}

    @staticmethod
    def _excl(reads, writes):
        pr = [k for k in reads if isinstance(k, tuple) and k[0] == "ps"]
        if pr:
            reads = [k for k in reads if not (isinstance(k, tuple) and k[0] == "ps")]
            writes = list(writes) + pr
        return reads, writes

    def op(self, eng, fn, reads=(), writes=(), inc=True):
        reads, writes = self._excl(reads, writes)
        self._deps(eng, reads, writes, eng.src)
        ins = fn()
        if inc:
            eng.src.cnt += 1
            ins.then_inc(eng.src.sem, 1)
            c = eng.src.cnt
        else:
            c = eng.src.cnt + 1
        self._record(eng.src, c, reads, writes)
        return ins

    def dma(self, q, dsrc, out, in_, reads=(), writes=(), **kw):
        self._deps(q, reads, writes, None)
        ins = q.e.dma_start(out=out, in_=in_, **kw)
        dsrc.cnt += 16
        ins.then_inc(dsrc.sem, 16)
        self._record(dsrc, dsrc.cnt, reads, writes)
        return ins

    def barrier(self, engs):
        for e in engs:
            for s in self.srcs:
                self._wait(e, s, s.cnt)
        self.lastw = {}
        self.readers = {}


def build_nc(stage=99):
    nc = bass.Bass("TRN2", target_bir_lowering=False)

    def din(name, shape, dt=F32):
        return nc.dram_tensor(name, list(shape), dt, kind="ExternalInput").ap()

    x_d = din("x", [S, D])
    pos_d = din("pos", [1, S], I32)
    cols_d = din("cols", [P, NCOL])
    lb_d = din("lbr", [P, 16])
    lnv_d = din("lnv", [6, D])
    lnc_d = din("lnc", [P, 48])
    gqkv_d = din("gqkv", [1, 512])
    convp_d = din("convp", [P, NJ * 8])
    w_in_d = din("w_in_r", [P, KC, 3136])
    w_krsw_d = din("w_krsw", [P, KC, 64])
    w_qb_d = din("w_qb_r", [P, 3, 768])
    w_qbsw_d = din("w_qbsw", [P, 3, 256])
    w_kvk_d = din("w_kvk", [P, 512])
    w_kvv_d = din("w_kvv", [P, 512])
    w_out_d = din("w_out_r", [P, KC, D])
    w_up_d = din("w_up_r", [NJ, P, KC, 256])
    w_dn_d = din("w_dn_r", [P, NJ, D])
    y_d = nc.dram_tensor("y", [S, D], F32, kind="ExternalOutput").ap()
    if stage < 99:
        dbg_d = nc.dram_tensor("dbg", [P, 8192], F32, kind="ExternalOutput").ap()

    with ExitStack() as es:
        T = Trk(nc, es)
        PE = Eng("pe", nc.tensor, T.new_src("s_pe"))
        ACT = Eng("act", nc.scalar, T.new_src("s_act"))
        DVE = Eng("dve", nc.vector, T.new_src("s_dve"))
        POOL = Eng("pool", nc.gpsimd, T.new_src("s_pool"))
        SP = Eng("sp", nc.sync, T.new_src("s_sp"))
        ALL = [PE, ACT, DVE, POOL, SP]
        d_const = T.new_src("d_const")
        d_c2 = [T.new_src("d_c2_%d" % i) for i in range(6)]
        d_x = [T.new_src("d_x%d" % i) for i in range(10)]
        d_w = [T.new_src("d_w%d" % i) for i in range(14)]
        d_out = [T.new_src("d_o%d" % i) for i in range(6)]

        sbn = [0]

        def sb(name, shape, dt, stk=es):
            sbn[0] += 1
            return stk.enter_context(nc.sbuf_tensor("sb%d_%s" % (sbn[0], name), list(shape), dt))

        ps = es.enter_context(nc.psum_tensor("ps", [P, 8, 512], F32))
        psb = ps.bitcast(BF16)

        def BK(b):
            return [("ps", b)]

        ident_f = sb("ident_f", [P, P], F32)
        ident_b = sb("ident_b", [P, P], BF16)
        maskF = sb("maskF", [P, P], F32)
        maskB = sb("maskB", [P, P], F32)
        m0 = sb("m0", [P, 512], F32)
        cst = sb("cst", [P, 4], F32)
        cols = sb("cols", [P, NCOL], F32)
        lbr = sb("lbr", [P, 16], F32)
        ones_b = sb("ones_b", [P, P], BF16)
        lnst = sb("lnst", [P, NT, 2], F32)
        brow = sb("brow", [P, 2, D], BF16)
        lbT = sb("lbT", [P, 8], F32)
        omlb = sb("omlb", [P, 8], F32)
        lnc = sb("lnc", [P, 6, KC], F32)
        convp = sb("convp", [P, NJ, 2, 4], F32)
        stt = sb("stt", [P, 24, 12], F32)
        mv = sb("mv", [P, 24, 2], F32)
        sd = sb("sd", [P, 24], F32)
        rs = sb("rs", [P, 24], F32)
        nmr = sb("nmr", [P, 24], F32)
        hT = sb("hT", [P, KC, S], BF16)
        ybig = sb("ybig", [P, NT * D], F32)
        ybig_b = ybig.bitcast(BF16)

        def carve_f(off, n, parts=P):
            return ybig[0:parts, off:off + n]

        def carve_b(off, n, parts=P):
            return ybig_b[0:parts, 2 * off:2 * off + n]

        pc = es.enter_context(ExitStack())
        catT = sb("catT", [P, KC, S], BF16, pc)

        C_ALL = ("const",)
        T.dma(SP, d_const, cols[:], cols_d[:, :], writes=[C_ALL])
        T.dma(SP, d_const, lbr[:], lb_d[:, :], writes=[C_ALL])
        T.dma(SP, d_const, convp[:].rearrange("p j a c -> p (j a c)"), convp_d[:, :], writes=[C_ALL])
        T.dma(SP, d_const, lnc[:].rearrange("p r k -> p (r k)"), lnc_d[:, :], writes=[C_ALL])

        T.op(POOL, lambda: nc.gpsimd.memset(ident_f[:], 1.0), writes=["ident_f"])
        T.op(POOL, lambda: nc.gpsimd.affine_select(out=ident_f[:], in_=ident_f[:], pattern=[[-1, P]],
                                                     compare_op=ALU.is_equal, fill=0.0, base=0, channel_multiplier=1),
             reads=["ident_f"], writes=["ident_f"])
        T.op(POOL, lambda: nc.gpsimd.memset(cst[:, 0:1], EPS), writes=["cst"], inc=False)
        T.op(POOL, lambda: nc.gpsimd.memset(cst[:, 1:2], EPS / (ALPHA * ALPHA)), writes=["cst"], inc=False)
        T.op(POOL, lambda: nc.gpsimd.memset(cst[:, 3:4], 1.0), writes=["cst"], inc=False)
        T.op(POOL, lambda: nc.gpsimd.memset(cst[:, 2:3], 0.0), writes=["cst"])
        T.op(DVE, lambda: nc.vector.tensor_copy(out=ident_b[:], in_=ident_f[:]), reads=["ident_f"], writes=["ident_b"])

        def emit_late_setup():
            T.op(POOL, lambda: nc.gpsimd.memset(maskF[:], 1.0), writes=["maskF"])
            T.op(POOL, lambda: nc.gpsimd.affine_select(out=maskF[:], in_=maskF[:], pattern=[[1, P]],
                                                         compare_op=ALU.is_ge, fill=0.0, base=0, channel_multiplier=-1),
                 reads=["maskF"], writes=["maskF"])
            T.op(POOL, lambda: nc.gpsimd.memset(maskB[:], 1.0), writes=["maskB"])
            T.op(POOL, lambda: nc.gpsimd.affine_select(out=maskB[:], in_=maskB[:], pattern=[[-1, P]],
                                                         compare_op=ALU.is_ge, fill=0.0, base=0, channel_multiplier=1),
                 reads=["maskB"], writes=["maskB"])
            T.op(POOL, lambda: nc.gpsimd.memset(m0[:], 1.0), writes=["m0"])
            T.op(POOL, lambda: nc.gpsimd.memset(m0[:].rearrange("p (c t) -> p c t", t=P)[:, :, 0:1], 0.0), reads=["m0"], writes=["m0"])
            T.op(POOL, lambda: nc.gpsimd.memset(ones_b[:], 1.0), writes=["ones_b"])
            T.op(POOL, lambda: nc.gpsimd.memset(brow[:], 0.0), writes=["brow"])
            lb3 = lbr[:].rearrange("p (d r h) -> p d r h", d=2, r=2)
            T.op(DVE, lambda: nc.vector.tensor_tensor(out=omlb[:].rearrange("p (d h) -> p d h", d=2), in0=lb3[:, :, 0, :],
                                                      in1=lb3[:, :, 1, :], op=ALU.subtract), reads=[C_ALL], writes=["omlb"])
            T.op(ACT, lambda: nc.scalar.activation(out=lbT[:], in_=omlb[:], func=AF.Sigmoid), reads=["omlb"], writes=["lbT"])
            T.op(DVE, lambda: nc.vector.tensor_scalar(out=omlb[:], in0=lbT[:], scalar1=-1.0, scalar2=1.0, op0=ALU.mult, op1=ALU.add),
                 reads=["lbT"], writes=["omlb"])

        def rstd_from(var_ap, out_ap, rkeys, wkey, tmp_ap, tkey, eps_col=0):
            T.op(ACT, lambda: nc.scalar.activation(out=tmp_ap, in_=var_ap, func=AF.Ln, bias=cst[:, eps_col:eps_col + 1]),
                 reads=list(rkeys) + ["cst"], writes=[tkey])
            T.op(ACT, lambda: nc.scalar.activation(out=out_ap, in_=tmp_ap, func=AF.Exp, scale=-0.5), reads=[tkey], writes=[wkey])

        def ln_stats_a(src, key, slot):
            kst = ("st", slot)
            T.op(DVE, lambda: nc.vector.bn_stats(out=stt[:, slot, 0:6], in_=src[:, 0:512]), reads=[key], writes=[kst], inc=False)
            T.op(DVE, lambda: nc.vector.bn_stats(out=stt[:, slot, 6:12], in_=src[:, 512:1024]), reads=[key], writes=[kst])

        def ln_stats_b(slot):
            kst, kmv = ("st", slot), ("mv", slot)
            T.op(DVE, lambda: nc.vector.bn_aggr(out=mv[:, slot, :], in_=stt[:, slot, :]), reads=[kst], writes=[kmv])
            T.op(DVE, lambda: nc.vector.tensor_scalar(out=nmr[:, slot:slot + 1], in0=mv[:, slot, 0:1], scalar1=-1.0, scalar2=None, op0=ALU.mult),
                 reads=[kmv], writes=[("nm0", slot)])

        def ln_stats(src, key, slot):
            ln_stats_a(src, key, slot)
            ln_stats_b(slot)

        def ln_stats_act(src, key, slot, junk, jkey):
            kst, kmv = ("st", slot), ("mv", slot)
            T.op(ACT, lambda: nc.scalar.activation(out=junk, in_=src, func=AF.Identity, scale=1.0 / D, accum_out=stt[:, slot, 0:1]),
                 reads=[key], writes=[jkey, kst])
            T.op(ACT, lambda: nc.scalar.activation(out=junk, in_=src, func=AF.Square, scale=1.0 / math.sqrt(D), accum_out=stt[:, slot, 1:2]),
                 reads=[key], writes=[jkey, kst])
            T.op(DVE, lambda: nc.vector.tensor_copy(out=mv[:, slot, 0:1], in_=stt[:, slot, 0:1]), reads=[kst], writes=[kmv], inc=False)
            T.op(DVE, lambda: nc.vector.tensor_scalar(out=nmr[:, slot:slot + 1], in0=stt[:, slot, 0:1], scalar1=-1.0, scalar2=None, op0=ALU.mult),
                 reads=[kst], writes=[("nm0", slot)])
            T.op(DVE, lambda: nc.vector.scalar_tensor_tensor(out=mv[:, slot, 1:2], in0=stt[:, slot, 0:1], scalar=nmr[:, slot:slot + 1],
                                                             in1=stt[:, slot, 1:2], op0=ALU.mult, op1=ALU.add),
                 reads=[kst, ("nm0", slot)], writes=[kmv])

        def ln_apply_ops(src, key, slot, eps_col=0, keep=None):
            kmv, ksd, krs, knm = ("mv", slot), ("sd", slot), ("rs", slot), ("nmr", slot)
            rs_ap, nb_ap = rs[:, slot:slot + 1], sd[:, slot:slot + 1]
            tmp_ap, ktmp = sd[:, slot:slot + 1], ksd
            if keep is not None:
                rs_ap, nb_ap = keep[0][:, 0:1], keep[0][:, 1:2]
                krs = knm = keep[1]
            return [
                lambda: T.op(ACT, lambda: nc.scalar.activation(out=tmp_ap, in_=mv[:, slot, 1:2], func=AF.Ln, bias=cst[:, eps_col:eps_col + 1]),
                             reads=[kmv, "cst"], writes=[ktmp]),
                lambda: T.op(ACT, lambda: nc.scalar.activation(out=rs_ap, in_=tmp_ap, func=AF.Exp, scale=-0.5),
                             reads=[ktmp], writes=[krs]),
                lambda: T.op(ACT, lambda: nc.scalar.activation(out=nb_ap, in_=nmr[:, slot:slot + 1], func=AF.Identity, scale=rs_ap),
                             reads=[("nm0", slot), krs, ktmp], writes=[knm]),
                lambda: T.op(ACT, lambda: nc.scalar.activation(out=src, in_=src, func=AF.Identity, scale=rs_ap, bias=nb_ap),
                             reads=[key, krs, knm], writes=[key]),
            ]

        def ln_apply(src, key, slot, keep=None):
            for f in ln_apply_ops(src, key, slot, keep=keep):
                f()

        def ln_norm(src, key, slot):
            ln_stats(src, key, slot)
            ln_apply(src, key, slot)

        def transpose_pe(src, src_keys, banks):
            for hb in range(2):
                b = banks[hb]
                for q in range(4):
                    kc = hb * 4 + q
                    T.op(PE, lambda kc=kc, q=q, b=b: nc.tensor.transpose(out=ps[:, b, q * P:(q + 1) * P], in_=src[:, kc * P:(kc + 1) * P],
                                                                         identity=ident_f[:]),
                         reads=list(src_keys) + ["ident_f"], writes=BK(b), inc=(q == 3))

        def transpose_evac(dstT, name, i, banks, gi, all_act=False):
            for hb in range(2):
                b = banks[hb]
                for q in range(4):
                    kc = hb * 4 + q
                    if hb == 0 or all_act:
                        T.op(ACT, lambda kc=kc, q=q, b=b: nc.scalar.activation(out=dstT[:, kc, i * P:(i + 1) * P], in_=ps[:, b, q * P:(q + 1) * P],
                                                                               func=AF.Identity, scale=lnc[:, gi, kc:kc + 1], bias=lnc[:, gi + 1, kc:kc + 1]),
                             reads=BK(b) + [C_ALL], writes=[(name, i)])
                    else:
                        T.op(DVE, lambda kc=kc, q=q, b=b: nc.vector.tensor_scalar(out=dstT[:, kc, i * P:(i + 1) * P], in0=ps[:, b, q * P:(q + 1) * P],
                                                                                  scalar1=lnc[:, gi, kc:kc + 1], scalar2=lnc[:, gi + 1, kc:kc + 1],
                                                                                  op0=ALU.mult, op1=ALU.add),
                             reads=BK(b) + [C_ALL], writes=[(name, i)])

        def hkeys(name, t0, t1):
            return [(name, i) for i in range(t0 // P, (t1 + P - 1) // P)]

        def dbg_dump(ap_list):
            T.barrier(ALL)
            off = 0
            with nc.sbuf_tensor("dbgt", [P, 1024], F32) as dbgt:
                k = 0
                for ap, n in ap_list:
                    for c0 in range(0, n, 1024):
                        cl = min(1024, n - c0)
                        T.op(DVE, lambda: nc.vector.memset(dbgt[:], 0.0), writes=["dbgt"])
                        T.op(DVE, lambda: nc.vector.tensor_copy(out=dbgt[0:ap.shape[0], 0:cl], in_=ap[:, c0:c0 + cl]), writes=["dbgt"])
                        T.dma(SP, d_out[k % 2], dbg_d[:, off:off + cl], dbgt[:, 0:cl], reads=["dbgt"])
                        k += 1
                        off += cl
                T.dma(SP, d_out[k % 2], y_d[0:P, :], dbgt[:, 0:D], reads=["dbgt"])
                T.barrier(ALL)

        NXS = 8

        def pipeline(stages, n, lag=1):
            ns = len(stages)
            for step in range(n + lag * (ns - 1)):
                for k, st in enumerate(stages):
                    i = step - lag * k
                    if 0 <= i < n:
                        st(i)

        pw = es.enter_context(ExitStack())
        wmla = sb("wmla", [P, KC, 576], BF16, pw)
        wkrsw = sb("wkrsw", [P, KC, 64], BF16, pw)
        wqb = sb("wqb", [P, 3, 768], BF16, pw)
        wqbsw = sb("wqbsw", [P, 3, 256], BF16, pw)
        wkvk = sb("wkvk", [P, 512], BF16, pw)
        wkvv = sb("wkvv", [P, 512], BF16, pw)
        gqkvB = sb("gqkvB", [P, 512], F32, pw)
        T.dma(SP, d_c2[0], gqkvB[:], gqkv_d.partition_broadcast(P), writes=["gqkvB"])
        cosT = carve_f(8256, S, 64)
        ssinT = carve_f(10304, S, 64)

        with ExitStack() as p1:
            xt = sb("xta", [P, NXS, D], F32, p1)
            NPRE = 4
            for i in range(NPRE):
                T.dma(SP, d_x[i % NXS], xt[:, i % NXS, :], x_d[i * P:(i + 1) * P, :], writes=[("xt", i % NXS)])
            T.dma(POOL, d_w[0], wmla[:], w_in_d[:, :, 2560:3136], writes=["wmla"])
            T.dma(POOL, d_w[1], wkrsw[:], w_krsw_d[:, :, :], writes=["wkrsw"])
            for c3 in range(3):
                T.dma(POOL, d_w[2], wqb[:, c3, :], w_qb_d[:, c3, :], writes=["wqb"])
            T.dma(POOL, d_w[3], wqbsw[:], w_qbsw_d[:, :, :], writes=["wqbsw"])
            T.dma(POOL, d_w[4], wkvk[:], w_kvk_d[:, :], writes=["wkvk"])
            T.dma(POOL, d_w[5], wkvv[:], w_kvv_d[:, :], writes=["wkvv"])
            emit_late_setup()
            HS = S // 2
            posi = sb("posi", [64, HS], I32, p1)
            ang = sb("ang", [64, HS], F32, p1)
            kf = sb("kf", [64, HS], F32, p1)
            rope_ops = []
            for hv in range(2):
                hsl = slice(hv * HS, (hv + 1) * HS)
                rope_ops.append(lambda hsl=hsl, hv=hv: T.dma(SP, d_c2[1 + hv], posi[:], pos_d[:, hsl].partition_broadcast(64), writes=["posi"]))
                rope_ops.append(lambda: T.op(DVE, lambda: nc.vector.tensor_copy(out=ang[:], in_=posi[:]), reads=["posi"], writes=["ang"]))
                rope_ops.append(lambda: T.op(DVE, lambda: nc.vector.tensor_scalar(out=ang[:], in0=ang[:], scalar1=cols[0:64, 6:7], scalar2=None,
                                                                                  op0=ALU.mult), reads=["ang", C_ALL], writes=["ang"]))
                for which, dst in ((0, ssinT), (1, cosT)):
                    shift = 0.0 if which == 0 else math.pi / 2
                    a2 = dst[:, hsl]
                    ak = ("rope%d" % which, hv)
                    rope_ops.append(lambda shift=shift: T.op(DVE, lambda: nc.vector.tensor_scalar(out=kf[:], in0=ang[:], scalar1=shift, scalar2=1.0 / TWO_PI,
                                                                                                  op0=ALU.add, op1=ALU.mult), reads=["ang"], writes=["kf"]))
                    rope_ops.append(lambda: T.op(DVE, lambda: nc.vector.tensor_copy(out=posi[:], in_=kf[:]), reads=["kf"], writes=["posi"]))
                    rope_ops.append(lambda: T.op(DVE, lambda: nc.vector.tensor_copy(out=kf[:], in_=posi[:]), reads=["posi"], writes=["kf"]))
                    rope_ops.append(lambda a2=a2, ak=ak: T.op(DVE, lambda: nc.vector.scalar_tensor_tensor(out=a2, in0=kf[:], scalar=-TWO_PI, in1=ang[:],
                                                                                                         op0=ALU.mult, op1=ALU.add),
                                                              reads=["kf", "ang"], writes=[ak]))
                    rope_ops.append(lambda a2=a2, ak=ak, shift=shift: T.op(DVE, lambda: nc.vector.tensor_scalar(out=a2, in0=a2, scalar1=shift, scalar2=-math.pi,
                                                                                                               op0=ALU.add, op1=ALU.max),
                                                                           reads=[ak], writes=[ak]))
                    rope_ops.append(lambda a2=a2, ak=ak: T.op(DVE, lambda: nc.vector.tensor_scalar(out=a2, in0=a2, scalar1=math.pi, scalar2=None, op0=ALU.min),
                                                              reads=[ak], writes=[ak]))
                    rope_ops.append(lambda a2=a2, ak=ak: T.op(ACT, lambda: nc.scalar.activation(out=a2, in_=a2, func=AF.Sin), reads=[ak], writes=[ak]))
                rope_ops.append(lambda hsl=hsl, hv=hv: T.op(DVE, lambda: nc.vector.tensor_scalar(out=ssinT[:, hsl], in0=ssinT[:, hsl], scalar1=cols[0:64, 7:8],
                                                                                                 scalar2=None, op0=ALU.mult),
                                                            reads=[("rope0", hv), C_ALL], writes=[("rope0", hv)]))

            def rope_trickle(i):
                for _ in range(3):
                    if rope_ops:
                        rope_ops.pop(0)()

            def p1_group_pe(i):
                if i % 4 != 3:
                    return
                i0 = i - 3
                for kc in range(KC):
                    for j in range(4):
                        sx = (i0 + j) % NXS
                        T.op(PE, lambda kc=kc, j=j, sx=sx: nc.tensor.transpose(out=ps[:, kc, j * P:(j + 1) * P], in_=xt[:, sx, kc * P:(kc + 1) * P],
                                                                               identity=ident_f[:]),
                             reads=[("xt", sx), "ident_f"], writes=BK(kc), inc=(j == 3))

            def p1_group_evac(i):
                if i % 4 != 3:
                    return
                i0 = i - 3
                for kc in range(KC):
                    dst = hT[:, kc, i0 * P:(i0 + 4) * P]
                    wk = [("hT", i0 + j) for j in range(4)]
                    if kc % 2 == 0:
                        T.op(ACT, lambda kc=kc, dst=dst: nc.scalar.activation(out=dst, in_=ps[:, kc, :], func=AF.Identity, scale=lnc[:, 0, kc:kc + 1],
                                                                              bias=lnc[:, 1, kc:kc + 1]), reads=BK(kc) + [C_ALL], writes=wk)
                    else:
                        T.op(DVE, lambda kc=kc, dst=dst: nc.vector.tensor_scalar(out=dst, in0=ps[:, kc, :], scalar1=lnc[:, 0, kc:kc + 1],
                                                                                 scalar2=lnc[:, 1, kc:kc + 1], op0=ALU.mult, op1=ALU.add),
                             reads=BK(kc) + [C_ALL], writes=wk)

            pipeline([
                lambda i: (T.dma(SP, d_x[i % NXS], xt[:, i % NXS, :], x_d[i * P:(i + 1) * P, :], writes=[("xt", i % NXS)]) if i >= NPRE else None),
                lambda i: ln_stats_a(xt[:, i % NXS, :], ("xt", i % NXS), i % NXS),
                lambda i: ln_stats_b(i % NXS),
                lambda i: ln_apply(xt[:, i % NXS, :], ("xt", i % NXS), i % NXS, keep=(lnst[:, i, :], ("lnst", i))),
                p1_group_pe,
                p1_group_evac,
                rope_trickle,
            ], NT)
            while rope_ops:
                rope_ops.pop(0)()
            bt32 = carve_f(0, 2 * D, 64).rearrange("p (r d) -> p r d", d=D)
            bh16 = carve_b(2048, 2 * D, 64).rearrange("p (r d) -> p r d", d=D)
            fl = lambda t: t.rearrange("p r d -> p (r d)")
            T.op(POOL, lambda: nc.gpsimd.memset(fl(bt32), 0.0), writes=["bt32"])
            for r_, (lrow, prt) in enumerate(((1, 0), (1, 32), (3, 0), (3, 32))):
                T.dma(SP, d_c2[5], bt32[prt:prt + 1, r_ // 2, :], lnv_d[lrow:lrow + 1, :], reads=[], writes=["bt32"])
            T.op(DVE, lambda: nc.vector.tensor_scalar(out=fl(bt32), in0=fl(bt32), scalar1=ALPHA, scalar2=None, op0=ALU.mult), reads=["bt32"], writes=["bt32"])
            T.op(DVE, lambda: nc.vector.tensor_copy(out=fl(bh16), in_=fl(bt32)), reads=["bt32"], writes=["bh16"])
            T.op(DVE, lambda: nc.vector.tensor_tensor(out=fl(bt32), in0=fl(bt32), in1=fl(bh16), op=ALU.subtract), reads=["bt32", "bh16"], writes=["bt32"])
            T.op(DVE, lambda: nc.vector.tensor_copy(out=brow[0:32, :, :], in_=bh16[0:32, :, :]), reads=["bh16", "brow"], writes=["brow"])
            T.op(DVE, lambda: nc.vector.tensor_copy(out=brow[32:64, :, :], in_=bt32[32:64, :, :]), reads=["bt32", "brow"], writes=["brow"])

            T.barrier(ALL)
        if stage == 1:
            dbg_dump([(hT[:, kc, 0:1024], 1024) for kc in range(8)])
            return nc

        with ExitStack() as p2:
            qkT = carve_b(0, 4 * S).rearrange("p (c s) -> p c s", s=S)
            vext = carve_b(4096, NT * 4 * 130).rearrange("p (i h d) -> p i h d", h=4, d=130)
            kpeT = carve_b(12352, S)
            qTn = carve_b(13376, S)
            qTr = carve_b(14400, S)
            kTn2 = sb("kTn2", [P, 2, S], BF16, p2)
            qTn2 = sb("qTn2", [P, S], BF16, p2)
            qTr2 = sb("qTr2", [P, S], BF16, p2)
            PT = sb("PT", [P, 3, 512], BF16, p2)
            scr = sb("scr", [P, 512], BF16, p2)
            qn = sb("qn", [P, 4, 512], BF16, p2)
            ssq = sb("ssq", [P, 4, 2], F32, p2)
            sd2 = sb("sd2", [P, 4, 2], F32, p2)
            rs2 = sb("rs2", [P, 4, 2], F32, p2)
            rt1 = sb("rt1", [64, 512], F32, p2)
            rt2 = sb("rt2", [64, 512], F32, p2)
            fin = sb("fin", [P, 2, 8, 4], F32, p2)
            onb = sb("onb", [P, 4, P], BF16, p2)

            T.op(POOL, lambda: nc.gpsimd.memset(vext[:], 1.0), writes=["vext"])
            T.op(POOL, lambda: nc.gpsimd.memset(kpeT[64:128, :], 0.0), writes=["pe_pad"], inc=False)
            T.op(POOL, lambda: nc.gpsimd.memset(qTr2[64:128, :], 0.0), writes=["pe_pad"], inc=False)
            T.op(POOL, lambda: nc.gpsimd.memset(qTr[64:128, :], 0.0), writes=["pe_pad"])

            def rope_combine(bA, bB, dst, dkey, blk):
                sl = slice(blk * 512, (blk + 1) * 512)
                T.op(DVE, lambda: nc.vector.tensor_tensor(out=rt1[:], in0=ps[0:64, bA, :], in1=cosT[:, sl], op=ALU.mult),
                     reads=BK(bA), writes=["rt1"])
                T.op(DVE, lambda: nc.vector.tensor_tensor(out=rt2[:], in0=ps[0:64, bB, :], in1=ssinT[:, sl], op=ALU.mult),
                     reads=BK(bB), writes=["rt2"])
                T.op(POOL, lambda: nc.gpsimd.tensor_tensor(out=dst[0:64, sl], in0=rt1[:], in1=rt2[:], op=ALU.add),
                     reads=["rt1", "rt2"], writes=[dkey])

            kr_steps = []
            for blk in range(4):
                def kr_a(blk=blk):
                    sl = slice(blk * 512, (blk + 1) * 512)
                    for kc in range(KC):
                        T.op(PE, lambda kc=kc: nc.tensor.matmul(ps[0:64, 5, :], lhsT=wmla[:, kc, 512:576], rhs=hT[:, kc, sl],
                                                                start=(kc == 0), stop=(kc == KC - 1)),
                             reads=["wmla"] + hkeys("hT", blk * 512, blk * 512 + 512), writes=BK(5), inc=(kc == KC - 1))

                def kr_b(blk=blk):
                    sl = slice(blk * 512, (blk + 1) * 512)
                    for kc in range(KC):
                        T.op(PE, lambda kc=kc: nc.tensor.matmul(ps[0:64, 6, :], lhsT=wkrsw[:, kc, :], rhs=hT[:, kc, sl],
                                                                start=(kc == 0), stop=(kc == KC - 1)),
                             reads=["wkrsw"] + hkeys("hT", blk * 512, blk * 512 + 512), writes=BK(6), inc=(kc == KC - 1))

                def kr_c(blk=blk):
                    rope_combine(5, 6, kpeT, ("kpeT", blk), blk)
                kr_steps += [kr_a, kr_b, kr_c]

            def kr_trickle(i):
                if kr_steps:
                    kr_steps.pop(0)()

            inv384 = 1.0 / math.sqrt(384.0)
            inv128 = 1.0 / math.sqrt(128.0)
            def qa_s0(i):
                b = i % 3
                for kc in range(KC):
                    T.op(PE, lambda kc=kc: nc.tensor.matmul(ps[:, b, :], lhsT=hT[:, kc, i * P:(i + 1) * P], rhs=wmla[:, kc, 0:512],
                                                            start=(kc == 0), stop=(kc == KC - 1)),
                         reads=["wmla", ("hT", i)], writes=BK(b), inc=(kc == KC - 1))

            def qa_s1(i):
                b = i % 3
                s2 = i % 4
                T.op(ACT, lambda: nc.scalar.activation(out=scr[:, 0:384], in_=ps[:, b, 0:384], func=AF.Square, scale=inv384,
                                                       accum_out=ssq[:, s2, 0:1]), reads=BK(b), writes=["scr", ("ssq", s2)])
                T.op(ACT, lambda: nc.scalar.activation(out=scr[:, 384:512], in_=ps[:, b, 384:512], func=AF.Square, scale=inv128,
                                                       accum_out=ssq[:, s2, 1:2]), reads=BK(b), writes=["scr", ("ssq", s2)])
                rstd_from(ssq[:, s2, :], rs2[:, s2, :], [("ssq", s2)], ("rs2", s2), sd2[:, s2, :], ("sd2", s2))

            def qa_s2(i):
                b = i % 3
                s2 = i % 4
                T.op(DVE, lambda: nc.vector.scalar_tensor_tensor(out=qn[:, s2, 0:384], in0=ps[:, b, 0:384], scalar=rs2[:, s2, 0:1],
                                                                 in1=gqkvB[:, 0:384], op0=ALU.mult, op1=ALU.mult),
                     reads=BK(b) + [("rs2", s2), "gqkvB"], writes=[("qn", s2)], inc=False)
                T.op(DVE, lambda: nc.vector.scalar_tensor_tensor(out=qn[:, s2, 384:512], in0=ps[:, b, 384:512], scalar=rs2[:, s2, 1:2],
                                                                 in1=gqkvB[:, 384:512], op0=ALU.mult, op1=ALU.mult),
                     reads=BK(b) + [("rs2", s2), "gqkvB"], writes=[("qn", s2)])

            def qa_s3(i):
                bt = 3 + (i % 2)
                s2 = i % 4
                for c in range(4):
                    T.op(PE, lambda c=c: nc.tensor.transpose(out=psb[:, bt, c * P:(c + 1) * P], in_=qn[:, s2, c * P:(c + 1) * P],
                                                             identity=ident_b[:]),
                         reads=[("qn", s2), "ident_b"], writes=BK(bt), inc=(c == 3))

            def qa_s4(i):
                bt = 3 + (i % 2)
                T.op(DVE, lambda: nc.vector.tensor_copy(out=qkT[:, :, i * P:(i + 1) * P],
                                                        in_=psb[:, bt, 0:512].rearrange("p (c t) -> p c t", t=P)),
                     reads=BK(bt), writes=[("qkT", i)])

            pipeline([kr_trickle, qa_s0, qa_s1, qa_s2, qa_s3, qa_s4], NT)
            while kr_steps:
                kr_steps.pop(0)()

            if stage == 2:
                dbg_dump([(qkT[:, c, 0:1024], 1024) for c in range(4)] + [(kpeT[0:64, 0:1024], 1024), (cosT[:, 0:1024], 1024), (ssinT[:, 0:1024], 1024)])
                return nc

            for i in range(NT):
                b = i % 2
                T.op(PE, lambda: nc.tensor.matmul(ps[:, b, :], lhsT=qkT[:, 3, i * P:(i + 1) * P], rhs=wkvv[:, :], start=True, stop=True),
                     reads=["wkvv", ("qkT", i)], writes=BK(b))
                T.op(DVE, lambda: nc.vector.tensor_copy(out=vext[:, i, :, 0:128], in_=ps[:, b, :].rearrange("p (h d) -> p h d", d=P)),
                     reads=BK(b), writes=["vext"])

            qTn_s = [qTn, qTn2[:, :]]
            qTr_s = [qTr, qTr2[:, :]]
            kTn_s = [kTn2[:, 0, :], kTn2[:, 1, :]]

            def proj_steps(h, blk, banks):
                st = h % 2
                bq, br, bs_, bk = banks
                sl = slice(blk * 512, (blk + 1) * 512)
                qk_keys = hkeys("qkT", blk * 512, blk * 512 + 512)

                def s_q():
                    for c in range(3):
                        T.op(PE, lambda c=c: nc.tensor.matmul(ps[:, bq, :], lhsT=wqb[:, c, h * 192:h * 192 + 128], rhs=qkT[:, c, sl],
                                                              start=(c == 0), stop=(c == 2)),
                             reads=["wqb"] + qk_keys, writes=BK(bq), inc=(c == 2))
                    T.op(DVE, lambda: nc.vector.tensor_copy(out=qTn_s[st][:, sl], in_=ps[:, bq, :]), reads=BK(bq), writes=[("qTn", st, blk)])

                def s_r():
                    for c in range(3):
                        T.op(PE, lambda c=c: nc.tensor.matmul(ps[0:64, br, :], lhsT=wqb[:, c, h * 192 + 128:h * 192 + 192], rhs=qkT[:, c, sl],
                                                              start=(c == 0), stop=(c == 2)),
                             reads=["wqb"] + qk_keys, writes=BK(br), inc=(c == 2))
                    T.op(DVE, lambda: nc.vector.tensor_tensor(out=rt1[:], in0=ps[0:64, br, :], in1=cosT[:, sl], op=ALU.mult),
                         reads=BK(br), writes=["rt1"])

                def s_s():
                    for c in range(3):
                        T.op(PE, lambda c=c: nc.tensor.matmul(ps[0:64, bs_, :], lhsT=wqbsw[:, c, h * 64:(h + 1) * 64], rhs=qkT[:, c, sl],
                                                              start=(c == 0), stop=(c == 2)),
                             reads=["wqbsw"] + qk_keys, writes=BK(bs_), inc=(c == 2))
                    T.op(DVE, lambda: nc.vector.tensor_tensor(out=rt2[:], in0=ps[0:64, bs_, :], in1=ssinT[:, sl], op=ALU.mult),
                         reads=BK(bs_), writes=["rt2"])
                    T.op(POOL, lambda: nc.gpsimd.tensor_tensor(out=qTr_s[st][0:64, sl], in0=rt1[:], in1=rt2[:], op=ALU.add),
                         reads=["rt1", "rt2"], writes=[("qTr", st, blk)])

                def s_k():
                    T.op(PE, lambda: nc.tensor.matmul(ps[:, bk, :], lhsT=wkvk[:, h * P:(h + 1) * P], rhs=qkT[:, 3, sl], start=True, stop=True),
                         reads=["wkvk"] + qk_keys, writes=BK(bk))
                    T.op(DVE, lambda: nc.vector.tensor_copy(out=kTn_s[st][:, sl], in_=ps[:, bk, :]), reads=BK(bk), writes=[("kTn", st, blk)])
                return [s_q, s_r, s_s, s_k]

            def emit_proj(h, blk, banks):
                for f in proj_steps(h, blk, banks):
                    f()

            for blk in range(4):
                emit_proj(0, blk, (0, 2, 3, 1) if blk % 2 == 0 else (4, 6, 7, 5))
            if stage == 3:
                dbg_dump([(qTn[:, 0:1024], 1024), (qTr[0:64, 0:1024], 1024), (kTn2[:, 0, 0:1024], 1024),
                          (vext[:, 0, :, :].rearrange("p h d -> p (h d)"), 520)])
                return nc

            deferred = []
            for h in range(4):
                st = h % 2
                qTn_h, qTr_h, kTn_h = qTn_s[st], qTr_s[st], kTn_s[st]

                def emit_qk(n):
                    qb, kt = divmod(n, NT)
                    bs = n % 3
                    ksl = slice(kt * P, (kt + 1) * P)
                    qsl = slice(qb * 512, (qb + 1) * 512)
                    T.op(PE, lambda: nc.tensor.matmul(ps[:, bs, :], lhsT=kTn_h[:, ksl], rhs=qTn_h[:, qsl], start=True, stop=False),
                         reads=[("kTn", st, kt // 4), ("qTn", st, qb)], writes=BK(bs), inc=False)
                    T.op(PE, lambda: nc.tensor.matmul(ps[:, bs, :], lhsT=kpeT[:, ksl], rhs=qTr_h[:, qsl], start=False, stop=True),
                         reads=[("kpeT", kt // 4), ("qTr", st, qb), "pe_pad"], writes=BK(bs))

                def fin_parts(qb, h=h):
                    ob = 4 + 2 * (qb % 2)
                    qsl = slice(qb * 512, (qb + 1) * 512)
                    kf_ = ("fin", qb % 2)
                    fv = fin[:, qb % 2, :, :]

                    def part_a():
                        for qi in range(4):
                            bo = ob + qi // 2
                            o0 = (qi % 2) * 130
                            T.op(ACT, lambda qi=qi, bo=bo, o0=o0: nc.scalar.activation(out=scr[:, 0:128], in_=ps[:, bo, o0:o0 + 128], func=AF.Square,
                                                                                       scale=inv128, accum_out=fv[:, 0, qi:qi + 1]),
                                 reads=BK(bo), writes=["scr", kf_])
                        for qi in range(4):
                            bo = ob + qi // 2
                            o0 = (qi % 2) * 130
                            T.op(DVE, lambda qi=qi, bo=bo, o0=o0: nc.vector.tensor_copy(out=fv[:, 1, qi:qi + 1], in_=ps[:, bo, o0 + 128:o0 + 129]),
                                 reads=BK(bo), writes=[kf_])
                        T.op(DVE, lambda: nc.vector.tensor_tensor(out=fv[:, 2, :], in0=fv[:, 1, :], in1=fv[:, 1, :], op=ALU.mult), reads=[kf_], writes=[kf_])
                        T.op(DVE, lambda: nc.vector.scalar_tensor_tensor(out=fv[:, 3, :], in0=fv[:, 2, :], scalar=EPS, in1=fv[:, 0, :],
                                                                         op0=ALU.mult, op1=ALU.add), reads=[kf_], writes=[kf_])

                    def part_b():
                        T.op(ACT, lambda: nc.scalar.activation(out=fv[:, 4, :], in_=fv[:, 3, :], func=AF.Ln), reads=[kf_], writes=[kf_])
                        T.op(ACT, lambda: nc.scalar.activation(out=fv[:, 5, :], in_=fv[:, 4, :], func=AF.Exp, scale=-0.5), reads=[kf_], writes=[kf_])
                        for qi in range(4):
                            bo = ob + qi // 2
                            o0 = (qi % 2) * 130
                            T.op(DVE, lambda qi=qi, bo=bo, o0=o0: nc.vector.tensor_scalar(out=onb[:, qi, :], in0=ps[:, bo, o0:o0 + 128],
                                                                                          scalar1=fv[:, 5, qi:qi + 1], scalar2=None, op0=ALU.mult),
                                 reads=BK(bo) + [kf_], writes=[("onb", qi)])

                    def part_c():
                        for qi in range(4):
                            T.op(PE, lambda qi=qi: nc.tensor.transpose(out=psb[:, 3, qi * P:(qi + 1) * P], in_=onb[:, qi, :], identity=ident_b[:]),
                                 reads=[("onb", qi), "ident_b"], writes=BK(3), inc=(qi == 3))
                        T.op(DVE, lambda: nc.vector.tensor_scalar(out=catT[:, 4 + h, qsl], in0=psb[:, 3, 0:512], scalar1=cols[:, 4:5], scalar2=None,
                                                                  op0=ALU.mult),
                             reads=BK(3) + [C_ALL], writes=[("catT", 4 + h, qb)])
                    return part_a, part_b, part_c

                NIT = 4 * NT
                emit_qk(0)
                emit_qk(1)
                for n in range(NIT):
                    qb, kt = divmod(n, NT)
                    if n + 2 < NIT:
                        emit_qk(n + 2)
                    bs = n % 3
                    pslot = n % 3
                    T.op(ACT, lambda: nc.scalar.activation(out=PT[:, pslot, :], in_=ps[:, bs, :], func=AF.Exp, scale=SCALE),
                         reads=BK(bs), writes=[("PT", pslot)])
                    ob = 4 + 2 * (qb % 2)
                    for qi in range(4):
                        bo = ob + qi // 2
                        o0 = (qi % 2) * 130
                        T.op(PE, lambda qi=qi, bo=bo, o0=o0: nc.tensor.matmul(ps[:, bo, o0:o0 + 130], lhsT=PT[:, pslot, qi * P:(qi + 1) * P],
                                                                              rhs=vext[:, kt, h, :], start=(kt == 0 and qi % 2 == 0),
                                                                              stop=(kt == NT - 1), skip_group_check=True),
                             reads=[("PT", pslot), "vext"], writes=BK(bo), inc=(qi % 2 == 1))
                    gn = h * NIT + n
                    for item in [d_ for d_ in deferred if d_[0] <= gn]:
                        item[1]()
                        deferred.remove(item)
                    if kt == NT - 1:
                        pa, pb_, pc_ = fin_parts(qb)
                        deferred.append((gn + 2, pa))
                        deferred.append((gn + 4, pb_))
                        deferred.append((gn + 7, pc_))
                    if h + 1 < 4 and kt in (3, 6, 9, 12):
                        proj_steps(h + 1, qb, (3, 3, 3, 3))[kt // 3 - 1]()
            while deferred:
                item = deferred.pop(0)
                item[1]()
            T.barrier(ALL)
        pw.close()
        if stage == 4:
            dbg_dump([(catT[:, 4 + h, 0:1024], 1024) for h in range(4)])
            return nc

        for rnd in range(2):
            with ExitStack() as p3:
                qsT = carve_f(0, 2 * S).rearrange("p (h s) -> p h s", s=S)
                opart = carve_f(4096, NT * 256).rearrange("p (i v) -> p i v", v=256)
                qtT = carve_b(8192, 4 * S).rearrange("p (c s) -> p c s", s=S)
                ktT = carve_b(12288, 4 * S).rearrange("p (c s) -> p c s", s=S)
                vtok = sb("vtok", [P, NT, 256], BF16, p3)
                sg = sb("sg", [P, NT, 256], BF16, p3)
                dA = sb("dA", [P, 4, NT], F32, p3)
                dB = sb("dB", [P, 4, NT], F32, p3)
                dM = sb("dM", [P, 4, NT], F32, p3)
                pg = p3.enter_context(ExitStack())
                wrf = sb("wrf", [P, 2, KC, 256], BF16, pg)
                pq = pg.enter_context(ExitStack())
                wrq = sb("wrq", [P, 3, KC, 256], BF16, pq)
                for gi, c0 in ((1, 512), (2, 2048), (0, 0)):
                    T.dma(POOL, d_w[gi], wrq[:, gi, :, :], w_in_d[:, :, c0 + rnd * 256:c0 + rnd * 256 + 256], writes=[("wrq", gi)])
                for gi, c0 in enumerate((1024, 1536)):
                    T.dma(POOL, d_w[3 + gi], wrf[:, gi, :, :], w_in_d[:, :, c0 + rnd * 256:c0 + rnd * 256 + 256], writes=[("wrf", gi)])

                def vg_mm(i):
                    for half, gi in ((0, 1), (1, 2)):
                        b = 2 * half + i % 2
                        for kc in range(KC):
                            T.op(PE, lambda kc=kc, b=b, gi=gi: nc.tensor.matmul(ps[:, b, 0:256], lhsT=hT[:, kc, i * P:(i + 1) * P], rhs=wrq[:, gi, kc, :],
                                                                              start=(kc == 0), stop=(kc == KC - 1)),
                                 reads=[("wrq", gi), ("hT", i)], writes=BK(b), inc=(kc == KC - 1))

                def vg_ev(i):
                    T.op(DVE, lambda: nc.vector.tensor_copy(out=vtok[:, i, :], in_=ps[:, i % 2, 0:256]), reads=BK(i % 2), writes=[("vtok", i)])
                    T.op(ACT, lambda: nc.scalar.activation(out=sg[:, i, :], in_=ps[:, 2 + i % 2, 0:256], func=AF.Silu), reads=BK(2 + i % 2), writes=[("sg", i)])

                pipeline([vg_mm, vg_ev], NT)

                def q_mm(n):
                    hh, blk = divmod(n, 4)
                    sl = slice(blk * 512, (blk + 1) * 512)
                    b = 6 + n % 2
                    for kc in range(KC):
                        T.op(PE, lambda kc=kc: nc.tensor.matmul(ps[:, b, :], lhsT=wrq[:, 0, kc, hh * P:(hh + 1) * P], rhs=hT[:, kc, sl],
                                                                start=(kc == 0), stop=(kc == KC - 1)),
                             reads=[("wrq", 0)] + hkeys("hT", blk * 512, blk * 512 + 512), writes=BK(b), inc=(kc == KC - 1))

                def q_ev(n):
                    hh, blk = divmod(n, 4)
                    sl = slice(blk * 512, (blk + 1) * 512)
                    b = 6 + n % 2
                    T.op(ACT, lambda: nc.scalar.activation(out=qsT[:, hh, sl], in_=ps[:, b, :], func=AF.Silu), reads=BK(b), writes=[("qsT", hh, blk)])

                pipeline([q_mm, q_ev], 8)
                T.barrier(ALL)
                pq.close()

                ge = sb("ge", [P, 2, 512], F32, pg)
                gl2 = sb("gl2", [P, 2, 512], F32, pg)
                gl1 = sb("gl1", [P, 2, 512], F32, pg)
                gsp = sb("gsp", [P, 2, 512], F32, pg)
                gsn = sb("gsn", [P, 3, 512], F32, pg)
                gG = sb("gG", [P, 512], F32, pg)
                grel = sb("grel", [P, 2, 512], F32, pg)
                grl2 = sb("grl2", [P, 2, 512], F32, pg)
                gE1 = sb("gE1", [P, 2, 512], F32, pg)
                gE2 = sb("gE2", [P, 2, 512], F32, pg)
                gsm = sb("gsm", [P, 2, 16], F32, pg)

                def piece(n):
                    d, r = divmod(n, 8)
                    hh, blk = divmod(r, 4)
                    return d, hh, blk, d * 2 + hh, d * 4 + rnd * 2 + hh

                def g_s0(n):
                    d, hh, blk, ci, lcol = piece(n)
                    sl = slice(blk * 512, (blk + 1) * 512)
                    b = 4 + n % 2
                    for kc in range(KC):
                        T.op(PE, lambda kc=kc: nc.tensor.matmul(ps[:, b, :], lhsT=wrf[:, d, kc, hh * P:(hh + 1) * P], rhs=hT[:, kc, sl],
                                                                start=(kc == 0), stop=(kc == KC - 1)),
                             reads=[("wrf", d)] + hkeys("hT", blk * 512, blk * 512 + 512), writes=BK(b), inc=(kc == KC - 1))

                def g_s1_ops(n):
                    d, hh, blk, ci, lcol = piece(n)
                    b = 4 + n % 2
                    s2 = n % 2
                    return [
                        lambda: T.op(ACT, lambda: nc.scalar.activation(out=ge[:, s2, :], in_=ps[:, b, :], func=AF.Exp, scale=-1.0), reads=BK(b), writes=[("ge", s2)]),
                        lambda: T.op(ACT, lambda: nc.scalar.activation(out=gl2[:, s2, :], in_=ge[:, s2, :], func=AF.Ln, bias=cst[:, 3:4]),
                                     reads=[("ge", s2), "cst"], writes=[("gl2", s2)]),
                        lambda: T.op(ACT, lambda: nc.scalar.activation(out=gsp[:, s2, :], in_=gl2[:, s2, :], func=AF.Exp, scale=-1.0),
                                     reads=[("gl2", s2)], writes=[("gsp", s2)]),
                        lambda: T.op(ACT, lambda: nc.scalar.activation(out=gl1[:, s2, :], in_=gsp[:, s2, :], func=AF.Ln, scale=omlb[:, lcol:lcol + 1],
                                                                       bias=lbT[:, lcol:lcol + 1]), reads=[("gsp", s2), "lbT", "omlb"], writes=[("gl1", s2)]),
                    ]

                def g_s2(n):
                    d, hh, blk, ci, lcol = piece(n)
                    s2 = n % 2
                    s3 = n % 3
                    g_ap = gl1[:, s2, :]
                    rel = grel[:, s2, :]
                    R3 = rel.rearrange("p (c t) -> p c t", t=P)
                    rkey = ("grel", s2)
                    sm = gsm[:, s2, :]
                    if d == 0:
                        T.op(DVE, lambda: nc.vector.tensor_tensor_scan(out=rel, data0=m0[:], data1=g_ap, initial=0.0, op0=ALU.mult, op1=ALU.add),
                             reads=[("gl1", s2), "m0"], writes=[rkey])
                        G3 = R3
                        gkey = rkey
                    else:
                        T.op(DVE, lambda: nc.vector.tensor_tensor_scan(out=gG[:], data0=m0[:], data1=g_ap, initial=0.0, op0=ALU.mult, op1=ALU.add),
                             reads=[("gl1", s2), "m0"], writes=["gG"])
                        G3 = gG[:].rearrange("p (c t) -> p c t", t=P)
                        gkey = "gG"
                    T.op(DVE, lambda: nc.vector.tensor_scalar(out=gsn[:, s3, :], in0=gsp[:, s2, :], scalar1=-1.0, scalar2=1.0, op0=ALU.mult, op1=ALU.add),
                         reads=[("gsp", s2)], writes=[("gsn", s3)])
                    if d == 1:
                        T.op(DVE, lambda: nc.vector.tensor_tensor(out=rel, in0=gG[:], in1=g_ap, op=ALU.subtract), reads=["gG", ("gl1", s2)], writes=[rkey])
                    T.op(DVE, lambda: nc.vector.tensor_copy(out=sm[:, 0:4].unsqueeze(2), in_=G3[:, :, 127:128]), reads=[gkey], writes=[("gsm", s2)])
                    T.op(DVE, lambda: nc.vector.tensor_copy(out=sm[:, 4:8].unsqueeze(2), in_=R3[:, :, 64:65]), reads=[rkey], writes=[("gsm", s2)])
                    T.op(DVE, lambda: nc.vector.tensor_tensor(out=grl2[:, s2, :].rearrange("p (c t) -> p c t", t=P), in0=R3,
                                                              in1=R3[:, :, 64:65].broadcast_to([P, 4, P]), op=ALU.subtract),
                         reads=[rkey], writes=[("grl2", s2)])
                    T.op(DVE, lambda: nc.vector.tensor_tensor(out=sm[:, 8:12], in0=sm[:, 0:4], in1=sm[:, 4:8], op=ALU.subtract),
                         reads=[("gsm", s2)], writes=[("gsm", s2)])

                def g_s3_ops(n):
                    d, hh, blk, ci, lcol = piece(n)
                    s2 = n % 2
                    cs = slice(blk * 4, blk * 4 + 4)
                    sm = gsm[:, s2, :]
                    sgn = 1.0 if d == 0 else -1.0
                    return [
                        lambda: T.op(ACT, lambda: nc.scalar.activation(out=gE1[:, s2, :], in_=grl2[:, s2, :], func=AF.Exp, scale=sgn),
                                     reads=[("grl2", s2)], writes=[("gE1", s2)]),
                        lambda: T.op(ACT, lambda: nc.scalar.activation(out=dA[:, ci, cs], in_=sm[:, 0:4], func=AF.Exp), reads=[("gsm", s2)], writes=[("dA", ci, blk)]),
                        lambda: T.op(ACT, lambda: nc.scalar.activation(out=gE2[:, s2, :], in_=grl2[:, s2, :], func=AF.Exp, scale=-sgn),
                                     reads=[("grl2", s2)], writes=[("gE2", s2)]),
                        lambda: T.op(ACT, lambda: nc.scalar.activation(out=(dM if d == 0 else dB)[:, ci, cs], in_=sm[:, 4:8], func=AF.Exp),
                                     reads=[("gsm", s2)], writes=[("dMB", ci, blk)]),
                        lambda: T.op(ACT, lambda: nc.scalar.activation(out=(dB if d == 0 else dM)[:, ci, cs], in_=sm[:, 8:12], func=AF.Exp),
                                     reads=[("gsm", s2)], writes=[("dBM", ci, blk)]),
                    ]

                def g_s13(step_n):
                    o1 = g_s1_ops(step_n) if step_n < 16 else []
                    o3 = g_s3_ops(step_n - 2) if 0 <= step_n - 2 < 16 else []
                    while o1 or o3:
                        if o1:
                            o1.pop(0)()
                        if o3:
                            o3.pop(0)()

                def g_s4(n):
                    d, hh, blk, ci, lcol = piece(n)
                    s2 = n % 2
                    s3 = n % 3
                    sl = slice(blk * 512, (blk + 1) * 512)
                    T.op(DVE, lambda: nc.vector.tensor_tensor(out=qtT[:, ci, sl], in0=qsT[:, hh, sl], in1=gE1[:, s2, :], op=ALU.mult),
                         reads=[("qsT", hh, blk), ("gE1", s2)], writes=[("qtT", ci, blk)])
                    T.op(DVE, lambda: nc.vector.scalar_tensor_tensor(out=ktT[:, ci, sl], in0=gsn[:, s3, :], scalar=omlb[:, lcol:lcol + 1],
                                                                     in1=gE2[:, s2, :], op0=ALU.mult, op1=ALU.mult),
                         reads=[("gsn", s3), ("gE2", s2), "omlb"], writes=[("ktT", ci, blk)])

                for gstep in range(16 + 5):
                    if gstep < 16:
                        g_s0(gstep)
                    if 0 <= gstep - 1 < 18:
                        g_s13(gstep - 1)
                    if 0 <= gstep - 2 < 16:
                        g_s2(gstep - 2)
                    if 0 <= gstep - 4 < 16:
                        g_s4(gstep - 4)
                if stage == 5 and rnd == 0:
                    dbg_dump([(qtT[:, 0, 0:512], 512), (ktT[:, 0, 0:512], 512), (qtT[:, 2, 0:512], 512), (ktT[:, 2, 0:512], 512),
                              (dA[:, :, :].rearrange("p a b -> p (a b)"), 64), (dB[:, :, :].rearrange("p a b -> p (a b)"), 64),
                              (dM[:, :, :].rearrange("p a b -> p (a b)"), 64), (qsT[:, 0, 0:512], 512), (vtok[:, 0, :], 256), (sg[:, 0, :], 256)])
                    return nc

                T.barrier(ALL)
                pg.close()
                St = sb("St", [P, 4, P], F32, p3)
                Stmp = sb("Stmp", [P, 4, P], F32, p3)
                Sb = sb("Sb", [P, 2, 4, P], BF16, p3)
                osum = sb("osum", [P, 4, 4, P], F32, p3)
                onh = sb("onh", [P, 2, 4, P], BF16, p3)
                fh = sb("fh", [P, 4, 4, 4], F32, p3)
                scr3 = sb("scr3", [P, P], BF16, p3)
                Am_all = sb("Am_all", [P, 4, NT, P], BF16, p3)
                ktok_all = sb("ktok_all", [P, 4, NT, P], BF16, p3)

                def chunk_of(ci, step):
                    return step if ci < 2 else NT - 1 - step

                nb1 = 0
                for ci in range(4):
                    mask = maskF if ci < 2 else maskB
                    mkey = "maskF" if ci < 2 else "maskB"
                    for cg in range(4):
                        ba = nb1 % 2
                        bt = 2 + nb1 % 2
                        nb1 += 1
                        for j in range(4):
                            c = cg * 4 + j
                            csl = slice(c * P, (c + 1) * P)
                            T.op(PE, lambda j=j, csl=csl: nc.tensor.matmul(ps[:, ba, j * P:(j + 1) * P], lhsT=ktT[:, ci, csl], rhs=qtT[:, ci, csl],
                                                                           start=True, stop=True), writes=BK(ba), inc=(j == 3))
                        for j in range(4):
                            c = cg * 4 + j
                            csl = slice(c * P, (c + 1) * P)
                            T.op(PE, lambda j=j, csl=csl: nc.tensor.transpose(out=psb[:, bt, j * P:(j + 1) * P], in_=ktT[:, ci, csl], identity=ident_b[:]),
                                 reads=["ident_b"], writes=BK(bt), inc=(j == 3))
                        T.op(DVE, lambda: nc.vector.tensor_tensor(out=Am_all[:, ci, cg * 4:(cg + 1) * 4, :],
                                                                  in0=ps[:, ba, :].rearrange("p (j t) -> p j t", t=P),
                                                                  in1=mask[:].unsqueeze(1).broadcast_to([P, 4, P]), op=ALU.mult),
                             reads=BK(ba) + [mkey], writes=[("Am", ci, cg)])
                        T.op(ACT, lambda: nc.scalar.activation(out=ktok_all[:, ci, cg * 4:(cg + 1) * 4, :],
                                                               in_=psb[:, bt, 0:512].rearrange("p (j t) -> p j t", t=P), func=AF.Copy),
                             reads=BK(bt), writes=[("ktok", ci, cg)])

                def emit_U(step):
                    bu = 4 + step % 2
                    for ci in range(4):
                        c = chunk_of(ci, step)
                        vsl = slice((ci % 2) * P, (ci % 2 + 1) * P)
                        T.op(PE, lambda ci=ci, c=c, vsl=vsl: nc.tensor.matmul(ps[:, bu, ci * P:(ci + 1) * P], lhsT=ktok_all[:, ci, c, :], rhs=vtok[:, c, vsl],
                                                                              start=True, stop=True),
                             reads=[("ktok", ci, c // 4)], writes=BK(bu), inc=(ci == 3))

                def emit_rec(step):
                    bu = 4 + step % 2
                    if step > 0:
                        for ci in range(4):
                            c = chunk_of(ci, step)
                            T.op(DVE, lambda ci=ci, c=c: nc.vector.tensor_scalar(out=Stmp[:, ci, :], in0=St[:, ci, :], scalar1=dA[:, ci, c:c + 1], scalar2=None,
                                                                                 op0=ALU.mult), reads=[("St", ci)], writes=[("Stmp", ci)])
                    for ci in range(4):
                        c = chunk_of(ci, step)
                        usl = slice(ci * P, (ci + 1) * P)
                        if step == 0:
                            T.op(DVE, lambda ci=ci, c=c, usl=usl: nc.vector.tensor_scalar(out=St[:, ci, :], in0=ps[:, bu, usl], scalar1=dB[:, ci, c:c + 1],
                                                                                          scalar2=None, op0=ALU.mult), reads=BK(bu), writes=[("St", ci)])
                        else:
                            T.op(DVE, lambda ci=ci, c=c, usl=usl: nc.vector.scalar_tensor_tensor(out=St[:, ci, :], in0=ps[:, bu, usl], scalar=dB[:, ci, c:c + 1],
                                                                                                 in1=Stmp[:, ci, :], op0=ALU.mult, op1=ALU.add),
                                 reads=BK(bu) + [("Stmp", ci)], writes=[("St", ci)])

                def emit_Sb(step):
                    if step >= NT - 1:
                        return
                    for ci in range(4):
                        cn = chunk_of(ci, step + 1)
                        T.op(ACT, lambda ci=ci, cn=cn: nc.scalar.activation(out=Sb[:, step % 2, ci, :], in_=St[:, ci, :], func=AF.Identity,
                                                                            scale=dM[:, ci, cn:cn + 1]),
                             reads=[("St", ci)], writes=[("Sb", step % 2, ci)])

                def emit_O(step):
                    bo = 6 + step % 2
                    for ci in range(4):
                        c = chunk_of(ci, step)
                        csl = slice(c * P, (c + 1) * P)
                        vsl = slice((ci % 2) * P, (ci % 2 + 1) * P)
                        osl = slice(ci * P, (ci + 1) * P)
                        if step > 0:
                            T.op(PE, lambda ci=ci, csl=csl, osl=osl: nc.tensor.matmul(ps[:, bo, osl], lhsT=qtT[:, ci, csl], rhs=Sb[:, (step - 1) % 2, ci, :],
                                                                                     start=True, stop=False),
                                 reads=[("Sb", (step - 1) % 2, ci)], writes=BK(bo), inc=False)
                        T.op(PE, lambda ci=ci, c=c, vsl=vsl, osl=osl: nc.tensor.matmul(ps[:, bo, osl], lhsT=Am_all[:, ci, c, :], rhs=vtok[:, c, vsl],
                                                                                      start=(step == 0), stop=True),
                             reads=[("Am", ci, c // 4)], writes=BK(bo), inc=(ci == 3))

                def fin_A(step):
                    bo = 6 + step % 2
                    sl4 = step % 4
                    for pr in range(2):
                        c = chunk_of(2 * pr, step)
                        src = ps[:, bo, pr * 256:(pr + 1) * 256]
                        if step < NT // 2:
                            T.op(ACT, lambda c=c, src=src: nc.scalar.activation(out=opart[:, c, :], in_=src, func=AF.Copy), reads=BK(bo), writes=[("opart", c)])
                        else:
                            T.op(DVE, lambda c=c, src=src, pr=pr: nc.vector.tensor_tensor(out=osum[:, sl4, 2 * pr:2 * pr + 2, :].rearrange("p c v -> p (c v)"),
                                                                                          in0=src, in1=opart[:, c, :], op=ALU.add),
                                 reads=BK(bo) + [("opart", c)], writes=[("osum", sl4)])

                def fin_B(step):
                    if step < NT // 2:
                        return
                    sl4 = step % 4
                    for ci in range(4):
                        T.op(ACT, lambda ci=ci: nc.scalar.activation(out=scr3[:], in_=osum[:, sl4, ci, :], func=AF.Square, scale=inv128,
                                                                     accum_out=fh[:, sl4, 0, ci:ci + 1]), reads=[("osum", sl4)], writes=["scr3", ("fh", sl4)])
                    T.op(ACT, lambda: nc.scalar.activation(out=fh[:, sl4, 1, :], in_=fh[:, sl4, 0, :], func=AF.Ln, bias=cst[:, 0:1]),
                         reads=[("fh", sl4), "cst"], writes=[("fh", sl4)])
                    T.op(ACT, lambda: nc.scalar.activation(out=fh[:, sl4, 2, :], in_=fh[:, sl4, 1, :], func=AF.Exp, scale=-0.5),
                         reads=[("fh", sl4)], writes=[("fh", sl4)])

                def fin_C(step):
                    if step < NT // 2:
                        return
                    sl4 = step % 4
                    for ci in range(4):
                        c = chunk_of(ci, step)
                        vsl = slice((ci % 2) * P, (ci % 2 + 1) * P)
                        T.op(DVE, lambda ci=ci, c=c, vsl=vsl: nc.vector.scalar_tensor_tensor(out=onh[:, step % 2, ci, :], in0=osum[:, sl4, ci, :],
                                                                                             scalar=fh[:, sl4, 2, ci:ci + 1], in1=sg[:, c, vsl],
                                                                                             op0=ALU.mult, op1=ALU.mult),
                             reads=[("osum", sl4), ("fh", sl4), ("sg", c)], writes=[("onh", step % 2)])

                def fin_D(step):
                    if step < NT // 2:
                        return
                    bt = step % 2
                    for ci in range(4):
                        T.op(PE, lambda ci=ci: nc.tensor.transpose(out=psb[:, bt, ci * P:(ci + 1) * P], in_=onh[:, step % 2, ci, :], identity=ident_b[:]),
                             reads=[("onh", step % 2), "ident_b"], writes=BK(bt), inc=(ci == 3))

                def fin_E(step):
                    if step < NT // 2:
                        return
                    bt = step % 2
                    for pr in range(2):
                        c = chunk_of(2 * pr, step)
                        T.op(ACT, lambda pr=pr, c=c: nc.scalar.activation(out=catT[:, rnd * 2:rnd * 2 + 2, c * P:(c + 1) * P],
                                                                          in_=psb[:, bt, pr * 256:(pr + 1) * 256].rearrange("p (h t) -> p h t", t=P),
                                                                          func=AF.Identity, scale=cols[:, 5:6]),
                             reads=BK(bt) + [C_ALL], writes=[("catT", rnd, c)])

                emit_U(0)
                for step in range(NT + 5):
                    if step + 1 < NT:
                        emit_U(step + 1)
                    if step < NT:
                        emit_rec(step)
                        emit_Sb(step)
                        emit_O(step)
                    for lagk, fn in ((1, fin_A), (2, fin_B), (3, fin_C), (4, fin_D), (5, fin_E)):
                        if 0 <= step - lagk < NT:
                            fn(step - lagk)
                T.barrier(ALL)
        if stage == 6:
            dbg_dump([(catT[:, h, 0:1024], 1024) for h in range(4)])
            return nc

        yres = ybig[:].rearrange("p (i d) -> p i d", d=D)
        with ExitStack() as p6:
            wout = sb("wout", [P, KC, D], BF16, p6)
            lnb = sb("lnb", [P, 2, D], F32, p6)
            for c4 in range(4):
                hc_, kh = divmod(c4, 2)
                T.dma(POOL, d_w[hc_], wout[:, 4 * kh:4 * kh + 4, hc_ * 512:(hc_ + 1) * 512], w_out_d[:, 4 * kh:4 * kh + 4, hc_ * 512:(hc_ + 1) * 512],
                      writes=[("wout", hc_)])
            for r6 in range(2):
                T.dma(SP, d_c2[3], lnb[:, r6, :], lnv_d[2 * r6:2 * r6 + 1, :].partition_broadcast(P), writes=["lnb"])
            T.op(DVE, lambda: nc.vector.tensor_scalar(out=lnb[:].rearrange("p r d -> p (r d)"), in0=lnb[:].rearrange("p r d -> p (r d)"),
                                                      scalar1=ALPHA, scalar2=None, op0=ALU.mult), reads=["lnb"], writes=["lnb"])

            NX6 = 10
            xt6 = sb("xtb", [P, NX6, D], F32, p6)
            xk = lambda i: ("xt", i % NX6)
            xa = lambda i: xt6[:, i % NX6, :]
            bk6 = lambda i: (2 * (i % 2), 2 * (i % 2) + 1)

            def p6_s0(i):
                T.dma(SP, d_x[i % NX6], xa(i), x_d[i * P:(i + 1) * P, :], writes=[xk(i)])

            def p6_s2(i):
                ln_apply(xa(i), xk(i), i % NX6)
                b0 = 4 + 2 * (i % 2)
                for hc in range(2):
                    b = b0 + hc
                    T.op(PE, lambda hc=hc, b=b: nc.tensor.matmul(ps[:, b, :], lhsT=ones_b[:], rhs=brow[:, 0, hc * 512:(hc + 1) * 512], start=True, stop=False),
                         reads=["ones_b", "brow"], writes=BK(b), inc=False)
                    for kc in range(KC):
                        T.op(PE, lambda kc=kc, hc=hc, b=b: nc.tensor.matmul(ps[:, b, :], lhsT=catT[:, kc, i * P:(i + 1) * P],
                                                                            rhs=wout[:, kc, hc * 512:(hc + 1) * 512], start=False, stop=(kc == KC - 1)),
                             reads=[("wout", hc)], writes=BK(b), inc=(kc == KC - 1))

            def p6_s3(i):
                xn_ap, key = xa(i), xk(i)
                b0 = 4 + 2 * (i % 2)
                T.op(DVE, lambda: nc.vector.tensor_tensor(out=xn_ap, in0=xn_ap, in1=lnb[:, 0, :], op=ALU.mult), reads=[key, "lnb"], writes=[key])
                T.op(DVE, lambda: nc.vector.tensor_tensor(out=xn_ap, in0=xn_ap, in1=ps[:, b0:b0 + 2, :].rearrange("p b n -> p (b n)"), op=ALU.add),
                     reads=[key] + BK(b0) + BK(b0 + 1), writes=[key])

            def p6_s6(i):
                transpose_evac(hT, "hT", i, bk6(i), 2)
                T.op(DVE, lambda: nc.vector.tensor_tensor(out=yres[:, i, :], in0=xa(i), in1=lnb[:, 1, :], op=ALU.mult), reads=[xk(i), "lnb"], writes=[("yres", i)])

            def p6_mm(i):
                b0 = 4 + 2 * (i % 2)
                for hc in range(2):
                    b = b0 + hc
                    T.op(PE, lambda hc=hc, b=b: nc.tensor.matmul(ps[:, b, :], lhsT=ones_b[:], rhs=brow[:, 0, hc * 512:(hc + 1) * 512], start=True, stop=False),
                         reads=["ones_b", "brow"], writes=BK(b), inc=False)
                    for kc in range(KC):
                        T.op(PE, lambda kc=kc, hc=hc, b=b: nc.tensor.matmul(ps[:, b, :], lhsT=catT[:, kc, i * P:(i + 1) * P],
                                                                            rhs=wout[:, kc, hc * 512:(hc + 1) * 512], start=False, stop=(kc == KC - 1)),
                             reads=[("wout", hc)], writes=BK(b), inc=(kc == KC - 1))

            def p6_apply_pair(i):
                o1 = [lambda: T.op(ACT, lambda: nc.scalar.activation(out=xa(i), in_=xa(i), func=AF.Identity, scale=lnst[:, i, 0:1], bias=lnst[:, i, 1:2]),
                                   reads=[xk(i), ("lnst", i)], writes=[xk(i)])] if 0 <= i < NT else []
                i2 = i - 4
                o2 = ln_apply_ops(xa(i2), xk(i2), 12 + i2 % NX6) if 0 <= i2 < NT else []
                while o1 or o2:
                    if o1:
                        o1.pop(0)()
                    if o2:
                        o2.pop(0)()
                if 0 <= i < NT:
                    p6_mm(i)

            inr = lambda i: 0 <= i < NT
            for step in range(NT + 10):
                if inr(step):
                    p6_s0(step)
                p6_apply_pair(step - 3)
                if inr(step - 4):
                    p6_s3(step - 4)
                if inr(step - 5):
                    ln_stats_a(xa(step - 5), xk(step - 5), 12 + (step - 5) % NX6)
                if inr(step - 6):
                    ln_stats_b(12 + (step - 6) % NX6)
                if inr(step - 8):
                    transpose_pe(xa(step - 8), [xk(step - 8)], bk6(step - 8))
                if inr(step - 9):
                    p6_s6(step - 9)
            T.barrier(ALL)
        if stage == 7:
            dbg_dump([(yres[:, i, :], 1024) for i in range(8)])
            return nc
        T.barrier(ALL)
        pc.close()

        with ExitStack() as p7:
            NUP = 3
            GS = 3
            NRA = 2 * GS + 1
            NRW = 2 * GS + 3
            wup = sb("wup", [P, NUP, KC, 256], BF16, p7)
            wdn = sb("wdn", [P, NRW, D], BF16, p7)
            actT = sb("actT", [P, NRA, S], BF16, p7)
            ubuf = sb("ubuf", [P, 2, S + 2], F32, p7)
            cbuf = sb("cbuf", [P, 3, S], F32, p7)
            T.op(POOL, lambda: nc.gpsimd.memset(ubuf[:, :, 0:1], 0.0), writes=["ubuf_pad"], inc=False)
            T.op(POOL, lambda: nc.gpsimd.memset(ubuf[:, :, S + 1:S + 2], 0.0), writes=["ubuf_pad"])
            groups = []
            j0 = 0
            while j0 < NJ:
                j1 = min(NJ, j0 + GS)
                if NJ - j1 == 1:
                    j1 = NJ
                groups.append((j0, j1))
                j0 = j1
            gend = {g[1] - 1: g for g in groups}
            dw_up = d_w[0:3]
            dw_dn = d_w[3:13]
            assert NRW <= 10

            def load_j(j):
                T.dma(POOL, dw_up[j % NUP], wup[:, j % NUP, :, :], w_up_d[j, :, :, :], writes=[("wup", j % NUP)])
                T.dma(POOL, dw_dn[j % NRW], wdn[:, j % NRW, :], w_dn_d[:, j, :], writes=[("wdn", j % NRW)])

            pending = []
            hold = [0]

            def down_unit(i, hc, g0, g1, n):
                def emit():
                    b = 4 + n % 4
                    if g0 == 0:
                        T.op(PE, lambda: nc.tensor.matmul(ps[:, b, :], lhsT=ones_b[:], rhs=brow[:, 1, hc * 512:(hc + 1) * 512], start=True, stop=False),
                             reads=["ones_b", "brow"], writes=BK(b), inc=False)
                    for jj in range(g0, g1):
                        T.op(PE, lambda jj=jj: nc.tensor.matmul(ps[:, b, :], lhsT=actT[:, jj % NRA, i * P:(i + 1) * P],
                                                                rhs=wdn[:, jj % NRW, hc * 512:(hc + 1) * 512],
                                                                start=(jj == g0 and g0 != 0), stop=(jj == g1 - 1)),
                             reads=[("actT", jj % NRA), ("wdn", jj % NRW)], writes=BK(b), inc=(jj == g1 - 1))
                    T.op(DVE, lambda: nc.vector.tensor_tensor(out=yres[:, i, hc * 512:(hc + 1) * 512],
                                                              in0=yres[:, i, hc * 512:(hc + 1) * 512], in1=ps[:, b, :], op=ALU.add),
                         reads=[("yres", i)] + BK(b), writes=[("yres", i)])
                return emit

            load_j(0)
            load_j(1)
            nb = 0
            nd = 0
            for j in range(NJ):
                if j + 2 < NJ:
                    load_j(j + 2)
                us = j % NUP
                rsl = j % NRA
                ca = j % 2
                for ab in range(2):
                    cb_i = ca if ab == 0 else 2
                    for half in range(2):
                        b0 = (nb % 2) * 2
                        nb += 1
                        for tb in range(2):
                            t0 = half * 1024 + tb * 512
                            for kc in range(KC):
                                T.op(PE, lambda kc=kc, tb=tb, t0=t0: nc.tensor.matmul(ps[:, b0 + tb, :], lhsT=wup[:, us, kc, ab * P:(ab + 1) * P],
                                                                                     rhs=hT[:, kc, t0:t0 + 512], start=(kc == 0), stop=(kc == KC - 1)),
                                     reads=[("wup", us)] + hkeys("hT", t0, t0 + 512), writes=BK(b0 + tb), inc=(kc == KC - 1))
                        src2 = ps[:, b0:b0 + 2, :].rearrange("p b n -> p (b n)")
                        T.op(ACT, lambda: nc.scalar.activation(out=ubuf[:, ab, 1 + half * 1024:1 + (half + 1) * 1024], in_=src2, func=AF.Copy),
                             reads=BK(b0) + BK(b0 + 1), writes=[("ubuf", ab, half)])
                        T.op(ACT, lambda: nc.scalar.activation(out=cbuf[:, cb_i, half * 1024:(half + 1) * 1024], in_=src2, func=AF.Identity,
                                                               scale=convp[:, j, ab, 1:2], bias=convp[:, j, ab, 3:4]),
                             reads=BK(b0) + BK(b0 + 1) + [C_ALL], writes=[("cbuf", cb_i, half)])
                        if hold[0] > 0:
                            hold[0] -= 1
                        else:
                            for _ in range(4):
                                if pending:
                                    pending.pop(0)()
                    ck = [("cbuf", cb_i, 0), ("cbuf", cb_i, 1)]
                    uk = [("ubuf", ab, 0), ("ubuf", ab, 1), "ubuf_pad"]
                    T.op(DVE, lambda: nc.vector.scalar_tensor_tensor(out=cbuf[:, cb_i, :], in0=ubuf[:, ab, 0:S], scalar=convp[:, j, ab, 0:1],
                                                                     in1=cbuf[:, cb_i, :], op0=ALU.mult, op1=ALU.add),
                         reads=ck + uk + [C_ALL], writes=ck)
                    T.op(DVE, lambda: nc.vector.scalar_tensor_tensor(out=cbuf[:, cb_i, :], in0=ubuf[:, ab, 2:S + 2], scalar=convp[:, j, ab, 2:3],
                                                                     in1=cbuf[:, cb_i, :], op0=ALU.mult, op1=ALU.add),
                         reads=ck + uk + [C_ALL], writes=ck)
                ck0 = [("cbuf", ca, 0), ("cbuf", ca, 1)]
                T.op(ACT, lambda: nc.scalar.activation(out=cbuf[:, ca, :], in_=cbuf[:, ca, :], func=AF.Gelu_apprx_tanh), reads=ck0, writes=ck0)
                T.op(POOL, lambda: nc.gpsimd.tensor_tensor(out=actT[:, rsl, :], in0=cbuf[:, ca, :], in1=cbuf[:, 2, :], op=ALU.mult),
                     reads=ck0 + [("cbuf", 2, 0), ("cbuf", 2, 1)], writes=[("actT", rsl)])
                if j in gend:
                    g0, g1 = gend[j]
                    while pending:
                        pending.pop(0)()
                    for i in range(NT):
                        for hc in range(2):
                            pending.append(down_unit(i, hc, g0, g1, nd))
                            nd += 1
                    hold[0] = 2
            lnv2 = ubuf[:, 1, 0:2 * D].rearrange("p (r d) -> p r d", d=D)
            for r6 in range(2):
                T.dma(SP, d_c2[4], lnv2[:, r6, :], lnv_d[4 + r6:5 + r6, :].partition_broadcast(P), reads=[], writes=["lnx", ("ubuf", 1, 0), ("ubuf", 1, 1)])
            assert len(pending) == 2 * NT
            otb = cbuf[:].rearrange("p a s -> p (a s)")
            ota = lambda i: otb[:, (i % 6) * D:(i % 6 + 1) * D]
            otk = lambda i: ("ot", i % 6)

            def tail_units(i):
                pending.pop(0)()
                pending.pop(0)()

            pipeline([
                tail_units,
                lambda i: (ln_stats_act(yres[:, i, :], ("yres", i), i % 8, ubuf[:, 0, 0:D], "ujunk") if i % 2 == 0
                           else ln_stats(yres[:, i, :], ("yres", i), i % 8)),
                lambda i: ln_apply(yres[:, i, :], ("yres", i), i % 8),
                lambda i: T.op(DVE, lambda: nc.vector.tensor_tensor(out=ota(i), in0=yres[:, i, :], in1=lnv2[:, 0, :], op=ALU.mult),
                               reads=[("yres", i), "lnx"], writes=[otk(i)]),
                lambda i: T.op(POOL, lambda: nc.gpsimd.tensor_tensor(out=ota(i), in0=ota(i), in1=lnv2[:, 1, :], op=ALU.add),
                               reads=[otk(i), "lnx"], writes=[otk(i)]),
                lambda i: T.dma(SP, d_out[i % 6], y_d[i * P:(i + 1) * P, :], ota(i), reads=[otk(i)]),
            ], NT)
            T.barrier(ALL)
    return nc


def _prep_shared(inp):
    f = np.float32
    c = lambda a: np.ascontiguousarray(a, dtype=f)
    w_in = np.asarray(inp["w_in"])[0]
    w_in_r = c(w_in.reshape(KC, P, 3136).transpose(1, 0, 2))
    kr = w_in_r[:, :, 3072:3136]
    w_krsw = c(np.concatenate([kr[:, :, 32:64], kr[:, :, 0:32]], axis=2))
    w_qb = np.asarray(inp["w_q_b"])[0]
    w_qb_r = c(w_qb.reshape(3, P, 768).transpose(1, 0, 2))
    sw = []
    for h in range(4):
        r = w_qb_r[:, :, h * 192 + 128:h * 192 + 192]
        sw.append(np.concatenate([r[:, :, 32:64], r[:, :, 0:32]], axis=2))
    w_qbsw = c(np.concatenate(sw, axis=2))
    w_kvb = np.asarray(inp["w_kv_b"])[0].reshape(P, 4, 256)
    w_kvk = c(w_kvb[:, :, 0:128].reshape(P, 512))
    w_kvv = c(w_kvb[:, :, 128:256].reshape(P, 512))
    w_out_r = c(np.asarray(inp["w_out"])[0].reshape(KC, P, D).transpose(1, 0, 2))
    w_up = np.asarray(inp["w_up"])[0].reshape(KC, P, 2, NJ, P)
    w_up_r = c(w_up.transpose(3, 1, 0, 2, 4).reshape(NJ, P, KC, 256))
    w_dn_r = c(np.asarray(inp["w_down"])[0].reshape(NJ, P, D).transpose(1, 0, 2))
    cols = np.zeros((P, NCOL), f)
    cols[:, 0:3] = np.asarray(inp["q_a_norm_g"])[0].reshape(3, P).T
    cols[:, 3] = np.asarray(inp["kv_a_norm_g"])[0]
    cols[:, 4] = np.asarray(inp["attn_norm_g"])[0]
    cols[:, 5] = np.asarray(inp["hg_norm_g"])[0]
    half = 32
    inv_freq = (1.0 / (10000.0 ** (np.arange(half, dtype=np.float32) / np.float32(half)))).astype(f)
    cols[0:64, 6] = np.concatenate([inv_freq, inv_freq])
    cols[0:32, 7] = -1.0
    cols[32:64, 7] = 1.0
    lbr = np.zeros((P, 16), f)
    for d, nm in enumerate(("lb_fwd", "lb_bwd")):
        a = np.asarray(inp[nm]).reshape(2, 4, P)
        lbr[:, d * 8:(d + 1) * 8] = a.transpose(2, 0, 1).reshape(P, 8)
    lnv = c(np.stack([np.asarray(inp["ln_in_g"]), np.asarray(inp["ln_in_b"]), np.asarray(inp["ln1_g"])[0], np.asarray(inp["ln1_b"])[0],
                      np.asarray(inp["ln2_g"])[0], np.asarray(inp["ln2_b"])[0]]))
    lnc = c(lnv.reshape(6, KC, P).transpose(2, 0, 1).reshape(P, 48))
    gqkv = c(np.concatenate([np.asarray(inp["q_a_norm_g"])[0], np.asarray(inp["kv_a_norm_g"])[0]])[None])
    cw = np.asarray(inp["conv_w"])[0].reshape(3, 2, NJ, P)
    cb = np.asarray(inp["conv_b"])[0].reshape(1, 2, NJ, P)
    convp = c(np.concatenate([cw, cb], axis=0).transpose(3, 2, 1, 0).reshape(P, NJ * 8))
    return {"cols": cols, "lbr": lbr, "lnv": lnv, "lnc": lnc, "gqkv": gqkv, "convp": convp, "w_in_r": w_in_r, "w_krsw": w_krsw, "w_qb_r": w_qb_r,
            "w_qbsw": w_qbsw, "w_kvk": w_kvk, "w_kvv": w_kvv, "w_out_r": w_out_r, "w_up_r": w_up_r, "w_dn_r": w_dn_r}


def make_in_maps(inp, n=8):
    shared = _prep_shared(inp)
    x = np.asarray(inp["x"], dtype=np.float32)
    pos = np.asarray(inp["positions"], dtype=np.int32)
    maps = []
    for b in range(n):
        m = dict(shared)
        m["x"] = np.ascontiguousarray(x[b])
        m["pos"] = np.ascontiguousarray(pos[b][None])
        maps.append(m)
    return maps


def kernel(**inputs):
    nc = build_nc()
    in_maps = make_in_maps(inputs, 8)
    res = run_bass_kernel_spmd(nc, in_maps, core_ids=list(range(8)))
    return np.stack([np.asarray(r["y"], dtype=np.float32) for r in res.results], axis=0)
```

```python
import math
from contextlib import ExitStack

import numpy as np
import concourse.bass as bass
import concourse.mybir as mybir
from concourse.bass_utils import run_bass_kernel_spmd

F32 = mybir.dt.float32
BF16 = mybir.dt.bfloat16
I32 = mybir.dt.int32
AF = mybir.ActivationFunctionType
ALU = mybir.AluOpType

P = 128
S = 2048
NT = 16
D = 1024
KC = 8
DFF = 2816
NJ = 22
EPS = 1e-5
ALPHA = 2.0 ** 0.25
SCALE = 192.0 ** -0.5
TWO_PI = 2.0 * math.pi
NCOL = 16


class Src:
    def __init__(self, sem, name):
        self.sem = sem
        self.cnt = 0
        self.name = name


class Eng:
    def __init__(self, name, e, src):
        self.name = name
        self.e = e
        self.src = src
        self.waited = {}


class Trk:
    def __init__(self, nc, es):
        self.nc = nc
        self.es = es
        self.lastw = {}
        self.readers = {}
        self.srcs = []
        self.nsem = 0

    def new_src(self, name):
        sem = self.es.enter_context(self.nc.semaphore(name))
        s = Src(sem, name)
        self.srcs.append(s)
        return s

    def _wait(self, eng, src, c):
        if c <= 0:
            return
        if eng.waited.get(src, 0) >= c:
            return
        assert src.cnt >= c, (eng.name, src.name, src.cnt, c)
        eng.e.wait_ge(src.sem, c)
        eng.waited[src] = c

    def _deps(self, eng, reads, writes, own):
        deps = {}

        def add(s, c):
            if deps.get(s, 0) < c:
                deps[s] = c

        for k in reads:
            w = self.lastw.get(k)
            if w is not None:
                add(*w)
        for k in writes:
            w = self.lastw.get(k)
            if w is not None and w[0] is not own:
                add(*w)
            for s, c in self.readers.get(k, {}).items():
                if s is not own:
                    add(s, c)
        for s, c in deps.items():
            self._wait(eng, s, c)

    def _record(self, src, c, reads, writes):
        for k in reads:
            d = self.readers.setdefault(k, {})
            if d.get(src, 0) < c:
                d[src] = c
        for k in writes:
            self.lastw[k] = (src, c)
            self.readers[k] = {}

    @staticmethod
    def _excl(reads, writes):
        pr = [k for k in reads if isinstance(k, tuple) and k[0] == "ps"]
        if pr:
            reads = [k for k in reads if not (isinstance(k, tuple) and k[0] == "ps")]
            writes = list(writes) + pr
        return reads, writes

    def op(self, eng, fn, reads=(), writes=(), inc=True):
        reads, writes = self._excl(reads, writes)
        self._deps(eng, reads, writes, eng.src)
        ins = fn()
        if inc:
            eng.src.cnt += 1
            ins.then_inc(eng.src.sem, 1)
            c = eng.src.cnt
        else:
            c = eng.src.cnt + 1
        self._record(eng.src, c, reads, writes)
        return ins

    def dma(self, q, dsrc, out, in_, reads=(), writes=(), **kw):
        self._deps(q, reads, writes, None)
        ins = q.e.dma_start(out=out, in_=in_, **kw)
        dsrc.cnt += 16
        ins.then_inc(dsrc.sem, 16)
        self._record(dsrc, dsrc.cnt, reads, writes)
        return ins

    def barrier(self, engs):
        for e in engs:
            for s in self.srcs:
                self._wait(e, s, s.cnt)
        self.lastw = {}
        self.readers = {}


def build_nc(stage=99):
    nc = bass.Bass("TRN2", target_bir_lowering=False)

    def din(name, shape, dt=F32):
        return nc.dram_tensor(name, list(shape), dt, kind="ExternalInput").ap()

    x_d = din("x", [S, D])
    pos_d = din("pos", [1, S], I32)
    cols_d = din("cols", [P, NCOL])
    lb_d = din("lbr", [P, 16])
    lnv_d = din("lnv", [6, D])
    lnc_d = din("lnc", [P, 48])
    gqkv_d = din("gqkv", [1, 512])
    convp_d = din("convp", [P, NJ * 8])
    w_in_d = din("w_in_r", [P, KC, 3136])
    w_krsw_d = din("w_krsw", [P, KC, 64])
    w_qb_d = din("w_qb_r", [P, 3, 768])
    w_qbsw_d = din("w_qbsw", [P, 3, 256])
    w_kvk_d = din("w_kvk", [P, 512])
    w_kvv_d = din("w_kvv", [P, 512])
    w_out_d = din("w_out_r", [P, KC, D])
    w_up_d = din("w_up_r", [NJ, P, KC, 256])
    w_dn_d = din("w_dn_r", [P, NJ, D])
    y_d = nc.dram_tensor("y", [S, D], F32, kind="ExternalOutput").ap()
    if stage < 99:
        dbg_d = nc.dram_tensor("dbg", [P, 8192], F32, kind="ExternalOutput").ap()

    with ExitStack() as es:
        T = Trk(nc, es)
        PE = Eng("pe", nc.tensor, T.new_src("s_pe"))
        ACT = Eng("act", nc.scalar, T.new_src("s_act"))
        DVE = Eng("dve", nc.vector, T.new_src("s_dve"))
        POOL = Eng("pool", nc.gpsimd, T.new_src("s_pool"))
        SP = Eng("sp", nc.sync, T.new_src("s_sp"))
        ALL = [PE, ACT, DVE, POOL, SP]
        d_const = T.new_src("d_const")
        d_c2 = [T.new_src("d_c2_%d" % i) for i in range(6)]
        d_x = [T.new_src("d_x%d" % i) for i in range(10)]
        d_w = [T.new_src("d_w%d" % i) for i in range(14)]
        d_out = [T.new_src("d_o%d" % i) for i in range(6)]

        sbn = [0]

        def sb(name, shape, dt, stk=es):
            sbn[0] += 1
            return stk.enter_context(nc.sbuf_tensor("sb%d_%s" % (sbn[0], name), list(shape), dt))

        ps = es.enter_context(nc.psum_tensor("ps", [P, 8, 512], F32))
        psb = ps.bitcast(BF16)

        def BK(b):
            return [("ps", b)]

        ident_f = sb("ident_f", [P, P], F32)
        ident_b = sb("ident_b", [P, P], BF16)
        maskF = sb("maskF", [P, P], F32)
        maskB = sb("maskB", [P, P], F32)
        m0 = sb("m0", [P, 512], F32)
        cst = sb("cst", [P, 4], F32)
        cols = sb("cols", [P, NCOL], F32)
        lbr = sb("lbr", [P, 16], F32)
        ones_b = sb("ones_b", [P, P], BF16)
        lnst = sb("lnst", [P, NT, 2], F32)
        brow = sb("brow", [P, 2, D], BF16)
        lbT = sb("lbT", [P, 8], F32)
        omlb = sb("omlb", [P, 8], F32)
        lnc = sb("lnc", [P, 6, KC], F32)
        convp = sb("convp", [P, NJ, 2, 4], F32)
        stt = sb("stt", [P, 24, 12], F32)
        mv = sb("mv", [P, 24, 2], F32)
        sd = sb("sd", [P, 24], F32)
        rs = sb("rs", [P, 24], F32)
        nmr = sb("nmr", [P, 24], F32)
        hT = sb("hT", [P, KC, S], BF16)
        ybig = sb("ybig", [P, NT * D], F32)
        ybig_b = ybig.bitcast(BF16)

        def carve_f(off, n, parts=P):
            return ybig[0:parts, off:off + n]

        def carve_b(off, n, parts=P):
            return ybig_b[0:parts, 2 * off:2 * off + n]

        wout_v = carve_b(0, KC * D).rearrange("p (k d) -> p k d", d=D)
        pc = es.enter_context(ExitStack())
        catT = sb("catT", [P, KC, S], BF16, pc)

        C_ALL = ("const",)
        T.dma(SP, d_const, cols[:], cols_d[:, :], writes=[C_ALL])
        T.dma(SP, d_const, lbr[:], lb_d[:, :], writes=[C_ALL])
        T.dma(SP, d_const, convp[:].rearrange("p j a c -> p (j a c)"), convp_d[:, :], writes=[C_ALL])
        T.dma(SP, d_const, lnc[:].rearrange("p r k -> p (r k)"), lnc_d[:, :], writes=[C_ALL])

        T.op(POOL, lambda: nc.gpsimd.memset(ident_f[:], 1.0), writes=["ident_f"])
        T.op(POOL, lambda: nc.gpsimd.affine_select(out=ident_f[:], in_=ident_f[:], pattern=[[-1, P]],
                                                     compare_op=ALU.is_equal, fill=0.0, base=0, channel_multiplier=1),
             reads=["ident_f"], writes=["ident_f"])
        T.op(POOL, lambda: nc.gpsimd.memset(cst[:, 0:1], EPS), writes=["cst"], inc=False)
        T.op(POOL, lambda: nc.gpsimd.memset(cst[:, 1:2], EPS / (ALPHA * ALPHA)), writes=["cst"], inc=False)
        T.op(POOL, lambda: nc.gpsimd.memset(cst[:, 3:4], 1.0), writes=["cst"], inc=False)
        T.op(POOL, lambda: nc.gpsimd.memset(cst[:, 2:3], 0.0), writes=["cst"])
        T.op(DVE, lambda: nc.vector.tensor_copy(out=ident_b[:], in_=ident_f[:]), reads=["ident_f"], writes=["ident_b"])

        def emit_late_setup():
            T.op(POOL, lambda: nc.gpsimd.memset(maskF[:], 1.0), writes=["maskF"])
            T.op(POOL, lambda: nc.gpsimd.affine_select(out=maskF[:], in_=maskF[:], pattern=[[1, P]],
                                                         compare_op=ALU.is_ge, fill=0.0, base=0, channel_multiplier=-1),
                 reads=["maskF"], writes=["maskF"])
            T.op(POOL, lambda: nc.gpsimd.memset(maskB[:], 1.0), writes=["maskB"])
            T.op(POOL, lambda: nc.gpsimd.affine_select(out=maskB[:], in_=maskB[:], pattern=[[-1, P]],
                                                         compare_op=ALU.is_ge, fill=0.0, base=0, channel_multiplier=1),
                 reads=["maskB"], writes=["maskB"])
            T.op(POOL, lambda: nc.gpsimd.memset(m0[:], 1.0), writes=["m0"])
            T.op(POOL, lambda: nc.gpsimd.memset(m0[:].rearrange("p (c t) -> p c t", t=P)[:, :, 0:1], 0.0), reads=["m0"], writes=["m0"])
            T.op(POOL, lambda: nc.gpsimd.memset(ones_b[:], 1.0), writes=["ones_b"])
            T.op(POOL, lambda: nc.gpsimd.memset(brow[:], 0.0), writes=["brow"])
            lb3 = lbr[:].rearrange("p (d r h) -> p d r h", d=2, r=2)
            T.op(DVE, lambda: nc.vector.tensor_tensor(out=omlb[:].rearrange("p (d h) -> p d h", d=2), in0=lb3[:, :, 0, :],
                                                      in1=lb3[:, :, 1, :], op=ALU.subtract), reads=[C_ALL], writes=["omlb"])
            T.op(ACT, lambda: nc.scalar.activation(out=lbT[:], in_=omlb[:], func=AF.Sigmoid), reads=["omlb"], writes=["lbT"])
            T.op(DVE, lambda: nc.vector.tensor_scalar(out=omlb[:], in0=lbT[:], scalar1=-1.0, scalar2=1.0, op0=ALU.mult, op1=ALU.add),
                 reads=["lbT"], writes=["omlb"])

        def rstd_from(var_ap, out_ap, rkeys, wkey, tmp_ap, tkey, eps_col=0):
            T.op(ACT, lambda: nc.scalar.activation(out=tmp_ap, in_=var_ap, func=AF.Ln, bias=cst[:, eps_col:eps_col + 1]),
                 reads=list(rkeys) + ["cst"], writes=[tkey])
            T.op(ACT, lambda: nc.scalar.activation(out=out_ap, in_=tmp_ap, func=AF.Exp, scale=-0.5), reads=[tkey], writes=[wkey])

        def ln_stats_a(src, key, slot):
            kst = ("st", slot)
            T.op(DVE, lambda: nc.vector.bn_stats(out=stt[:, slot, 0:6], in_=src[:, 0:512]), reads=[key], writes=[kst], inc=False)
            T.op(DVE, lambda: nc.vector.bn_stats(out=stt[:, slot, 6:12], in_=src[:, 512:1024]), reads=[key], writes=[kst])

        def ln_stats_b(slot):
            kst, kmv = ("st", slot), ("mv", slot)
            T.op(DVE, lambda: nc.vector.bn_aggr(out=mv[:, slot, :], in_=stt[:, slot, :]), reads=[kst], writes=[kmv])
            T.op(DVE, lambda: nc.vector.tensor_scalar(out=nmr[:, slot:slot + 1], in0=mv[:, slot, 0:1], scalar1=-1.0, scalar2=None, op0=ALU.mult),
                 reads=[kmv], writes=[("nm0", slot)])

        def ln_stats(src, key, slot):
            ln_stats_a(src, key, slot)
            ln_stats_b(slot)

        def ln_stats_act(src, key, slot, junk, jkey):
            kst, kmv = ("st", slot), ("mv", slot)
            T.op(ACT, lambda: nc.scalar.activation(out=junk, in_=src, func=AF.Identity, scale=1.0 / D, accum_out=stt[:, slot, 0:1]),
                 reads=[key], writes=[jkey, kst])
            T.op(ACT, lambda: nc.scalar.activation(out=junk, in_=src, func=AF.Square, scale=1.0 / math.sqrt(D), accum_out=stt[:, slot, 1:2]),
                 reads=[key], writes=[jkey, kst])
            T.op(DVE, lambda: nc.vector.tensor_copy(out=mv[:, slot, 0:1], in_=stt[:, slot, 0:1]), reads=[kst], writes=[kmv], inc=False)
            T.op(DVE, lambda: nc.vector.tensor_scalar(out=nmr[:, slot:slot + 1], in0=stt[:, slot, 0:1], scalar1=-1.0, scalar2=None, op0=ALU.mult),
                 reads=[kst], writes=[("nm0", slot)])
            T.op(DVE, lambda: nc.vector.scalar_tensor_tensor(out=mv[:, slot, 1:2], in0=stt[:, slot, 0:1], scalar=nmr[:, slot:slot + 1],
                                                             in1=stt[:, slot, 1:2], op0=ALU.mult, op1=ALU.add),
                 reads=[kst, ("nm0", slot)], writes=[kmv])

        def ln_apply_ops(src, key, slot, eps_col=0, keep=None):
            kmv, ksd, krs, knm = ("mv", slot), ("sd", slot), ("rs", slot), ("nmr", slot)
            rs_ap, nb_ap = rs[:, slot:slot + 1], sd[:, slot:slot + 1]
            tmp_ap, ktmp = sd[:, slot:slot + 1], ksd
            if keep is not None:
                rs_ap, nb_ap = keep[0][:, 0:1], keep[0][:, 1:2]
                krs = knm = keep[1]
            return [
                lambda: T.op(ACT, lambda: nc.scalar.activation(out=tmp_ap, in_=mv[:, slot, 1:2], func=AF.Ln, bias=cst[:, eps_col:eps_col + 1]),
                             reads=[kmv, "cst"], writes=[ktmp]),
                lambda: T.op(ACT, lambda: nc.scalar.activation(out=rs_ap, in_=tmp_ap, func=AF.Exp, scale=-0.5),
                             reads=[ktmp], writes=[krs]),
                lambda: T.op(ACT, lambda: nc.scalar.activation(out=nb_ap, in_=nmr[:, slot:slot + 1], func=AF.Identity, scale=rs_ap),
                             reads=[("nm0", slot), krs, ktmp], writes=[knm]),
                lambda: T.op(ACT, lambda: nc.scalar.activation(out=src, in_=src, func=AF.Identity, scale=rs_ap, bias=nb_ap),
                             reads=[key, krs, knm], writes=[key]),
            ]

        def ln_apply(src, key, slot, keep=None):
            for f in ln_apply_ops(src, key, slot, keep=keep):
                f()

        def ln_norm(src, key, slot):
            ln_stats(src, key, slot)
            ln_apply(src, key, slot)

        def transpose_pe(src, src_keys, banks):
            for hb in range(2):
                b = banks[hb]
                for q in range(4):
                    kc = hb * 4 + q
                    T.op(PE, lambda kc=kc, q=q, b=b: nc.tensor.transpose(out=ps[:, b, q * P:(q + 1) * P], in_=src[:, kc * P:(kc + 1) * P],
                                                                         identity=ident_f[:]),
                         reads=list(src_keys) + ["ident_f"], writes=BK(b), inc=(q == 3))

        def transpose_evac(dstT, name, i, banks, gi, all_act=False):
            for hb in range(2):
                b = banks[hb]
                for q in range(4):
                    kc = hb * 4 + q
                    if hb == 0 or all_act:
                        T.op(ACT, lambda kc=kc, q=q, b=b: nc.scalar.activation(out=dstT[:, kc, i * P:(i + 1) * P], in_=ps[:, b, q * P:(q + 1) * P],
                                                                               func=AF.Identity, scale=lnc[:, gi, kc:kc + 1], bias=lnc[:, gi + 1, kc:kc + 1]),
                             reads=BK(b) + [C_ALL], writes=[(name, i)])
                    else:
                        T.op(DVE, lambda kc=kc, q=q, b=b: nc.vector.tensor_scalar(out=dstT[:, kc, i * P:(i + 1) * P], in0=ps[:, b, q * P:(q + 1) * P],
                                                                                  scalar1=lnc[:, gi, kc:kc + 1], scalar2=lnc[:, gi + 1, kc:kc + 1],
                                                                                  op0=ALU.mult, op1=ALU.add),
                             reads=BK(b) + [C_ALL], writes=[(name, i)])

        def hkeys(name, t0, t1):
            return [(name, i) for i in range(t0 // P, (t1 + P - 1) // P)]

        def dbg_dump(ap_list):
            T.barrier(ALL)
            off = 0
            with nc.sbuf_tensor("dbgt", [P, 1024], F32) as dbgt:
                k = 0
                for ap, n in ap_list:
                    for c0 in range(0, n, 1024):
                        cl = min(1024, n - c0)
                        T.op(DVE, lambda: nc.vector.memset(dbgt[:], 0.0), writes=["dbgt"])
                        T.op(DVE, lambda: nc.vector.tensor_copy(out=dbgt[0:ap.shape[0], 0:cl], in_=ap[:, c0:c0 + cl]), writes=["dbgt"])
                        T.dma(SP, d_out[k % 2], dbg_d[:, off:off + cl], dbgt[:, 0:cl], reads=["dbgt"])
                        k += 1
                        off += cl
                T.dma(SP, d_out[k % 2], y_d[0:P, :], dbgt[:, 0:D], reads=["dbgt"])
                T.barrier(ALL)

        NXS = 8

        def pipeline(stages, n, lag=1):
            ns = len(stages)
            for step in range(n + lag * (ns - 1)):
                for k, st in enumerate(stages):
                    i = step - lag * k
                    if 0 <= i < n:
                        st(i)

        pw = es.enter_context(ExitStack())
        wmla = sb("wmla", [P, KC, 576], BF16, pw)
        wkrsw = sb("wkrsw", [P, KC, 64], BF16, pw)
        wqb = sb("wqb", [P, 3, 768], BF16, pw)
        wqbsw = sb("wqbsw", [P, 3, 256], BF16, pw)
        wkvk = sb("wkvk", [P, 512], BF16, pw)
        wkvv = sb("wkvv", [P, 512], BF16, pw)
        gqkvB = sb("gqkvB", [P, 512], F32, pw)
        T.dma(SP, d_c2[0], gqkvB[:], gqkv_d.partition_broadcast(P), writes=["gqkvB"])
        cosT = carve_f(8256, S, 64)
        ssinT = carve_f(10304, S, 64)

        with ExitStack() as p1:
            xt = sb("xta", [P, NXS, D], F32, p1)
            NPRE = 4
            for i in range(NPRE):
                T.dma(SP, d_x[i % NXS], xt[:, i % NXS, :], x_d[i * P:(i + 1) * P, :], writes=[("xt", i % NXS)])
            T.dma(POOL, d_w[0], wmla[:], w_in_d[:, :, 2560:3136], writes=["wmla"])
            T.dma(POOL, d_w[1], wkrsw[:], w_krsw_d[:, :, :], writes=["wkrsw"])
            for c3 in range(3):
                T.dma(POOL, d_w[2], wqb[:, c3, :], w_qb_d[:, c3, :], writes=["wqb"])
            T.dma(POOL, d_w[3], wqbsw[:], w_qbsw_d[:, :, :], writes=["wqbsw"])
            T.dma(POOL, d_w[4], wkvk[:], w_kvk_d[:, :], writes=["wkvk"])
            T.dma(POOL, d_w[5], wkvv[:], w_kvv_d[:, :], writes=["wkvv"])
            emit_late_setup()
            HS = S // 2
            posi = sb("posi", [64, HS], I32, p1)
            ang = sb("ang", [64, HS], F32, p1)
            kf = sb("kf", [64, HS], F32, p1)
            rope_ops = []
            for hv in range(2):
                hsl = slice(hv * HS, (hv + 1) * HS)
                rope_ops.append(lambda hsl=hsl, hv=hv: T.dma(SP, d_c2[1 + hv], posi[:], pos_d[:, hsl].partition_broadcast(64), writes=["posi"]))
                rope_ops.append(lambda: T.op(DVE, lambda: nc.vector.tensor_copy(out=ang[:], in_=posi[:]), reads=["posi"], writes=["ang"]))
                rope_ops.append(lambda: T.op(DVE, lambda: nc.vector.tensor_scalar(out=ang[:], in0=ang[:], scalar1=cols[0:64, 6:7], scalar2=None,
                                                                                  op0=ALU.mult), reads=["ang", C_ALL], writes=["ang"]))
                for which, dst in ((0, ssinT), (1, cosT)):
                    shift = 0.0 if which == 0 else math.pi / 2
                    a2 = dst[:, hsl]
                    ak = ("rope%d" % which, hv)
                    rope_ops.append(lambda shift=shift: T.op(DVE, lambda: nc.vector.tensor_scalar(out=kf[:], in0=ang[:], scalar1=shift, scalar2=1.0 / TWO_PI,
                                                                                                  op0=ALU.add, op1=ALU.mult), reads=["ang"], writes=["kf"]))
                    rope_ops.append(lambda: T.op(DVE, lambda: nc.vector.tensor_copy(out=posi[:], in_=kf[:]), reads=["kf"], writes=["posi"]))
                    rope_ops.append(lambda: T.op(DVE, lambda: nc.vector.tensor_copy(out=kf[:], in_=posi[:]), reads=["posi"], writes=["kf"]))
                    rope_ops.append(lambda a2=a2, ak=ak: T.op(DVE, lambda: nc.vector.scalar_tensor_tensor(out=a2, in0=kf[:], scalar=-TWO_PI, in1=ang[:],
                                                                                                         op0=ALU.mult, op1=ALU.add),
                                                              reads=["kf", "ang"], writes=[ak]))
                    rope_ops.append(lambda a2=a2, ak=ak, shift=shift: T.op(DVE, lambda: nc.vector.tensor_scalar(out=a2, in0=a2, scalar1=shift, scalar2=-math.pi,
                                                                                                               op0=ALU.add, op1=ALU.max),
                                                                           reads=[ak], writes=[ak]))
                    rope_ops.append(lambda a2=a2, ak=ak: T.op(DVE, lambda: nc.vector.tensor_scalar(out=a2, in0=a2, scalar1=math.pi, scalar2=None, op0=ALU.min),
                                                              reads=[ak], writes=[ak]))
                    rope_ops.append(lambda a2=a2, ak=ak: T.op(ACT, lambda: nc.scalar.activation(out=a2, in_=a2, func=AF.Sin), reads=[ak], writes=[ak]))
                rope_ops.append(lambda hsl=hsl, hv=hv: T.op(DVE, lambda: nc.vector.tensor_scalar(out=ssinT[:, hsl], in0=ssinT[:, hsl], scalar1=cols[0:64, 7:8],
                                                                                                 scalar2=None, op0=ALU.mult),
                                                            reads=[("rope0", hv), C_ALL], writes=[("rope0", hv)]))

            def rope_trickle(i):
                for _ in range(3):
                    if rope_ops:
                        rope_ops.pop(0)()

            def p1_group_pe(i):
                if i % 4 != 3:
                    return
                i0 = i - 3
                for kc in range(KC):
                    for j in range(4):
                        sx = (i0 + j) % NXS
                        T.op(PE, lambda kc=kc, j=j, sx=sx: nc.tensor.transpose(out=ps[:, kc, j * P:(j + 1) * P], in_=xt[:, sx, kc * P:(kc + 1) * P],
                                                                               identity=ident_f[:]),
                             reads=[("xt", sx), "ident_f"], writes=BK(kc), inc=(j == 3))

            def p1_group_evac(i):
                if i % 4 != 3:
                    return
                i0 = i - 3
                for kc in range(KC):
                    dst = hT[:, kc, i0 * P:(i0 + 4) * P]
                    wk = [("hT", i0 + j) for j in range(4)]
                    if kc % 2 == 0:
                        T.op(ACT, lambda kc=kc, dst=dst: nc.scalar.activation(out=dst, in_=ps[:, kc, :], func=AF.Identity, scale=lnc[:, 0, kc:kc + 1],
                                                                              bias=lnc[:, 1, kc:kc + 1]), reads=BK(kc) + [C_ALL], writes=wk)
                    else:
                        T.op(DVE, lambda kc=kc, dst=dst: nc.vector.tensor_scalar(out=dst, in0=ps[:, kc, :], scalar1=lnc[:, 0, kc:kc + 1],
                                                                                 scalar2=lnc[:, 1, kc:kc + 1], op0=ALU.mult, op1=ALU.add),
                             reads=BK(kc) + [C_ALL], writes=wk)

            pipeline([
                lambda i: (T.dma(SP, d_x[i % NXS], xt[:, i % NXS, :], x_d[i * P:(i + 1) * P, :], writes=[("xt", i % NXS)]) if i >= NPRE else None),
                lambda i: ln_stats_a(xt[:, i % NXS, :], ("xt", i % NXS), i % NXS),
                lambda i: ln_stats_b(i % NXS),
                lambda i: ln_apply(xt[:, i % NXS, :], ("xt", i % NXS), i % NXS, keep=(lnst[:, i, :], ("lnst", i))),
                p1_group_pe,
                p1_group_evac,
                rope_trickle,
            ], NT)
            while rope_ops:
                rope_ops.pop(0)()
            bt32 = carve_f(0, 2 * D, 64).rearrange("p (r d) -> p r d", d=D)
            bh16 = carve_b(2048, 2 * D, 64).rearrange("p (r d) -> p r d", d=D)
            fl = lambda t: t.rearrange("p r d -> p (r d)")
            T.op(POOL, lambda: nc.gpsimd.memset(fl(bt32), 0.0), writes=["bt32"])
            for r_, (lrow, prt) in enumerate(((1, 0), (1, 32), (3, 0), (3, 32))):
                T.dma(SP, d_c2[5], bt32[prt:prt + 1, r_ // 2, :], lnv_d[lrow:lrow + 1, :], reads=[], writes=["bt32"])
            T.op(DVE, lambda: nc.vector.tensor_scalar(out=fl(bt32), in0=fl(bt32), scalar1=ALPHA, scalar2=None, op0=ALU.mult), reads=["bt32"], writes=["bt32"])
            T.op(DVE, lambda: nc.vector.tensor_copy(out=fl(bh16), in_=fl(bt32)), reads=["bt32"], writes=["bh16"])
            T.op(DVE, lambda: nc.vector.tensor_tensor(out=fl(bt32), in0=fl(bt32), in1=fl(bh16), op=ALU.subtract), reads=["bt32", "bh16"], writes=["bt32"])
            T.op(DVE, lambda: nc.vector.tensor_copy(out=brow[0:32, :, :], in_=bh16[0:32, :, :]), reads=["bh16", "brow"], writes=["brow"])
            T.op(DVE, lambda: nc.vector.tensor_copy(out=brow[32:64, :, :], in_=bt32[32:64, :, :]), reads=["bt32", "brow"], writes=["brow"])

            T.barrier(ALL)
        if stage == 1:
            dbg_dump([(hT[:, kc, 0:1024], 1024) for kc in range(8)])
            return nc

        with ExitStack() as p2:
            qkT = carve_b(0, 4 * S).rearrange("p (c s) -> p c s", s=S)
            vext = carve_b(4096, NT * 4 * 130).rearrange("p (i h d) -> p i h d", h=4, d=130)
            kpeT = carve_b(12352, S)
            qTn = carve_b(13376, S)
            qTr = carve_b(14400, S)
            kTn2 = sb("kTn2", [P, 2, S], BF16, p2)
            qTn2 = sb("qTn2", [P, S], BF16, p2)
            qTr2 = sb("qTr2", [P, S], BF16, p2)
            PT = sb("PT", [P, 3, 512], BF16, p2)
            scr = sb("scr", [P, 512], BF16, p2)
            qn = sb("qn", [P, 4, 512], BF16, p2)
            ssq = sb("ssq", [P, 4, 2], F32, p2)
            sd2 = sb("sd2", [P, 4, 2], F32, p2)
            rs2 = sb("rs2", [P, 4, 2], F32, p2)
            rt1 = sb("rt1", [64, 512], F32, p2)
            rt2 = sb("rt2", [64, 512], F32, p2)
            fin = sb("fin", [P, 2, 8, 4], F32, p2)
            onb = sb("onb", [P, 4, P], BF16, p2)

            T.op(POOL, lambda: nc.gpsimd.memset(vext[:], 1.0), writes=["vext"])
            T.op(POOL, lambda: nc.gpsimd.memset(kpeT[64:128, :], 0.0), writes=["pe_pad"], inc=False)
            T.op(POOL, lambda: nc.gpsimd.memset(qTr2[64:128, :], 0.0), writes=["pe_pad"], inc=False)
            T.op(POOL, lambda: nc.gpsimd.memset(qTr[64:128, :], 0.0), writes=["pe_pad"])

            def rope_combine(bA, bB, dst, dkey, blk):
                sl = slice(blk * 512, (blk + 1) * 512)
                T.op(DVE, lambda: nc.vector.tensor_tensor(out=rt1[:], in0=ps[0:64, bA, :], in1=cosT[:, sl], op=ALU.mult),
                     reads=BK(bA), writes=["rt1"])
                T.op(DVE, lambda: nc.vector.tensor_tensor(out=rt2[:], in0=ps[0:64, bB, :], in1=ssinT[:, sl], op=ALU.mult),
                     reads=BK(bB), writes=["rt2"])
                T.op(POOL, lambda: nc.gpsimd.tensor_tensor(out=dst[0:64, sl], in0=rt1[:], in1=rt2[:], op=ALU.add),
                     reads=["rt1", "rt2"], writes=[dkey])

            kr_steps = []
            for blk in range(4):
                def kr_a(blk=blk):
                    sl = slice(blk * 512, (blk + 1) * 512)
                    for kc in range(KC):
                        T.op(PE, lambda kc=kc: nc.tensor.matmul(ps[0:64, 5, :], lhsT=wmla[:, kc, 512:576], rhs=hT[:, kc, sl],
                                                                start=(kc == 0), stop=(kc == KC - 1)),
                             reads=["wmla"] + hkeys("hT", blk * 512, blk * 512 + 512), writes=BK(5), inc=(kc == KC - 1))

                def kr_b(blk=blk):
                    sl = slice(blk * 512, (blk + 1) * 512)
                    for kc in range(KC):
                        T.op(PE, lambda kc=kc: nc.tensor.matmul(ps[0:64, 6, :], lhsT=wkrsw[:, kc, :], rhs=hT[:, kc, sl],
                                                                start=(kc == 0), stop=(kc == KC - 1)),
                             reads=["wkrsw"] + hkeys("hT", blk * 512, blk * 512 + 512), writes=BK(6), inc=(kc == KC - 1))

                def kr_c(blk=blk):
                    rope_combine(5, 6, kpeT, ("kpeT", blk), blk)
                kr_steps += [kr_a, kr_b, kr_c]

            def kr_trickle(i):
                if kr_steps:
                    kr_steps.pop(0)()

            inv384 = 1.0 / math.sqrt(384.0)
            inv128 = 1.0 / math.sqrt(128.0)
            def qa_s0(i):
                b = i % 3
                for kc in range(KC):
                    T.op(PE, lambda kc=kc: nc.tensor.matmul(ps[:, b, :], lhsT=hT[:, kc, i * P:(i + 1) * P], rhs=wmla[:, kc, 0:512],
                                                            start=(kc == 0), stop=(kc == KC - 1)),
                         reads=["wmla", ("hT", i)], writes=BK(b), inc=(kc == KC - 1))

            def qa_s1(i):
                b = i % 3
                s2 = i % 4
                T.op(ACT, lambda: nc.scalar.activation(out=scr[:, 0:384], in_=ps[:, b, 0:384], func=AF.Square, scale=inv384,
                                                       accum_out=ssq[:, s2, 0:1]), reads=BK(b), writes=["scr", ("ssq", s2)])
                T.op(ACT, lambda: nc.scalar.activation(out=scr[:, 384:512], in_=ps[:, b, 384:512], func=AF.Square, scale=inv128,
                                                       accum_out=ssq[:, s2, 1:2]), reads=BK(b), writes=["scr", ("ssq", s2)])
                rstd_from(ssq[:, s2, :], rs2[:, s2, :], [("ssq", s2)], ("rs2", s2), sd2[:, s2, :], ("sd2", s2))

            def qa_s2(i):
                b = i % 3
                s2 = i % 4
                T.op(DVE, lambda: nc.vector.scalar_tensor_tensor(out=qn[:, s2, 0:384], in0=ps[:, b, 0:384], scalar=rs2[:, s2, 0:1],
                                                                 in1=gqkvB[:, 0:384], op0=ALU.mult, op1=ALU.mult),
                     reads=BK(b) + [("rs2", s2), "gqkvB"], writes=[("qn", s2)], inc=False)
                T.op(DVE, lambda: nc.vector.scalar_tensor_tensor(out=qn[:, s2, 384:512], in0=ps[:, b, 384:512], scalar=rs2[:, s2, 1:2],
                                                                 in1=gqkvB[:, 384:512], op0=ALU.mult, op1=ALU.mult),
                     reads=BK(b) + [("rs2", s2), "gqkvB"], writes=[("qn", s2)])

            def qa_s3(i):
                bt = 3 + (i % 2)
                s2 = i % 4
                for c in range(4):
                    T.op(PE, lambda c=c: nc.tensor.transpose(out=psb[:, bt, c * P:(c + 1) * P], in_=qn[:, s2, c * P:(c + 1) * P],
                                                             identity=ident_b[:]),
                         reads=[("qn", s2), "ident_b"], writes=BK(bt), inc=(c == 3))

            def qa_s4(i):
                bt = 3 + (i % 2)
                T.op(DVE, lambda: nc.vector.tensor_copy(out=qkT[:, :, i * P:(i + 1) * P],
                                                        in_=psb[:, bt, 0:512].rearrange("p (c t) -> p c t", t=P)),
                     reads=BK(bt), writes=[("qkT", i)])

            pipeline([kr_trickle, qa_s0, qa_s1, qa_s2, qa_s3, qa_s4], NT)
            while kr_steps:
                kr_steps.pop(0)()

            if stage == 2:
                dbg_dump([(qkT[:, c, 0:1024], 1024) for c in range(4)] + [(kpeT[0:64, 0:1024], 1024), (cosT[:, 0:1024], 1024), (ssinT[:, 0:1024], 1024)])
                return nc

            for i in range(NT):
                b = i % 2
                T.op(PE, lambda: nc.tensor.matmul(ps[:, b, :], lhsT=qkT[:, 3, i * P:(i + 1) * P], rhs=wkvv[:, :], start=True, stop=True),
                     reads=["wkvv", ("qkT", i)], writes=BK(b))
                T.op(DVE, lambda: nc.vector.tensor_copy(out=vext[:, i, :, 0:128], in_=ps[:, b, :].rearrange("p (h d) -> p h d", d=P)),
                     reads=BK(b), writes=["vext"])

            qTn_s = [qTn, qTn2[:, :]]
            qTr_s = [qTr, qTr2[:, :]]
            kTn_s = [kTn2[:, 0, :], kTn2[:, 1, :]]

            def proj_steps(h, blk, banks):
                st = h % 2
                bq, br, bs_, bk = banks
                sl = slice(blk * 512, (blk + 1) * 512)
                qk_keys = hkeys("qkT", blk * 512, blk * 512 + 512)

                def s_q():
                    for c in range(3):
                        T.op(PE, lambda c=c: nc.tensor.matmul(ps[:, bq, :], lhsT=wqb[:, c, h * 192:h * 192 + 128], rhs=qkT[:, c, sl],
                                                              start=(c == 0), stop=(c == 2)),
                             reads=["wqb"] + qk_keys, writes=BK(bq), inc=(c == 2))
                    T.op(DVE, lambda: nc.vector.tensor_copy(out=qTn_s[st][:, sl], in_=ps[:, bq, :]), reads=BK(bq), writes=[("qTn", st, blk)])

                def s_r():
                    for c in range(3):
                        T.op(PE, lambda c=c: nc.tensor.matmul(ps[0:64, br, :], lhsT=wqb[:, c, h * 192 + 128:h * 192 + 192], rhs=qkT[:, c, sl],
                                                              start=(c == 0), stop=(c == 2)),
                             reads=["wqb"] + qk_keys, writes=BK(br), inc=(c == 2))
                    T.op(DVE, lambda: nc.vector.tensor_tensor(out=rt1[:], in0=ps[0:64, br, :], in1=cosT[:, sl], op=ALU.mult),
                         reads=BK(br), writes=["rt1"])

                def s_s():
                    for c in range(3):
                        T.op(PE, lambda c=c: nc.tensor.matmul(ps[0:64, bs_, :], lhsT=wqbsw[:, c, h * 64:(h + 1) * 64], rhs=qkT[:, c, sl],
                                                              start=(c == 0), stop=(c == 2)),
                             reads=["wqbsw"] + qk_keys, writes=BK(bs_), inc=(c == 2))
                    T.op(DVE, lambda: nc.vector.tensor_tensor(out=rt2[:], in0=ps[0:64, bs_, :], in1=ssinT[:, sl], op=ALU.mult),
                         reads=BK(bs_), writes=["rt2"])
                    T.op(POOL, lambda: nc.gpsimd.tensor_tensor(out=qTr_s[st][0:64, sl], in0=rt1[:], in1=rt2[:], op=ALU.add),
                         reads=["rt1", "rt2"], writes=[("qTr", st, blk)])

                def s_k():
                    T.op(PE, lambda: nc.tensor.matmul(ps[:, bk, :], lhsT=wkvk[:, h * P:(h + 1) * P], rhs=qkT[:, 3, sl], start=True, stop=True),
                         reads=["wkvk"] + qk_keys, writes=BK(bk))
                    T.op(DVE, lambda: nc.vector.tensor_copy(out=kTn_s[st][:, sl], in_=ps[:, bk, :]), reads=BK(bk), writes=[("kTn", st, blk)])
                return [s_q, s_r, s_s, s_k]

            def emit_proj(h, blk, banks):
                for f in proj_steps(h, blk, banks):
                    f()

            for blk in range(4):
                emit_proj(0, blk, (0, 2, 3, 1) if blk % 2 == 0 else (4, 6, 7, 5))
            if stage == 3:
                dbg_dump([(qTn[:, 0:1024], 1024), (qTr[0:64, 0:1024], 1024), (kTn2[:, 0, 0:1024], 1024),
                          (vext[:, 0, :, :].rearrange("p h d -> p (h d)"), 520)])
                return nc

            deferred = []
            for h in range(4):
                st = h % 2
                qTn_h, qTr_h, kTn_h = qTn_s[st], qTr_s[st], kTn_s[st]

                def emit_qk(n):
                    qb, kt = divmod(n, NT)
                    bs = n % 3
                    ksl = slice(kt * P, (kt + 1) * P)
                    qsl = slice(qb * 512, (qb + 1) * 512)
                    T.op(PE, lambda: nc.tensor.matmul(ps[:, bs, :], lhsT=kTn_h[:, ksl], rhs=qTn_h[:, qsl], start=True, stop=False),
                         reads=[("kTn", st, kt // 4), ("qTn", st, qb)], writes=BK(bs), inc=False)
                    T.op(PE, lambda: nc.tensor.matmul(ps[:, bs, :], lhsT=kpeT[:, ksl], rhs=qTr_h[:, qsl], start=False, stop=True),
                         reads=[("kpeT", kt // 4), ("qTr", st, qb), "pe_pad"], writes=BK(bs))

                def fin_parts(qb, h=h):
                    ob = 4 + 2 * (qb % 2)
                    qsl = slice(qb * 512, (qb + 1) * 512)
                    kf_ = ("fin", qb % 2)
                    fv = fin[:, qb % 2, :, :]

                    def part_a():
                        for qi in range(4):
                            bo = ob + qi // 2
                            o0 = (qi % 2) * 130
                            T.op(ACT, lambda qi=qi, bo=bo, o0=o0: nc.scalar.activation(out=scr[:, 0:128], in_=ps[:, bo, o0:o0 + 128], func=AF.Square,
                                                                                       scale=inv128, accum_out=fv[:, 0, qi:qi + 1]),
                                 reads=BK(bo), writes=["scr", kf_])
                        for qi in range(4):
                            bo = ob + qi // 2
                            o0 = (qi % 2) * 130
                            T.op(DVE, lambda qi=qi, bo=bo, o0=o0: nc.vector.tensor_copy(out=fv[:, 1, qi:qi + 1], in_=ps[:, bo, o0 + 128:o0 + 129]),
                                 reads=BK(bo), writes=[kf_])
                        T.op(DVE, lambda: nc.vector.tensor_tensor(out=fv[:, 2, :], in0=fv[:, 1, :], in1=fv[:, 1, :], op=ALU.mult), reads=[kf_], writes=[kf_])
                        T.op(DVE, lambda: nc.vector.scalar_tensor_tensor(out=fv[:, 3, :], in0=fv[:, 2, :], scalar=EPS, in1=fv[:, 0, :],
                                                                         op0=ALU.mult, op1=ALU.add), reads=[kf_], writes=[kf_])

                    def part_b():
                        T.op(ACT, lambda: nc.scalar.activation(out=fv[:, 4, :], in_=fv[:, 3, :], func=AF.Ln), reads=[kf_], writes=[kf_])
                        T.op(ACT, lambda: nc.scalar.activation(out=fv[:, 5, :], in_=fv[:, 4, :], func=AF.Exp, scale=-0.5), reads=[kf_], writes=[kf_])
                        for qi in range(4):
                            bo = ob + qi // 2
                            o0 = (qi % 2) * 130
                            T.op(DVE, lambda qi=qi, bo=bo, o0=o0: nc.vector.tensor_scalar(out=onb[:, qi, :], in0=ps[:, bo, o0:o0 + 128],
                                                                                          scalar1=fv[:, 5, qi:qi + 1], scalar2=None, op0=ALU.mult),
                                 reads=BK(bo) + [kf_], writes=[("onb", qi)])

                    def part_c():
                        for qi in range(4):
                            T.op(PE, lambda qi=qi: nc.tensor.transpose(out=psb[:, 3, qi * P:(qi + 1) * P], in_=onb[:, qi, :], identity=ident_b[:]),
                                 reads=[("onb", qi), "ident_b"], writes=BK(3), inc=(qi == 3))
                        T.op(DVE, lambda: nc.vector.tensor_scalar(out=catT[:, 4 + h, qsl], in0=psb[:, 3, 0:512], scalar1=cols[:, 4:5], scalar2=None,
                                                                  op0=ALU.mult),
                             reads=BK(3) + [C_ALL], writes=[("catT", 4 + h, qb)])
                    return part_a, part_b, part_c

                NIT = 4 * NT
                emit_qk(0)
                emit_qk(1)
                for n in range(NIT):
                    qb, kt = divmod(n, NT)
                    if n + 2 < NIT:
                        emit_qk(n + 2)
                    bs = n % 3
                    pslot = n % 3
                    T.op(ACT, lambda: nc.scalar.activation(out=PT[:, pslot, :], in_=ps[:, bs, :], func=AF.Exp, scale=SCALE),
                         reads=BK(bs), writes=[("PT", pslot)])
                    ob = 4 + 2 * (qb % 2)
                    for qi in range(4):
                        bo = ob + qi // 2
                        o0 = (qi % 2) * 130
                        T.op(PE, lambda qi=qi, bo=bo, o0=o0: nc.tensor.matmul(ps[:, bo, o0:o0 + 130], lhsT=PT[:, pslot, qi * P:(qi + 1) * P],
                                                                              rhs=vext[:, kt, h, :], start=(kt == 0 and qi % 2 == 0),
                                                                              stop=(kt == NT - 1), skip_group_check=True),
                             reads=[("PT", pslot), "vext"], writes=BK(bo), inc=(qi % 2 == 1))
                    gn = h * NIT + n
                    for item in [d_ for d_ in deferred if d_[0] <= gn]:
                        item[1]()
                        deferred.remove(item)
                    if kt == NT - 1:
                        pa, pb_, pc_ = fin_parts(qb)
                        deferred.append((gn + 2, pa))
                        deferred.append((gn + 4, pb_))
                        deferred.append((gn + 7, pc_))
                    if h + 1 < 4 and kt in (3, 6, 9, 12):
                        proj_steps(h + 1, qb, (3, 3, 3, 3))[kt // 3 - 1]()
            while deferred:
                item = deferred.pop(0)
                item[1]()
            T.barrier(ALL)
        pw.close()
        if stage == 4:
            dbg_dump([(catT[:, 4 + h, 0:1024], 1024) for h in range(4)])
            return nc

        for rnd in range(2):
            with ExitStack() as p3:
                qsT = carve_f(0, 2 * S).rearrange("p (h s) -> p h s", s=S)
                opart = carve_f(4096, NT * 256).rearrange("p (i v) -> p i v", v=256)
                qtT = carve_b(8192, 4 * S).rearrange("p (c s) -> p c s", s=S)
                ktT = carve_b(12288, 4 * S).rearrange("p (c s) -> p c s", s=S)
                vtok = sb("vtok", [P, NT, 256], BF16, p3)
                sg = sb("sg", [P, NT, 256], BF16, p3)
                dA = sb("dA", [P, 4, NT], F32, p3)
                dB = sb("dB", [P, 4, NT], F32, p3)
                dM = sb("dM", [P, 4, NT], F32, p3)
                pg = p3.enter_context(ExitStack())
                wrf = sb("wrf", [P, 2, KC, 256], BF16, pg)
                pq = pg.enter_context(ExitStack())
                wrq = sb("wrq", [P, 3, KC, 256], BF16, pq)
                for gi, c0 in ((1, 512), (2, 2048), (0, 0)):
                    T.dma(POOL, d_w[gi], wrq[:, gi, :, :], w_in_d[:, :, c0 + rnd * 256:c0 + rnd * 256 + 256], writes=[("wrq", gi)])
                for gi, c0 in enumerate((1024, 1536)):
                    T.dma(POOL, d_w[3 + gi], wrf[:, gi, :, :], w_in_d[:, :, c0 + rnd * 256:c0 + rnd * 256 + 256], writes=[("wrf", gi)])

                def vg_mm(i):
                    for half, gi in ((0, 1), (1, 2)):
                        b = 2 * half + i % 2
                        for kc in range(KC):
                            T.op(PE, lambda kc=kc, b=b, gi=gi: nc.tensor.matmul(ps[:, b, 0:256], lhsT=hT[:, kc, i * P:(i + 1) * P], rhs=wrq[:, gi, kc, :],
                                                                              start=(kc == 0), stop=(kc == KC - 1)),
                                 reads=[("wrq", gi), ("hT", i)], writes=BK(b), inc=(kc == KC - 1))

                def vg_ev(i):
                    T.op(DVE, lambda: nc.vector.tensor_copy(out=vtok[:, i, :], in_=ps[:, i % 2, 0:256]), reads=BK(i % 2), writes=[("vtok", i)])
                    T.op(ACT, lambda: nc.scalar.activation(out=sg[:, i, :], in_=ps[:, 2 + i % 2, 0:256], func=AF.Silu), reads=BK(2 + i % 2), writes=[("sg", i)])

                pipeline([vg_mm, vg_ev], NT)

                def q_mm(n):
                    hh, blk = divmod(n, 4)
                    sl = slice(blk * 512, (blk + 1) * 512)
                    b = 6 + n % 2
                    for kc in range(KC):
                        T.op(PE, lambda kc=kc: nc.tensor.matmul(ps[:, b, :], lhsT=wrq[:, 0, kc, hh * P:(hh + 1) * P], rhs=hT[:, kc, sl],
                                                                start=(kc == 0), stop=(kc == KC - 1)),
                             reads=[("wrq", 0)] + hkeys("hT", blk * 512, blk * 512 + 512), writes=BK(b), inc=(kc == KC - 1))

                def q_ev(n):
                    hh, blk = divmod(n, 4)
                    sl = slice(blk * 512, (blk + 1) * 512)
                    b = 6 + n % 2
                    T.op(ACT, lambda: nc.scalar.activation(out=qsT[:, hh, sl], in_=ps[:, b, :], func=AF.Silu), reads=BK(b), writes=[("qsT", hh, blk)])

                pipeline([q_mm, q_ev], 8)
                T.barrier(ALL)
                pq.close()

                ge = sb("ge", [P, 2, 512], F32, pg)
                gl2 = sb("gl2", [P, 2, 512], F32, pg)
                gl1 = sb("gl1", [P, 2, 512], F32, pg)
                gsp = sb("gsp", [P, 2, 512], F32, pg)
                gsn = sb("gsn", [P, 3, 512], F32, pg)
                gG = sb("gG", [P, 512], F32, pg)
                grel = sb("grel", [P, 2, 512], F32, pg)
                grl2 = sb("grl2", [P, 2, 512], F32, pg)
                gE1 = sb("gE1", [P, 2, 512], F32, pg)
                gE2 = sb("gE2", [P, 2, 512], F32, pg)
                gsm = sb("gsm", [P, 2, 16], F32, pg)

                def piece(n):
                    d, r = divmod(n, 8)
                    hh, blk = divmod(r, 4)
                    return d, hh, blk, d * 2 + hh, d * 4 + rnd * 2 + hh

                def g_s0(n):
                    d, hh, blk, ci, lcol = piece(n)
                    sl = slice(blk * 512, (blk + 1) * 512)
                    b = 4 + n % 2
                    for kc in range(KC):
                        T.op(PE, lambda kc=kc: nc.tensor.matmul(ps[:, b, :], lhsT=wrf[:, d, kc, hh * P:(hh + 1) * P], rhs=hT[:, kc, sl],
                                                                start=(kc == 0), stop=(kc == KC - 1)),
                             reads=[("wrf", d)] + hkeys("hT", blk * 512, blk * 512 + 512), writes=BK(b), inc=(kc == KC - 1))

                def g_s1_ops(n):
                    d, hh, blk, ci, lcol = piece(n)
                    b = 4 + n % 2
                    s2 = n % 2
                    return [
                        lambda: T.op(ACT, lambda: nc.scalar.activation(out=ge[:, s2, :], in_=ps[:, b, :], func=AF.Exp, scale=-1.0), reads=BK(b), writes=[("ge", s2)]),
                        lambda: T.op(ACT, lambda: nc.scalar.activation(out=gl2[:, s2, :], in_=ge[:, s2, :], func=AF.Ln, bias=cst[:, 3:4]),
                                     reads=[("ge", s2), "cst"], writes=[("gl2", s2)]),
                        lambda: T.op(ACT, lambda: nc.scalar.activation(out=gsp[:, s2, :], in_=gl2[:, s2, :], func=AF.Exp, scale=-1.0),
                                     reads=[("gl2", s2)], writes=[("gsp", s2)]),
                        lambda: T.op(ACT, lambda: nc.scalar.activation(out=gl1[:, s2, :], in_=gsp[:, s2, :], func=AF.Ln, scale=omlb[:, lcol:lcol + 1],
                                                                       bias=lbT[:, lcol:lcol + 1]), reads=[("gsp", s2), "lbT", "omlb"], writes=[("gl1", s2)]),
                    ]

                def g_s2(n):
                    d, hh, blk, ci, lcol = piece(n)
                    s2 = n % 2
                    s3 = n % 3
                    g_ap = gl1[:, s2, :]
                    rel = grel[:, s2, :]
                    R3 = rel.rearrange("p (c t) -> p c t", t=P)
                    rkey = ("grel", s2)
                    sm = gsm[:, s2, :]
                    if d == 0:
                        T.op(DVE, lambda: nc.vector.tensor_tensor_scan(out=rel, data0=m0[:], data1=g_ap, initial=0.0, op0=ALU.mult, op1=ALU.add),
                             reads=[("gl1", s2), "m0"], writes=[rkey])
                        G3 = R3
                        gkey = rkey
                    else:
                        T.op(DVE, lambda: nc.vector.tensor_tensor_scan(out=gG[:], data0=m0[:], data1=g_ap, initial=0.0, op0=ALU.mult, op1=ALU.add),
                             reads=[("gl1", s2), "m0"], writes=["gG"])
                        G3 = gG[:].rearrange("p (c t) -> p c t", t=P)
                        gkey = "gG"
                    T.op(DVE, lambda: nc.vector.tensor_scalar(out=gsn[:, s3, :], in0=gsp[:, s2, :], scalar1=-1.0, scalar2=1.0, op0=ALU.mult, op1=ALU.add),
                         reads=[("gsp", s2)], writes=[("gsn", s3)])
                    if d == 1:
                        T.op(DVE, lambda: nc.vector.tensor_tensor(out=rel, in0=gG[:], in1=g_ap, op=ALU.subtract), reads=["gG", ("gl1", s2)], writes=[rkey])
                    T.op(DVE, lambda: nc.vector.tensor_copy(out=sm[:, 0:4].unsqueeze(2), in_=G3[:, :, 127:128]), reads=[gkey], writes=[("gsm", s2)])
                    T.op(DVE, lambda: nc.vector.tensor_copy(out=sm[:, 4:8].unsqueeze(2), in_=R3[:, :, 64:65]), reads=[rkey], writes=[("gsm", s2)])
                    T.op(DVE, lambda: nc.vector.tensor_tensor(out=grl2[:, s2, :].rearrange("p (c t) -> p c t", t=P), in0=R3,
                                                              in1=R3[:, :, 64:65].broadcast_to([P, 4, P]), op=ALU.subtract),
                         reads=[rkey], writes=[("grl2", s2)])
                    T.op(DVE, lambda: nc.vector.tensor_tensor(out=sm[:, 8:12], in0=sm[:, 0:4], in1=sm[:, 4:8], op=ALU.subtract),
                         reads=[("gsm", s2)], writes=[("gsm", s2)])

                def g_s3_ops(n):
                    d, hh, blk, ci, lcol = piece(n)
                    s2 = n % 2
                    cs = slice(blk * 4, blk * 4 + 4)
                    sm = gsm[:, s2, :]
                    sgn = 1.0 if d == 0 else -1.0
                    return [
                        lambda: T.op(ACT, lambda: nc.scalar.activation(out=gE1[:, s2, :], in_=grl2[:, s2, :], func=AF.Exp, scale=sgn),
                                     reads=[("grl2", s2)], writes=[("gE1", s2)]),
                        lambda: T.op(ACT, lambda: nc.scalar.activation(out=dA[:, ci, cs], in_=sm[:, 0:4], func=AF.Exp), reads=[("gsm", s2)], writes=[("dA", ci, blk)]),
                        lambda: T.op(ACT, lambda: nc.scalar.activation(out=gE2[:, s2, :], in_=grl2[:, s2, :], func=AF.Exp, scale=-sgn),
                                     reads=[("grl2", s2)], writes=[("gE2", s2)]),
                        lambda: T.op(ACT, lambda: nc.scalar.activation(out=(dM if d == 0 else dB)[:, ci, cs], in_=sm[:, 4:8], func=AF.Exp),
                                     reads=[("gsm", s2)], writes=[("dMB", ci, blk)]),
                        lambda: T.op(ACT, lambda: nc.scalar.activation(out=(dB if d == 0 else dM)[:, ci, cs], in_=sm[:, 8:12], func=AF.Exp),
                                     reads=[("gsm", s2)], writes=[("dBM", ci, blk)]),
                    ]

                def g_s13(step_n):
                    o1 = g_s1_ops(step_n) if step_n < 16 else []
                    o3 = g_s3_ops(step_n - 2) if 0 <= step_n - 2 < 16 else []
                    while o1 or o3:
                        if o1:
                            o1.pop(0)()
                        if o3:
                            o3.pop(0)()

                def g_s4(n):
                    d, hh, blk, ci, lcol = piece(n)
                    s2 = n % 2
                    s3 = n % 3
                    sl = slice(blk * 512, (blk + 1) * 512)
                    T.op(DVE, lambda: nc.vector.tensor_tensor(out=qtT[:, ci, sl], in0=qsT[:, hh, sl], in1=gE1[:, s2, :], op=ALU.mult),
                         reads=[("qsT", hh, blk), ("gE1", s2)], writes=[("qtT", ci, blk)])
                    T.op(DVE, lambda: nc.vector.scalar_tensor_tensor(out=ktT[:, ci, sl], in0=gsn[:, s3, :], scalar=omlb[:, lcol:lcol + 1],
                                                                     in1=gE2[:, s2, :], op0=ALU.mult, op1=ALU.mult),
                         reads=[("gsn", s3), ("gE2", s2), "omlb"], writes=[("ktT", ci, blk)])

                for gstep in range(16 + 5):
                    if gstep < 16:
                        g_s0(gstep)
                    if 0 <= gstep - 1 < 18:
                        g_s13(gstep - 1)
                    if 0 <= gstep - 2 < 16:
                        g_s2(gstep - 2)
                    if 0 <= gstep - 4 < 16:
                        g_s4(gstep - 4)
                if stage == 5 and rnd == 0:
                    dbg_dump([(qtT[:, 0, 0:512], 512), (ktT[:, 0, 0:512], 512), (qtT[:, 2, 0:512], 512), (ktT[:, 2, 0:512], 512),
                              (dA[:, :, :].rearrange("p a b -> p (a b)"), 64), (dB[:, :, :].rearrange("p a b -> p (a b)"), 64),
                              (dM[:, :, :].rearrange("p a b -> p (a b)"), 64), (qsT[:, 0, 0:512], 512), (vtok[:, 0, :], 256), (sg[:, 0, :], 256)])
                    return nc

                T.barrier(ALL)
                pg.close()
                if rnd == 1:
                    for c4 in range(4):
                        hc_, kh = divmod(c4, 2)
                        T.dma(POOL, d_w[12 + hc_], wout_v[:, 4 * kh:4 * kh + 4, hc_ * 512:(hc_ + 1) * 512],
                              w_out_d[:, 4 * kh:4 * kh + 4, hc_ * 512:(hc_ + 1) * 512], writes=[("wout", hc_)])
                St = sb("St", [P, 4, P], F32, p3)
                Stmp = sb("Stmp", [P, 4, P], F32, p3)
                Sb = sb("Sb", [P, 2, 4, P], BF16, p3)
                osum = sb("osum", [P, 4, 4, P], F32, p3)
                onh = sb("onh", [P, 2, 4, P], BF16, p3)
                fh = sb("fh", [P, 4, 4, 4], F32, p3)
                scr3 = sb("scr3", [P, P], BF16, p3)
                Am_all = sb("Am_all", [P, 4, NT, P], BF16, p3)
                ktok_all = sb("ktok_all", [P, 4, NT, P], BF16, p3)

                def chunk_of(ci, step):
                    return step if ci < 2 else NT - 1 - step

                nb1 = 0
                for ci in range(4):
                    mask = maskF if ci < 2 else maskB
                    mkey = "maskF" if ci < 2 else "maskB"
                    for cg in range(4):
                        ba = nb1 % 2
                        bt = 2 + nb1 % 2
                        nb1 += 1
                        for j in range(4):
                            c = cg * 4 + j
                            csl = slice(c * P, (c + 1) * P)
                            T.op(PE, lambda j=j, csl=csl: nc.tensor.matmul(ps[:, ba, j * P:(j + 1) * P], lhsT=ktT[:, ci, csl], rhs=qtT[:, ci, csl],
                                                                           start=True, stop=True), writes=BK(ba), inc=(j == 3))
                        for j in range(4):
                            c = cg * 4 + j
                            csl = slice(c * P, (c + 1) * P)
                            T.op(PE, lambda j=j, csl=csl: nc.tensor.transpose(out=psb[:, bt, j * P:(j + 1) * P], in_=ktT[:, ci, csl], identity=ident_b[:]),
                                 reads=["ident_b"], writes=BK(bt), inc=(j == 3))
                        T.op(DVE, lambda: nc.vector.tensor_tensor(out=Am_all[:, ci, cg * 4:(cg + 1) * 4, :],
                                                                  in0=ps[:, ba, :].rearrange("p (j t) -> p j t", t=P),
                                                                  in1=mask[:].unsqueeze(1).broadcast_to([P, 4, P]), op=ALU.mult),
                             reads=BK(ba) + [mkey], writes=[("Am", ci, cg)])
                        T.op(ACT, lambda: nc.scalar.activation(out=ktok_all[:, ci, cg * 4:(cg + 1) * 4, :],
                                                               in_=psb[:, bt, 0:512].rearrange("p (j t) -> p j t", t=P), func=AF.Copy),
                             reads=BK(bt), writes=[("ktok", ci, cg)])

                def emit_U(step):
                    bu = 4 + step % 2
                    for ci in range(4):
                        c = chunk_of(ci, step)
                        vsl = slice((ci % 2) * P, (ci % 2 + 1) * P)
                        T.op(PE, lambda ci=ci, c=c, vsl=vsl: nc.tensor.matmul(ps[:, bu, ci * P:(ci + 1) * P], lhsT=ktok_all[:, ci, c, :], rhs=vtok[:, c, vsl],
                                                                              start=True, stop=True),
                             reads=[("ktok", ci, c // 4)], writes=BK(bu), inc=(ci == 3))

                def emit_rec(step):
                    bu = 4 + step % 2
                    if step > 0:
                        for ci in range(4):
                            c = chunk_of(ci, step)
                            T.op(DVE, lambda ci=ci, c=c: nc.vector.tensor_scalar(out=Stmp[:, ci, :], in0=St[:, ci, :], scalar1=dA[:, ci, c:c + 1], scalar2=None,
                                                                                 op0=ALU.mult), reads=[("St", ci)], writes=[("Stmp", ci)])
                    for ci in range(4):
                        c = chunk_of(ci, step)
                        usl = slice(ci * P, (ci + 1) * P)
                        if step == 0:
                            T.op(DVE, lambda ci=ci, c=c, usl=usl: nc.vector.tensor_scalar(out=St[:, ci, :], in0=ps[:, bu, usl], scalar1=dB[:, ci, c:c + 1],
                                                                                          scalar2=None, op0=ALU.mult), reads=BK(bu), writes=[("St", ci)])
                        else:
                            T.op(DVE, lambda ci=ci, c=c, usl=usl: nc.vector.scalar_tensor_tensor(out=St[:, ci, :], in0=ps[:, bu, usl], scalar=dB[:, ci, c:c + 1],
                                                                                                 in1=Stmp[:, ci, :], op0=ALU.mult, op1=ALU.add),
                                 reads=BK(bu) + [("Stmp", ci)], writes=[("St", ci)])

                def emit_Sb(step):
                    if step >= NT - 1:
                        return
                    for ci in range(4):
                        cn = chunk_of(ci, step + 1)
                        T.op(ACT, lambda ci=ci, cn=cn: nc.scalar.activation(out=Sb[:, step % 2, ci, :], in_=St[:, ci, :], func=AF.Identity,
                                                                            scale=dM[:, ci, cn:cn + 1]),
                             reads=[("St", ci)], writes=[("Sb", step % 2, ci)])

                def emit_O(step):
                    bo = 6 + step % 2
                    for ci in range(4):
                        c = chunk_of(ci, step)
                        csl = slice(c * P, (c + 1) * P)
                        vsl = slice((ci % 2) * P, (ci % 2 + 1) * P)
                        osl = slice(ci * P, (ci + 1) * P)
                        if step > 0:
                            T.op(PE, lambda ci=ci, csl=csl, osl=osl: nc.tensor.matmul(ps[:, bo, osl], lhsT=qtT[:, ci, csl], rhs=Sb[:, (step - 1) % 2, ci, :],
                                                                                     start=True, stop=False),
                                 reads=[("Sb", (step - 1) % 2, ci)], writes=BK(bo), inc=False)
                        T.op(PE, lambda ci=ci, c=c, vsl=vsl, osl=osl: nc.tensor.matmul(ps[:, bo, osl], lhsT=Am_all[:, ci, c, :], rhs=vtok[:, c, vsl],
                                                                                      start=(step == 0), stop=True),
                             reads=[("Am", ci, c // 4)], writes=BK(bo), inc=(ci == 3))

                def fin_A(step):
                    bo = 6 + step % 2
                    sl4 = step % 4
                    for pr in range(2):
                        c = chunk_of(2 * pr, step)
                        src = ps[:, bo, pr * 256:(pr + 1) * 256]
                        if step < NT // 2:
                            T.op(ACT, lambda c=c, src=src: nc.scalar.activation(out=opart[:, c, :], in_=src, func=AF.Copy), reads=BK(bo), writes=[("opart", c)])
                        else:
                            T.op(DVE, lambda c=c, src=src, pr=pr: nc.vector.tensor_tensor(out=osum[:, sl4, 2 * pr:2 * pr + 2, :].rearrange("p c v -> p (c v)"),
                                                                                          in0=src, in1=opart[:, c, :], op=ALU.add),
                                 reads=BK(bo) + [("opart", c)], writes=[("osum", sl4)])

                def fin_B(step):
                    if step < NT // 2:
                        return
                    sl4 = step % 4
                    for ci in range(4):
                        T.op(ACT, lambda ci=ci: nc.scalar.activation(out=scr3[:], in_=osum[:, sl4, ci, :], func=AF.Square, scale=inv128,
                                                                     accum_out=fh[:, sl4, 0, ci:ci + 1]), reads=[("osum", sl4)], writes=["scr3", ("fh", sl4)])
                    T.op(ACT, lambda: nc.scalar.activation(out=fh[:, sl4, 1, :], in_=fh[:, sl4, 0, :], func=AF.Ln, bias=cst[:, 0:1]),
                         reads=[("fh", sl4), "cst"], writes=[("fh", sl4)])
                    T.op(ACT, lambda: nc.scalar.activation(out=fh[:, sl4, 2, :], in_=fh[:, sl4, 1, :], func=AF.Exp, scale=-0.5),
                         reads=[("fh", sl4)], writes=[("fh", sl4)])

                def fin_C(step):
                    if step < NT // 2:
                        return
                    sl4 = step % 4
                    for ci in range(4):
                        c = chunk_of(ci, step)
                        vsl = slice((ci % 2) * P, (ci % 2 + 1) * P)
                        T.op(DVE, lambda ci=ci, c=c, vsl=vsl: nc.vector.scalar_tensor_tensor(out=onh[:, step % 2, ci, :], in0=osum[:, sl4, ci, :],
                                                                                             scalar=fh[:, sl4, 2, ci:ci + 1], in1=sg[:, c, vsl],
                                                                                             op0=ALU.mult, op1=ALU.mult),
                             reads=[("osum", sl4), ("fh", sl4), ("sg", c)], writes=[("onh", step % 2)])

                def fin_D(step):
                    if step < NT // 2:
                        return
                    bt = step % 2
                    for ci in range(4):
                        T.op(PE, lambda ci=ci: nc.tensor.transpose(out=psb[:, bt, ci * P:(ci + 1) * P], in_=onh[:, step % 2, ci, :], identity=ident_b[:]),
                             reads=[("onh", step % 2), "ident_b"], writes=BK(bt), inc=(ci == 3))

                def fin_E(step):
                    if step < NT // 2:
                        return
                    bt = step % 2
                    for pr in range(2):
                        c = chunk_of(2 * pr, step)
                        T.op(ACT, lambda pr=pr, c=c: nc.scalar.activation(out=catT[:, rnd * 2:rnd * 2 + 2, c * P:(c + 1) * P],
                                                                          in_=psb[:, bt, pr * 256:(pr + 1) * 256].rearrange("p (h t) -> p h t", t=P),
                                                                          func=AF.Identity, scale=cols[:, 5:6]),
                             reads=BK(bt) + [C_ALL], writes=[("catT", rnd, c)])

                emit_U(0)
                for step in range(NT + 5):
                    if step + 1 < NT:
                        emit_U(step + 1)
                    if step < NT:
                        emit_rec(step)
                        emit_Sb(step)
                        emit_O(step)
                    for lagk, fn in ((1, fin_A), (2, fin_B), (3, fin_C), (4, fin_D), (5, fin_E)):
                        if 0 <= step - lagk < NT:
                            fn(step - lagk)
                T.barrier(ALL)
        if stage == 6:
            dbg_dump([(catT[:, h, 0:1024], 1024) for h in range(4)])
            return nc

        yres = ybig[:].rearrange("p (i d) -> p i d", d=D)
        with ExitStack() as p6:
            wout = wout_v
            lnb = sb("lnb", [P, 2, D], F32, p6)
            for r6 in range(2):
                T.dma(SP, d_c2[3], lnb[:, r6, :], lnv_d[2 * r6:2 * r6 + 1, :].partition_broadcast(P), writes=["lnb"])
            T.op(DVE, lambda: nc.vector.tensor_scalar(out=lnb[:].rearrange("p r d -> p (r d)"), in0=lnb[:].rearrange("p r d -> p (r d)"),
                                                      scalar1=ALPHA, scalar2=None, op0=ALU.mult), reads=["lnb"], writes=["lnb"])

            NX6 = 10
            order6 = list(range(4, NT)) + [0, 1, 2, 3]
            til = lambda p_: order6[p_]
            xt6 = sb("xtb", [P, NX6, D], F32, p6)
            xk = lambda i: ("xt", i % NX6)
            xa = lambda i: xt6[:, i % NX6, :]
            bk6 = lambda i: (2 * (i % 2), 2 * (i % 2) + 1)

            def p6_s0(i):
                T.dma(SP, d_x[i % NX6], xa(i), x_d[til(i) * P:(til(i) + 1) * P, :], writes=[xk(i)])

            def p6_s2(i):
                ln_apply(xa(i), xk(i), i % NX6)
                b0 = 4 + 2 * (i % 2)
                for hc in range(2):
                    b = b0 + hc
                    T.op(PE, lambda hc=hc, b=b: nc.tensor.matmul(ps[:, b, :], lhsT=ones_b[:], rhs=brow[:, 0, hc * 512:(hc + 1) * 512], start=True, stop=False),
                         reads=["ones_b", "brow"], writes=BK(b), inc=False)
                    for kc in range(KC):
                        T.op(PE, lambda kc=kc, hc=hc, b=b: nc.tensor.matmul(ps[:, b, :], lhsT=catT[:, kc, til(i) * P:(til(i) + 1) * P],
                                                                            rhs=wout[:, kc, hc * 512:(hc + 1) * 512], start=False, stop=(kc == KC - 1)),
                             reads=[("wout", hc)], writes=BK(b), inc=(kc == KC - 1))

            def p6_s3(i):
                xn_ap, key = xa(i), xk(i)
                b0 = 4 + 2 * (i % 2)
                T.op(DVE, lambda: nc.vector.tensor_tensor(out=xn_ap, in0=xn_ap, in1=lnb[:, 0, :], op=ALU.mult), reads=[key, "lnb"], writes=[key])
                T.op(DVE, lambda: nc.vector.tensor_tensor(out=xn_ap, in0=xn_ap, in1=ps[:, b0:b0 + 2, :].rearrange("p b n -> p (b n)"), op=ALU.add),
                     reads=[key] + BK(b0) + BK(b0 + 1), writes=[key])

            def p6_s6(i):
                ti = til(i)
                transpose_evac(hT, "hT", ti, bk6(i), 2)
                extra = [("wout", 0), ("wout", 1)] if ti < 4 else []
                T.op(DVE, lambda: nc.vector.tensor_tensor(out=yres[:, ti, :], in0=xa(i), in1=lnb[:, 1, :], op=ALU.mult), reads=[xk(i), "lnb"],
                     writes=[("yres", ti)] + extra)

            def p6_mm(i):
                b0 = 4 + 2 * (i % 2)
                for hc in range(2):
                    b = b0 + hc
                    T.op(PE, lambda hc=hc, b=b: nc.tensor.matmul(ps[:, b, :], lhsT=ones_b[:], rhs=brow[:, 0, hc * 512:(hc + 1) * 512], start=True, stop=False),
                         reads=["ones_b", "brow"], writes=BK(b), inc=False)
                    for kc in range(KC):
                        T.op(PE, lambda kc=kc, hc=hc, b=b: nc.tensor.matmul(ps[:, b, :], lhsT=catT[:, kc, til(i) * P:(til(i) + 1) * P],
                                                                            rhs=wout[:, kc, hc * 512:(hc + 1) * 512], start=False, stop=(kc == KC - 1)),
                             reads=[("wout", hc)], writes=BK(b), inc=(kc == KC - 1))

            def p6_apply_pair(i):
                o1 = [lambda: T.op(ACT, lambda: nc.scalar.activation(out=xa(i), in_=xa(i), func=AF.Identity, scale=lnst[:, til(i), 0:1], bias=lnst[:, til(i), 1:2]),
                                   reads=[xk(i), ("lnst", til(i))], writes=[xk(i)])] if 0 <= i < NT else []
                i2 = i - 4
                o2 = ln_apply_ops(xa(i2), xk(i2), 12 + i2 % NX6) if 0 <= i2 < NT else []
                while o1 or o2:
                    if o1:
                        o1.pop(0)()
                    if o2:
                        o2.pop(0)()
                if 0 <= i < NT:
                    p6_mm(i)

            inr = lambda i: 0 <= i < NT
            for step in range(NT + 10):
                if inr(step):
                    p6_s0(step)
                p6_apply_pair(step - 3)
                if inr(step - 4):
                    p6_s3(step - 4)
                if inr(step - 5):
                    ln_stats_a(xa(step - 5), xk(step - 5), 12 + (step - 5) % NX6)
                if inr(step - 6):
                    ln_stats_b(12 + (step - 6) % NX6)
                if inr(step - 8):
                    transpose_pe(xa(step - 8), [xk(step - 8)], bk6(step - 8))
                if inr(step - 9):
                    p6_s6(step - 9)
            T.barrier(ALL)
        if stage == 7:
            dbg_dump([(yres[:, i, :], 1024) for i in range(8)])
            return nc
        T.barrier(ALL)
        pc.close()

        with ExitStack() as p7:
            NUP = 3
            GS = 3
            NRA = 2 * GS + 1
            NRW = 2 * GS + 3
            wup = sb("wup", [P, NUP, KC, 256], BF16, p7)
            wdn = sb("wdn", [P, NRW, D], BF16, p7)
            actT = sb("actT", [P, NRA, S], BF16, p7)
            ubuf = sb("ubuf", [P, 2, S + 2], F32, p7)
            cbuf = sb("cbuf", [P, 3, S], F32, p7)
            T.op(POOL, lambda: nc.gpsimd.memset(ubuf[:, :, 0:1], 0.0), writes=["ubuf_pad"], inc=False)
            T.op(POOL, lambda: nc.gpsimd.memset(ubuf[:, :, S + 1:S + 2], 0.0), writes=["ubuf_pad"])
            groups = []
            j0 = 0
            while j0 < NJ:
                j1 = min(NJ, j0 + GS)
                if NJ - j1 == 1:
                    j1 = NJ
                groups.append((j0, j1))
                j0 = j1
            gend = {g[1] - 1: g for g in groups}
            dw_up = d_w[0:3]
            dw_dn = d_w[3:13]
            assert NRW <= 10

            def load_j(j):
                T.dma(POOL, dw_up[j % NUP], wup[:, j % NUP, :, :], w_up_d[j, :, :, :], writes=[("wup", j % NUP)])
                T.dma(POOL, dw_dn[j % NRW], wdn[:, j % NRW, :], w_dn_d[:, j, :], writes=[("wdn", j % NRW)])

            pending = []
            hold = [0]

            def down_unit(i, hc, g0, g1, n):
                def emit():
                    b = 4 + n % 4
                    if g0 == 0:
                        T.op(PE, lambda: nc.tensor.matmul(ps[:, b, :], lhsT=ones_b[:], rhs=brow[:, 1, hc * 512:(hc + 1) * 512], start=True, stop=False),
                             reads=["ones_b", "brow"], writes=BK(b), inc=False)
                    for jj in range(g0, g1):
                        T.op(PE, lambda jj=jj: nc.tensor.matmul(ps[:, b, :], lhsT=actT[:, jj % NRA, i * P:(i + 1) * P],
                                                                rhs=wdn[:, jj % NRW, hc * 512:(hc + 1) * 512],
                                                                start=(jj == g0 and g0 != 0), stop=(jj == g1 - 1)),
                             reads=[("actT", jj % NRA), ("wdn", jj % NRW)], writes=BK(b), inc=(jj == g1 - 1))
                    T.op(DVE, lambda: nc.vector.tensor_tensor(out=yres[:, i, hc * 512:(hc + 1) * 512],
                                                              in0=yres[:, i, hc * 512:(hc + 1) * 512], in1=ps[:, b, :], op=ALU.add),
                         reads=[("yres", i)] + BK(b), writes=[("yres", i)])
                return emit

            load_j(0)
            load_j(1)
            nb = 0
            nd = 0
            for j in range(NJ):
                if j + 2 < NJ:
                    load_j(j + 2)
                us = j % NUP
                rsl = j % NRA
                ca = j % 2
                for ab in range(2):
                    cb_i = ca if ab == 0 else 2
                    for half in range(2):
                        b0 = (nb % 2) * 2
                        nb += 1
                        for tb in range(2):
                            t0 = half * 1024 + tb * 512
                            for kc in range(KC):
                                T.op(PE, lambda kc=kc, tb=tb, t0=t0: nc.tensor.matmul(ps[:, b0 + tb, :], lhsT=wup[:, us, kc, ab * P:(ab + 1) * P],
                                                                                     rhs=hT[:, kc, t0:t0 + 512], start=(kc == 0), stop=(kc == KC - 1)),
                                     reads=[("wup", us)] + hkeys("hT", t0, t0 + 512), writes=BK(b0 + tb), inc=(kc == KC - 1))
                        src2 = ps[:, b0:b0 + 2, :].rearrange("p b n -> p (b n)")
                        T.op(ACT, lambda: nc.scalar.activation(out=ubuf[:, ab, 1 + half * 1024:1 + (half + 1) * 1024], in_=src2, func=AF.Copy),
                             reads=BK(b0) + BK(b0 + 1), writes=[("ubuf", ab, half)])
                        T.op(ACT, lambda: nc.scalar.activation(out=cbuf[:, cb_i, half * 1024:(half + 1) * 1024], in_=src2, func=AF.Identity,
                                                               scale=convp[:, j, ab, 1:2], bias=convp[:, j, ab, 3:4]),
                             reads=BK(b0) + BK(b0 + 1) + [C_ALL], writes=[("cbuf", cb_i, half)])
                        if hold[0] > 0:
                            hold[0] -= 1
                        else:
                            for _ in range(4):
                                if pending:
                                    pending.pop(0)()
                    ck = [("cbuf", cb_i, 0), ("cbuf", cb_i, 1)]
                    uk = [("ubuf", ab, 0), ("ubuf", ab, 1), "ubuf_pad"]
                    T.op(DVE, lambda: nc.vector.scalar_tensor_tensor(out=cbuf[:, cb_i, :], in0=ubuf[:, ab, 0:S], scalar=convp[:, j, ab, 0:1],
                                                                     in1=cbuf[:, cb_i, :], op0=ALU.mult, op1=ALU.add),
                         reads=ck + uk + [C_ALL], writes=ck)
                    T.op(DVE, lambda: nc.vector.scalar_tensor_tensor(out=cbuf[:, cb_i, :], in0=ubuf[:, ab, 2:S + 2], scalar=convp[:, j, ab, 2:3],
                                                                     in1=cbuf[:, cb_i, :], op0=ALU.mult, op1=ALU.add),
                         reads=ck + uk + [C_ALL], writes=ck)
                ck0 = [("cbuf", ca, 0), ("cbuf", ca, 1)]
                T.op(ACT, lambda: nc.scalar.activation(out=cbuf[:, ca, :], in_=cbuf[:, ca, :], func=AF.Gelu_apprx_tanh), reads=ck0, writes=ck0)
                T.op(POOL, lambda: nc.gpsimd.tensor_tensor(out=actT[:, rsl, :], in0=cbuf[:, ca, :], in1=cbuf[:, 2, :], op=ALU.mult),
                     reads=ck0 + [("cbuf", 2, 0), ("cbuf", 2, 1)], writes=[("actT", rsl)])
                if j in gend:
                    g0, g1 = gend[j]
                    while pending:
                        pending.pop(0)()
                    for i in range(NT):
                        for hc in range(2):
                            pending.append(down_unit(i, hc, g0, g1, nd))
                            nd += 1
                    hold[0] = 2
            lnv2 = ubuf[:, 1, 0:2 * D].rearrange("p (r d) -> p r d", d=D)
            for r6 in range(2):
                T.dma(SP, d_c2[4], lnv2[:, r6, :], lnv_d[4 + r6:5 + r6, :].partition_broadcast(P), reads=[], writes=["lnx", ("ubuf", 1, 0), ("ubuf", 1, 1)])
            assert len(pending) == 2 * NT
            otb = cbuf[:].rearrange("p a s -> p (a s)")
            ota = lambda i: otb[:, (i % 6) * D:(i % 6 + 1) * D]
            otk = lambda i: ("ot", i % 6)

            def tail_units(i):
                pending.pop(0)()
                pending.pop(0)()

            pipeline([
                tail_units,
                lambda i: (ln_stats_act(yres[:, i, :], ("yres", i), i % 8, ubuf[:, 0, 0:D], "ujunk") if i % 2 == 0
                           else ln_stats(yres[:, i, :], ("yres", i), i % 8)),
                lambda i: ln_apply(yres[:, i, :], ("yres", i), i % 8),
                lambda i: T.op(DVE, lambda: nc.vector.tensor_tensor(out=ota(i), in0=yres[:, i, :], in1=lnv2[:, 0, :], op=ALU.mult),
                               reads=[("yres", i), "lnx"], writes=[otk(i)]),
                lambda i: T.op(POOL, lambda: nc.gpsimd.tensor_tensor(out=ota(i), in0=ota(i), in1=lnv2[:, 1, :], op=ALU.add),
                               reads=[otk(i), "lnx"], writes=[otk(i)]),
                lambda i: T.dma(SP, d_out[i % 6], y_d[i * P:(i + 1) * P, :], ota(i), reads=[otk(i)]),
            ], NT)
            T.barrier(ALL)
    return nc


def _prep_shared(inp):
    f = np.float32
    c = lambda a: np.ascontiguousarray(a, dtype=f)
    w_in = np.asarray(inp["w_in"])[0]
    w_in_r = c(w_in.reshape(KC, P, 3136).transpose(1, 0, 2))
    kr = w_in_r[:, :, 3072:3136]
    w_krsw = c(np.concatenate([kr[:, :, 32:64], kr[:, :, 0:32]], axis=2))
    w_qb = np.asarray(inp["w_q_b"])[0]
    w_qb_r = c(w_qb.reshape(3, P, 768).transpose(1, 0, 2))
    sw = []
    for h in range(4):
        r = w_qb_r[:, :, h * 192 + 128:h * 192 + 192]
        sw.append(np.concatenate([r[:, :, 32:64], r[:, :, 0:32]], axis=2))
    w_qbsw = c(np.concatenate(sw, axis=2))
    w_kvb = np.asarray(inp["w_kv_b"])[0].reshape(P, 4, 256)
    w_kvk = c(w_kvb[:, :, 0:128].reshape(P, 512))
    w_kvv = c(w_kvb[:, :, 128:256].reshape(P, 512))
    w_out_r = c(np.asarray(inp["w_out"])[0].reshape(KC, P, D).transpose(1, 0, 2))
    w_up = np.asarray(inp["w_up"])[0].reshape(KC, P, 2, NJ, P)
    w_up_r = c(w_up.transpose(3, 1, 0, 2, 4).reshape(NJ, P, KC, 256))
    w_dn_r = c(np.asarray(inp["w_down"])[0].reshape(NJ, P, D).transpose(1, 0, 2))
    cols = np.zeros((P, NCOL), f)
    cols[:, 0:3] = np.asarray(inp["q_a_norm_g"])[0].reshape(3, P).T
    cols[:, 3] = np.asarray(inp["kv_a_norm_g"])[0]
    cols[:, 4] = np.asarray(inp["attn_norm_g"])[0]
    cols[:, 5] = np.asarray(inp["hg_norm_g"])[0]
    half = 32
    inv_freq = (1.0 / (10000.0 ** (np.arange(half, dtype=np.float32) / np.float32(half)))).astype(f)
    cols[0:64, 6] = np.concatenate([inv_freq, inv_freq])
    cols[0:32, 7] = -1.0
    cols[32:64, 7] = 1.0
    lbr = np.zeros((P, 16), f)
    for d, nm in enumerate(("lb_fwd", "lb_bwd")):
        a = np.asarray(inp[nm]).reshape(2, 4, P)
        lbr[:, d * 8:(d + 1) * 8] = a.transpose(2, 0, 1).reshape(P, 8)
    lnv = c(np.stack([np.asarray(inp["ln_in_g"]), np.asarray(inp["ln_in_b"]), np.asarray(inp["ln1_g"])[0], np.asarray(inp["ln1_b"])[0],
                      np.asarray(inp["ln2_g"])[0], np.asarray(inp["ln2_b"])[0]]))
    lnc = c(lnv.reshape(6, KC, P).transpose(2, 0, 1).reshape(P, 48))
    gqkv = c(np.concatenate([np.asarray(inp["q_a_norm_g"])[0], np.asarray(inp["kv_a_norm_g"])[0]])[None])
    cw = np.asarray(inp["conv_w"])[0].reshape(3, 2, NJ, P)
    cb = np.asarray(inp["conv_b"])[0].reshape(1, 2, NJ, P)
    convp = c(np.concatenate([cw, cb], axis=0).transpose(3, 2, 1, 0).reshape(P, NJ * 8))
    return {"cols": cols, "lbr": lbr, "lnv": lnv, "lnc": lnc, "gqkv": gqkv, "convp": convp, "w_in_r": w_in_r, "w_krsw": w_krsw, "w_qb_r": w_qb_r,
            "w_qbsw": w_qbsw, "w_kvk": w_kvk, "w_kvv": w_kvv, "w_out_r": w_out_r, "w_up_r": w_up_r, "w_dn_r": w_dn_r}


def make_in_maps(inp, n=8):
    shared = _prep_shared(inp)
    x = np.asarray(inp["x"], dtype=np.float32)
    pos = np.asarray(inp["positions"], dtype=np.int32)
    maps = []
    for b in range(n):
        m = dict(shared)
        m["x"] = np.ascontiguousarray(x[b])
        m["pos"] = np.ascontiguousarray(pos[b][None])
        maps.append(m)
    return maps


def kernel(**inputs):
    nc = build_nc()
    in_maps = make_in_maps(inputs, 8)
    res = run_bass_kernel_spmd(nc, in_maps, core_ids=list(range(8)))
    return np.stack([np.asarray(r["y"], dtype=np.float32) for r in res.results], axis=0)
```

```python
import math
from contextlib import ExitStack

import numpy as np
import concourse.bass as bass
import concourse.mybir as mybir
from concourse.bass_utils import run_bass_kernel_spmd

F32 = mybir.dt.float32
BF16 = mybir.dt.bfloat16
I32 = mybir.dt.int32
AF = mybir.ActivationFunctionType
ALU = mybir.AluOpType

P = 128
S = 2048
NT = 16
D = 1024
KC = 8
DFF = 2816
NJ = 22
EPS = 1e-5
ALPHA = 2.0 ** 0.25
SCALE = 192.0 ** -0.5
TWO_PI = 2.0 * math.pi
NCOL = 16


class Src:
    def __init__(self, sem, name):
        self.sem = sem
        self.cnt = 0
        self.name = name


class Eng:
    def __init__(self, name, e, src):
        self.name = name
        self.e = e
        self.src = src
        self.waited = {}


class Trk:
    def __init__(self, nc, es):
        self.nc = nc
        self.es = es
        self.lastw = {}
        self.readers = {}
        self.srcs = []
        self.nsem = 0

    def new_src(self, name):
        sem = self.es.enter_context(self.nc.semaphore(name))
        s = Src(sem, name)
        self.srcs.append(s)
        return s

    def _wait(self, eng, src, c):
        if c <= 0:
            return
        if eng.waited.get(src, 0) >= c:
            return
        assert src.cnt >= c, (eng.name, src.name, src.cnt, c)
        eng.e.wait_ge(src.sem, c)
        eng.waited[src] = c

    def _deps(self, eng, reads, writes, own):
        deps = {}

        def add(s, c):
            if deps.get(s, 0) < c:
                deps[s] = c

        for k in reads:
            w = self.lastw.get(k)
            if w is not None:
                add(*w)
        for k in writes:
            w = self.lastw.get(k)
            if w is not None and w[0] is not own:
                add(*w)
            for s, c in self.readers.get(k, {}).items():
                if s is not own:
                    add(s, c)
        for s, c in deps.items():
            self._wait(eng, s, c)

    def _record(self, src, c, reads, writes):
        for k in reads:
            d = self.readers.setdefault(k, {})
            if d.get(src, 0) < c:
                d[src] = c
        for k in writes:
            self.lastw[k] = (src, c)
            self.readers[k] = {}

    @staticmethod
    def _excl(reads, writes):
        pr = [k for k in reads if isinstance(k, tuple) and k[0] == "ps"]
        if pr:
            reads = [k for k in reads if not (isinstance(k, tuple) and k[0] == "ps")]
            writes = list(writes) + pr
        return reads, writes

    def op(self, eng, fn, reads=(), writes=(), inc=True):
        reads, writes = self._excl(reads, writes)
        self._deps(eng, reads, writes, eng.src)
        ins = fn()
        if inc:
            eng.src.cnt += 1
            ins.then_inc(eng.src.sem, 1)
            c = eng.src.cnt
        else:
            c = eng.src.cnt + 1
        self._record(eng.src, c, reads, writes)
        return ins

    def dma(self, q, dsrc, out, in_, reads=(), writes=(), **kw):
        self._deps(q, reads, writes, None)
        ins = q.e.dma_start(out=out, in_=in_, **kw)
        dsrc.cnt += 16
        ins.then_inc(dsrc.sem, 16)
        self._record(dsrc, dsrc.cnt, reads, writes)
        return ins

    def barrier(self, engs):
        for e in engs:
            for s in self.srcs:
                self._wait(e, s, s.cnt)
        self.lastw = {}
        self.readers = {}


def build_nc(stage=99):
    nc = bass.Bass("TRN2", target_bir_lowering=False)

    def din(name, shape, dt=F32):
        return nc.dram_tensor(name, list(shape), dt, kind="ExternalInput").ap()

    x_d = din("x", [S, D])
    pos_d = din("pos", [1, S], I32)
    cols_d = din("cols", [P, NCOL])
    lb_d = din("lbr", [P, 16])
    lnv_d = din("lnv", [6, D])
    lnc_d = din("lnc", [P, 48])
    gqkv_d = din("gqkv", [1, 512])
    convp_d = din("convp", [P, NJ * 8])
    w_in_d = din("w_in_r", [P, KC, 3136])
    w_krsw_d = din("w_krsw", [P, KC, 64])
    w_qb_d = din("w_qb_r", [P, 3, 768])
    w_qbsw_d = din("w_qbsw", [P, 3, 256])
    w_kvk_d = din("w_kvk", [P, 512])
    w_kvv_d = din("w_kvv", [P, 512])
    w_out_d = din("w_out_r", [P, KC, D])
    w_up_d = din("w_up_r", [NJ, P, KC, 256])
    w_dn_d = din("w_dn_r", [P, NJ, D])
    y_d = nc.dram_tensor("y", [S, D], F32, kind="ExternalOutput").ap()
    if stage < 99:
        dbg_d = nc.dram_tensor("dbg", [P, 8192], F32, kind="ExternalOutput").ap()

    with ExitStack() as es:
        T = Trk(nc, es)
        PE = Eng("pe", nc.tensor, T.new_src("s_pe"))
        ACT = Eng("act", nc.scalar, T.new_src("s_act"))
        DVE = Eng("dve", nc.vector, T.new_src("s_dve"))
        POOL = Eng("pool", nc.gpsimd, T.new_src("s_pool"))
        SP = Eng("sp", nc.sync, T.new_src("s_sp"))
        ALL = [PE, ACT, DVE, POOL, SP]
        d_const = T.new_src("d_const")
        d_c2 = [T.new_src("d_c2_%d" % i) for i in range(6)]
        d_x = [T.new_src("d_x%d" % i) for i in range(10)]
        d_w = [T.new_src("d_w%d" % i) for i in range(14)]
        d_out = [T.new_src("d_o%d" % i) for i in range(6)]

        sbn = [0]

        def sb(name, shape, dt, stk=es):
            sbn[0] += 1
            return stk.enter_context(nc.sbuf_tensor("sb%d_%s" % (sbn[0], name), list(shape), dt))

        ps = es.enter_context(nc.psum_tensor("ps", [P, 8, 512], F32))
        psb = ps.bitcast(BF16)

        def BK(b):
            return [("ps", b)]

        ident_f = sb("ident_f", [P, P], F32)
        ident_b = sb("ident_b", [P, P], BF16)
        maskF = sb("maskF", [P, P], F32)
        maskB = sb("maskB", [P, P], F32)
        m0 = sb("m0", [P, 512], F32)
        cst = sb("cst", [P, 4], F32)
        cols = sb("cols", [P, NCOL], F32)
        lbr = sb("lbr", [P, 16], F32)
        ones_b = sb("ones_b", [P, P], BF16)
        lnst = sb("lnst", [P, NT, 2], F32)
        brow = sb("brow", [P, 2, D], BF16)
        lbT = sb("lbT", [P, 8], F32)
        omlb = sb("omlb", [P, 8], F32)
        lnc = sb("lnc", [P, 6, KC], F32)
        convp = sb("convp", [P, NJ, 2, 4], F32)
        stt = sb("stt", [P, 24, 12], F32)
        mv = sb("mv", [P, 24, 2], F32)
        sd = sb("sd", [P, 24], F32)
        rs = sb("rs", [P, 24], F32)
        nmr = sb("nmr", [P, 24], F32)
        hT = sb("hT", [P, KC, S], BF16)
        ybig = sb("ybig", [P, NT * D], F32)
        ybig_b = ybig.bitcast(BF16)

        def carve_f(off, n, parts=P):
            return ybig[0:parts, off:off + n]

        def carve_b(off, n, parts=P):
            return ybig_b[0:parts, 2 * off:2 * off + n]

        wout_v = carve_b(0, KC * D).rearrange("p (k d) -> p k d", d=D)
        pc = es.enter_context(ExitStack())
        catT = sb("catT", [P, KC, S], BF16, pc)

        C_ALL = ("const",)
        T.dma(SP, d_const, cols[:], cols_d[:, :], writes=[C_ALL])
        T.dma(SP, d_const, lbr[:], lb_d[:, :], writes=[C_ALL])
        T.dma(SP, d_const, convp[:].rearrange("p j a c -> p (j a c)"), convp_d[:, :], writes=[C_ALL])
        T.dma(SP, d_const, lnc[:].rearrange("p r k -> p (r k)"), lnc_d[:, :], writes=[C_ALL])

        T.op(POOL, lambda: nc.gpsimd.memset(ident_f[:], 1.0), writes=["ident_f"])
        T.op(POOL, lambda: nc.gpsimd.affine_select(out=ident_f[:], in_=ident_f[:], pattern=[[-1, P]],
                                                     compare_op=ALU.is_equal, fill=0.0, base=0, channel_multiplier=1),
             reads=["ident_f"], writes=["ident_f"])
        T.op(POOL, lambda: nc.gpsimd.memset(cst[:, 0:1], EPS), writes=["cst"], inc=False)
        T.op(POOL, lambda: nc.gpsimd.memset(cst[:, 1:2], EPS / (ALPHA * ALPHA)), writes=["cst"], inc=False)
        T.op(POOL, lambda: nc.gpsimd.memset(cst[:, 3:4], 1.0), writes=["cst"], inc=False)
        T.op(POOL, lambda: nc.gpsimd.memset(cst[:, 2:3], 0.0), writes=["cst"])
        T.op(DVE, lambda: nc.vector.tensor_copy(out=ident_b[:], in_=ident_f[:]), reads=["ident_f"], writes=["ident_b"])

        def emit_late_setup():
            T.op(POOL, lambda: nc.gpsimd.memset(maskF[:], 1.0), writes=["maskF"])
            T.op(POOL, lambda: nc.gpsimd.affine_select(out=maskF[:], in_=maskF[:], pattern=[[1, P]],
                                                         compare_op=ALU.is_ge, fill=0.0, base=0, channel_multiplier=-1),
                 reads=["maskF"], writes=["maskF"])
            T.op(POOL, lambda: nc.gpsimd.memset(maskB[:], 1.0), writes=["maskB"])
            T.op(POOL, lambda: nc.gpsimd.affine_select(out=maskB[:], in_=maskB[:], pattern=[[-1, P]],
                                                         compare_op=ALU.is_ge, fill=0.0, base=0, channel_multiplier=1),
                 reads=["maskB"], writes=["maskB"])
            T.op(POOL, lambda: nc.gpsimd.memset(m0[:], 1.0), writes=["m0"])
            T.op(POOL, lambda: nc.gpsimd.memset(m0[:].rearrange("p (c t) -> p c t", t=P)[:, :, 0:1], 0.0), reads=["m0"], writes=["m0"])
            T.op(POOL, lambda: nc.gpsimd.memset(ones_b[:], 1.0), writes=["ones_b"])
            T.op(POOL, lambda: nc.gpsimd.memset(brow[:], 0.0), writes=["brow"])
            lb3 = lbr[:].rearrange("p (d r h) -> p d r h", d=2, r=2)
            T.op(DVE, lambda: nc.vector.tensor_tensor(out=omlb[:].rearrange("p (d h) -> p d h", d=2), in0=lb3[:, :, 0, :],
                                                      in1=lb3[:, :, 1, :], op=ALU.subtract), reads=[C_ALL], writes=["omlb"])
            T.op(ACT, lambda: nc.scalar.activation(out=lbT[:], in_=omlb[:], func=AF.Sigmoid), reads=["omlb"], writes=["lbT"])
            T.op(DVE, lambda: nc.vector.tensor_scalar(out=omlb[:], in0=lbT[:], scalar1=-1.0, scalar2=1.0, op0=ALU.mult, op1=ALU.add),
                 reads=["lbT"], writes=["omlb"])

        def rstd_from(var_ap, out_ap, rkeys, wkey, tmp_ap, tkey, eps_col=0):
            T.op(ACT, lambda: nc.scalar.activation(out=tmp_ap, in_=var_ap, func=AF.Ln, bias=cst[:, eps_col:eps_col + 1]),
                 reads=list(rkeys) + ["cst"], writes=[tkey])
            T.op(ACT, lambda: nc.scalar.activation(out=out_ap, in_=tmp_ap, func=AF.Exp, scale=-0.5), reads=[tkey], writes=[wkey])

        def ln_stats_a(src, key, slot):
            kst = ("st", slot)
            T.op(DVE, lambda: nc.vector.bn_stats(out=stt[:, slot, 0:6], in_=src[:, 0:512]), reads=[key], writes=[kst], inc=False)
            T.op(DVE, lambda: nc.vector.bn_stats(out=stt[:, slot, 6:12], in_=src[:, 512:1024]), reads=[key], writes=[kst])

        def ln_stats_b(slot):
            kst, kmv = ("st", slot), ("mv", slot)
            T.op(DVE, lambda: nc.vector.bn_aggr(out=mv[:, slot, :], in_=stt[:, slot, :]), reads=[kst], writes=[kmv])
            T.op(DVE, lambda: nc.vector.tensor_scalar(out=nmr[:, slot:slot + 1], in0=mv[:, slot, 0:1], scalar1=-1.0, scalar2=None, op0=ALU.mult),
                 reads=[kmv], writes=[("nm0", slot)])

        def ln_stats(src, key, slot):
            ln_stats_a(src, key, slot)
            ln_stats_b(slot)

        def ln_stats_act(src, key, slot, junk, jkey):
            kst, kmv = ("st", slot), ("mv", slot)
            T.op(ACT, lambda: nc.scalar.activation(out=junk, in_=src, func=AF.Identity, scale=1.0 / D, accum_out=stt[:, slot, 0:1]),
                 reads=[key], writes=[jkey, kst])
            T.op(ACT, lambda: nc.scalar.activation(out=junk, in_=src, func=AF.Square, scale=1.0 / math.sqrt(D), accum_out=stt[:, slot, 1:2]),
                 reads=[key], writes=[jkey, kst])
            T.op(DVE, lambda: nc.vector.tensor_copy(out=mv[:, slot, 0:1], in_=stt[:, slot, 0:1]), reads=[kst], writes=[kmv], inc=False)
            T.op(DVE, lambda: nc.vector.tensor_scalar(out=nmr[:, slot:slot + 1], in0=stt[:, slot, 0:1], scalar1=-1.0, scalar2=None, op0=ALU.mult),
                 reads=[kst], writes=[("nm0", slot)])
            T.op(DVE, lambda: nc.vector.scalar_tensor_tensor(out=mv[:, slot, 1:2], in0=stt[:, slot, 0:1], scalar=nmr[:, slot:slot + 1],
                                                             in1=stt[:, slot, 1:2], op0=ALU.mult, op1=ALU.add),
                 reads=[kst, ("nm0", slot)], writes=[kmv])

        def ln_apply_ops(src, key, slot, eps_col=0, keep=None):
            kmv, ksd, krs, knm = ("mv", slot), ("sd", slot), ("rs", slot), ("nmr", slot)
            rs_ap, nb_ap = rs[:, slot:slot + 1], sd[:, slot:slot + 1]
            tmp_ap, ktmp = sd[:, slot:slot + 1], ksd
            if keep is not None:
                rs_ap, nb_ap = keep[0][:, 0:1], keep[0][:, 1:2]
                krs = knm = keep[1]
            return [
                lambda: T.op(ACT, lambda: nc.scalar.activation(out=tmp_ap, in_=mv[:, slot, 1:2], func=AF.Ln, bias=cst[:, eps_col:eps_col + 1]),
                             reads=[kmv, "cst"], writes=[ktmp]),
                lambda: T.op(ACT, lambda: nc.scalar.activation(out=rs_ap, in_=tmp_ap, func=AF.Exp, scale=-0.5),
                             reads=[ktmp], writes=[krs]),
                lambda: T.op(ACT, lambda: nc.scalar.activation(out=nb_ap, in_=nmr[:, slot:slot + 1], func=AF.Identity, scale=rs_ap),
                             reads=[("nm0", slot), krs, ktmp], writes=[knm]),
                lambda: T.op(ACT, lambda: nc.scalar.activation(out=src, in_=src, func=AF.Identity, scale=rs_ap, bias=nb_ap),
                             reads=[key, krs, knm], writes=[key]),
            ]

        def ln_apply(src, key, slot, keep=None):
            for f in ln_apply_ops(src, key, slot, keep=keep):
                f()

        def ln_norm(src, key, slot):
            ln_stats(src, key, slot)
            ln_apply(src, key, slot)

        def transpose_pe(src, src_keys, banks):
            for hb in range(2):
                b = banks[hb]
                for q in range(4):
                    kc = hb * 4 + q
                    T.op(PE, lambda kc=kc, q=q, b=b: nc.tensor.transpose(out=ps[:, b, q * P:(q + 1) * P], in_=src[:, kc * P:(kc + 1) * P],
                                                                         identity=ident_f[:]),
                         reads=list(src_keys) + ["ident_f"], writes=BK(b), inc=(q == 3))

        def transpose_evac(dstT, name, i, banks, gi, all_act=False, act_extra=0):
            for hb in range(2):
                b = banks[hb]
                for q in range(4):
                    kc = hb * 4 + q
                    if hb == 0 or all_act or q < act_extra:
                        T.op(ACT, lambda kc=kc, q=q, b=b: nc.scalar.activation(out=dstT[:, kc, i * P:(i + 1) * P], in_=ps[:, b, q * P:(q + 1) * P],
                                                                               func=AF.Identity, scale=lnc[:, gi, kc:kc + 1], bias=lnc[:, gi + 1, kc:kc + 1]),
                             reads=BK(b) + [C_ALL], writes=[(name, i)])
                    else:
                        T.op(DVE, lambda kc=kc, q=q, b=b: nc.vector.tensor_scalar(out=dstT[:, kc, i * P:(i + 1) * P], in0=ps[:, b, q * P:(q + 1) * P],
                                                                                  scalar1=lnc[:, gi, kc:kc + 1], scalar2=lnc[:, gi + 1, kc:kc + 1],
                                                                                  op0=ALU.mult, op1=ALU.add),
                             reads=BK(b) + [C_ALL], writes=[(name, i)])

        def hkeys(name, t0, t1):
            return [(name, i) for i in range(t0 // P, (t1 + P - 1) // P)]

        def dbg_dump(ap_list):
            T.barrier(ALL)
            off = 0
            with nc.sbuf_tensor("dbgt", [P, 1024], F32) as dbgt:
                k = 0
                for ap, n in ap_list:
                    for c0 in range(0, n, 1024):
                        cl = min(1024, n - c0)
                        T.op(DVE, lambda: nc.vector.memset(dbgt[:], 0.0), writes=["dbgt"])
                        T.op(DVE, lambda: nc.vector.tensor_copy(out=dbgt[0:ap.shape[0], 0:cl], in_=ap[:, c0:c0 + cl]), writes=["dbgt"])
                        T.dma(SP, d_out[k % 2], dbg_d[:, off:off + cl], dbgt[:, 0:cl], reads=["dbgt"])
                        k += 1
                        off += cl
                T.dma(SP, d_out[k % 2], y_d[0:P, :], dbgt[:, 0:D], reads=["dbgt"])
                T.barrier(ALL)

        NXS = 8

        def pipeline(stages, n, lag=1):
            ns = len(stages)
            for step in range(n + lag * (ns - 1)):
                for k, st in enumerate(stages):
                    i = step - lag * k
                    if 0 <= i < n:
                        st(i)

        pw = es.enter_context(ExitStack())
        wmla = sb("wmla", [P, KC, 576], BF16, pw)
        wkrsw = sb("wkrsw", [P, KC, 64], BF16, pw)
        wqb = sb("wqb", [P, 3, 768], BF16, pw)
        wqbsw = sb("wqbsw", [P, 3, 256], BF16, pw)
        wkvk = sb("wkvk", [P, 512], BF16, pw)
        wkvv = sb("wkvv", [P, 512], BF16, pw)
        gqkvB = sb("gqkvB", [P, 512], F32, pw)
        T.dma(SP, d_c2[0], gqkvB[:], gqkv_d.partition_broadcast(P), writes=["gqkvB"])
        cosT = carve_f(8256, S, 64)
        ssinT = carve_f(10304, S, 64)

        with ExitStack() as p1:
            xt = sb("xta", [P, NXS, D], F32, p1)
            NPRE = 4
            for i in range(NPRE):
                T.dma(SP, d_x[i % NXS], xt[:, i % NXS, :], x_d[i * P:(i + 1) * P, :], writes=[("xt", i % NXS)])
            T.dma(POOL, d_w[0], wmla[:], w_in_d[:, :, 2560:3136], writes=["wmla"])
            T.dma(POOL, d_w[1], wkrsw[:], w_krsw_d[:, :, :], writes=["wkrsw"])
            for c3 in range(3):
                T.dma(POOL, d_w[2], wqb[:, c3, :], w_qb_d[:, c3, :], writes=["wqb"])
            T.dma(POOL, d_w[3], wqbsw[:], w_qbsw_d[:, :, :], writes=["wqbsw"])
            T.dma(POOL, d_w[4], wkvk[:], w_kvk_d[:, :], writes=["wkvk"])
            T.dma(POOL, d_w[5], wkvv[:], w_kvv_d[:, :], writes=["wkvv"])
            emit_late_setup()
            HS = S // 2
            posi = sb("posi", [64, HS], I32, p1)
            ang = sb("ang", [64, HS], F32, p1)
            kf = sb("kf", [64, HS], F32, p1)
            rope_ops = []
            for hv in range(2):
                hsl = slice(hv * HS, (hv + 1) * HS)
                rope_ops.append(lambda hsl=hsl, hv=hv: T.dma(SP, d_c2[1 + hv], posi[:], pos_d[:, hsl].partition_broadcast(64), writes=["posi"]))
                rope_ops.append(lambda: T.op(DVE, lambda: nc.vector.tensor_copy(out=ang[:], in_=posi[:]), reads=["posi"], writes=["ang"]))
                rope_ops.append(lambda: T.op(DVE, lambda: nc.vector.tensor_scalar(out=ang[:], in0=ang[:], scalar1=cols[0:64, 6:7], scalar2=None,
                                                                                  op0=ALU.mult), reads=["ang", C_ALL], writes=["ang"]))
                for which, dst in ((0, ssinT), (1, cosT)):
                    shift = 0.0 if which == 0 else math.pi / 2
                    a2 = dst[:, hsl]
                    ak = ("rope%d" % which, hv)
                    rope_ops.append(lambda shift=shift: T.op(DVE, lambda: nc.vector.tensor_scalar(out=kf[:], in0=ang[:], scalar1=shift, scalar2=1.0 / TWO_PI,
                                                                                                  op0=ALU.add, op1=ALU.mult), reads=["ang"], writes=["kf"]))
                    rope_ops.append(lambda: T.op(DVE, lambda: nc.vector.tensor_copy(out=posi[:], in_=kf[:]), reads=["kf"], writes=["posi"]))
                    rope_ops.append(lambda: T.op(DVE, lambda: nc.vector.tensor_copy(out=kf[:], in_=posi[:]), reads=["posi"], writes=["kf"]))
                    rope_ops.append(lambda a2=a2, ak=ak: T.op(DVE, lambda: nc.vector.scalar_tensor_tensor(out=a2, in0=kf[:], scalar=-TWO_PI, in1=ang[:],
                                                                                                         op0=ALU.mult, op1=ALU.add),
                                                              reads=["kf", "ang"], writes=[ak]))
                    rope_ops.append(lambda a2=a2, ak=ak, shift=shift: T.op(DVE, lambda: nc.vector.tensor_scalar(out=a2, in0=a2, scalar1=shift, scalar2=-math.pi,
                                                                                                               op0=ALU.add, op1=ALU.max),
                                                                           reads=[ak], writes=[ak]))
                    rope_ops.append(lambda a2=a2, ak=ak: T.op(DVE, lambda: nc.vector.tensor_scalar(out=a2, in0=a2, scalar1=math.pi, scalar2=None, op0=ALU.min),
                                                              reads=[ak], writes=[ak]))
                    rope_ops.append(lambda a2=a2, ak=ak: T.op(ACT, lambda: nc.scalar.activation(out=a2, in_=a2, func=AF.Sin), reads=[ak], writes=[ak]))
                rope_ops.append(lambda hsl=hsl, hv=hv: T.op(DVE, lambda: nc.vector.tensor_scalar(out=ssinT[:, hsl], in0=ssinT[:, hsl], scalar1=cols[0:64, 7:8],
                                                                                                 scalar2=None, op0=ALU.mult),
                                                            reads=[("rope0", hv), C_ALL], writes=[("rope0", hv)]))

            def rope_trickle(i):
                for _ in range(3):
                    if rope_ops:
                        rope_ops.pop(0)()

            def p1_group_pe(i):
                if i % 4 != 3:
                    return
                i0 = i - 3
                for kc in range(KC):
                    for j in range(4):
                        sx = (i0 + j) % NXS
                        T.op(PE, lambda kc=kc, j=j, sx=sx: nc.tensor.transpose(out=ps[:, kc, j * P:(j + 1) * P], in_=xt[:, sx, kc * P:(kc + 1) * P],
                                                                               identity=ident_f[:]),
                             reads=[("xt", sx), "ident_f"], writes=BK(kc), inc=(j == 3))

            def p1_group_evac(i):
                if i % 4 != 3:
                    return
                i0 = i - 3
                for kc in range(KC):
                    dst = hT[:, kc, i0 * P:(i0 + 4) * P]
                    wk = [("hT", i0 + j) for j in range(4)]
                    if kc % 2 == 0:
                        T.op(ACT, lambda kc=kc, dst=dst: nc.scalar.activation(out=dst, in_=ps[:, kc, :], func=AF.Identity, scale=lnc[:, 0, kc:kc + 1],
                                                                              bias=lnc[:, 1, kc:kc + 1]), reads=BK(kc) + [C_ALL], writes=wk)
                    else:
                        T.op(DVE, lambda kc=kc, dst=dst: nc.vector.tensor_scalar(out=dst, in0=ps[:, kc, :], scalar1=lnc[:, 0, kc:kc + 1],
                                                                                 scalar2=lnc[:, 1, kc:kc + 1], op0=ALU.mult, op1=ALU.add),
                             reads=BK(kc) + [C_ALL], writes=wk)

            pipeline([
                lambda i: (T.dma(SP, d_x[i % NXS], xt[:, i % NXS, :], x_d[i * P:(i + 1) * P, :], writes=[("xt", i % NXS)]) if i >= NPRE else None),
                lambda i: ln_stats_a(xt[:, i % NXS, :], ("xt", i % NXS), i % NXS),
                lambda i: ln_stats_b(i % NXS),
                lambda i: ln_apply(xt[:, i % NXS, :], ("xt", i % NXS), i % NXS, keep=(lnst[:, i, :], ("lnst", i))),
                p1_group_pe,
                p1_group_evac,
                rope_trickle,
            ], NT)
            while rope_ops:
                rope_ops.pop(0)()
            bt32 = carve_f(0, 2 * D, 64).rearrange("p (r d) -> p r d", d=D)
            bh16 = carve_b(2048, 2 * D, 64).rearrange("p (r d) -> p r d", d=D)
            fl = lambda t: t.rearrange("p r d -> p (r d)")
            T.op(POOL, lambda: nc.gpsimd.memset(fl(bt32), 0.0), writes=["bt32"])
            for r_, (lrow, prt) in enumerate(((1, 0), (1, 32), (3, 0), (3, 32))):
                T.dma(SP, d_c2[5], bt32[prt:prt + 1, r_ // 2, :], lnv_d[lrow:lrow + 1, :], reads=[], writes=["bt32"])
            T.op(DVE, lambda: nc.vector.tensor_scalar(out=fl(bt32), in0=fl(bt32), scalar1=ALPHA, scalar2=None, op0=ALU.mult), reads=["bt32"], writes=["bt32"])
            T.op(DVE, lambda: nc.vector.tensor_copy(out=fl(bh16), in_=fl(bt32)), reads=["bt32"], writes=["bh16"])
            T.op(DVE, lambda: nc.vector.tensor_tensor(out=fl(bt32), in0=fl(bt32), in1=fl(bh16), op=ALU.subtract), reads=["bt32", "bh16"], writes=["bt32"])
            T.op(DVE, lambda: nc.vector.tensor_copy(out=brow[0:32, :, :], in_=bh16[0:32, :, :]), reads=["bh16", "brow"], writes=["brow"])
            T.op(DVE, lambda: nc.vector.tensor_copy(out=brow[32:64, :, :], in_=bt32[32:64, :, :]), reads=["bt32", "brow"], writes=["brow"])

            T.barrier(ALL)
        if stage == 1:
            dbg_dump([(hT[:, kc, 0:1024], 1024) for kc in range(8)])
            return nc

        with ExitStack() as p2:
            qkT = carve_b(0, 4 * S).rearrange("p (c s) -> p c s", s=S)
            vext = carve_b(4096, NT * 4 * 130).rearrange("p (i h d) -> p i h d", h=4, d=130)
            kpeT = carve_b(12352, S)
            qTn = carve_b(13376, S)
            qTr = carve_b(14400, S)
            kTn2 = sb("kTn2", [P, 2, S], BF16, p2)
            qTn2 = sb("qTn2", [P, S], BF16, p2)
            qTr2 = sb("qTr2", [P, S], BF16, p2)
            PT = sb("PT", [P, 3, 512], BF16, p2)
            scr = sb("scr", [P, 512], BF16, p2)
            qn = sb("qn", [P, 4, 512], BF16, p2)
            ssq = sb("ssq", [P, 4, 2], F32, p2)
            sd2 = sb("sd2", [P, 4, 2], F32, p2)
            rs2 = sb("rs2", [P, 4, 2], F32, p2)
            rt1 = sb("rt1", [64, 512], F32, p2)
            rt2 = sb("rt2", [64, 512], F32, p2)
            fin = sb("fin", [P, 2, 8, 4], F32, p2)
            onb = sb("onb", [P, 4, P], BF16, p2)

            T.op(POOL, lambda: nc.gpsimd.memset(vext[:], 1.0), writes=["vext"])
            T.op(POOL, lambda: nc.gpsimd.memset(kpeT[64:128, :], 0.0), writes=["pe_pad"], inc=False)
            T.op(POOL, lambda: nc.gpsimd.memset(qTr2[64:128, :], 0.0), writes=["pe_pad"], inc=False)
            T.op(POOL, lambda: nc.gpsimd.memset(qTr[64:128, :], 0.0), writes=["pe_pad"])

            def rope_combine(bA, bB, dst, dkey, blk):
                sl = slice(blk * 512, (blk + 1) * 512)
                T.op(DVE, lambda: nc.vector.tensor_tensor(out=rt1[:], in0=ps[0:64, bA, :], in1=cosT[:, sl], op=ALU.mult),
                     reads=BK(bA), writes=["rt1"])
                T.op(DVE, lambda: nc.vector.tensor_tensor(out=rt2[:], in0=ps[0:64, bB, :], in1=ssinT[:, sl], op=ALU.mult),
                     reads=BK(bB), writes=["rt2"])
                T.op(POOL, lambda: nc.gpsimd.tensor_tensor(out=dst[0:64, sl], in0=rt1[:], in1=rt2[:], op=ALU.add),
                     reads=["rt1", "rt2"], writes=[dkey])

            kr_steps = []
            for blk in range(4):
                def kr_a(blk=blk):
                    sl = slice(blk * 512, (blk + 1) * 512)
                    for kc in range(KC):
                        T.op(PE, lambda kc=kc: nc.tensor.matmul(ps[0:64, 5, :], lhsT=wmla[:, kc, 512:576], rhs=hT[:, kc, sl],
                                                                start=(kc == 0), stop=(kc == KC - 1)),
                             reads=["wmla"] + hkeys("hT", blk * 512, blk * 512 + 512), writes=BK(5), inc=(kc == KC - 1))

                def kr_b(blk=blk):
                    sl = slice(blk * 512, (blk + 1) * 512)
                    for kc in range(KC):
                        T.op(PE, lambda kc=kc: nc.tensor.matmul(ps[0:64, 6, :], lhsT=wkrsw[:, kc, :], rhs=hT[:, kc, sl],
                                                                start=(kc == 0), stop=(kc == KC - 1)),
                             reads=["wkrsw"] + hkeys("hT", blk * 512, blk * 512 + 512), writes=BK(6), inc=(kc == KC - 1))

                def kr_c(blk=blk):
                    rope_combine(5, 6, kpeT, ("kpeT", blk), blk)
                kr_steps += [kr_a, kr_b, kr_c]

            def kr_trickle(i):
                if kr_steps:
                    kr_steps.pop(0)()

            inv384 = 1.0 / math.sqrt(384.0)
            inv128 = 1.0 / math.sqrt(128.0)
            def qa_s0(i):
                b = i % 3
                for kc in range(KC):
                    T.op(PE, lambda kc=kc: nc.tensor.matmul(ps[:, b, :], lhsT=hT[:, kc, i * P:(i + 1) * P], rhs=wmla[:, kc, 0:512],
                                                            start=(kc == 0), stop=(kc == KC - 1)),
                         reads=["wmla", ("hT", i)], writes=BK(b), inc=(kc == KC - 1))

            def qa_s1(i):
                b = i % 3
                s2 = i % 4
                T.op(ACT, lambda: nc.scalar.activation(out=scr[:, 0:384], in_=ps[:, b, 0:384], func=AF.Square, scale=inv384,
                                                       accum_out=ssq[:, s2, 0:1]), reads=BK(b), writes=["scr", ("ssq", s2)])
                T.op(ACT, lambda: nc.scalar.activation(out=scr[:, 384:512], in_=ps[:, b, 384:512], func=AF.Square, scale=inv128,
                                                       accum_out=ssq[:, s2, 1:2]), reads=BK(b), writes=["scr", ("ssq", s2)])
                rstd_from(ssq[:, s2, :], rs2[:, s2, :], [("ssq", s2)], ("rs2", s2), sd2[:, s2, :], ("sd2", s2))

            def qa_s2(i):
                b = i % 3
                s2 = i % 4
                T.op(DVE, lambda: nc.vector.scalar_tensor_tensor(out=qn[:, s2, 0:384], in0=ps[:, b, 0:384], scalar=rs2[:, s2, 0:1],
                                                                 in1=gqkvB[:, 0:384], op0=ALU.mult, op1=ALU.mult),
                     reads=BK(b) + [("rs2", s2), "gqkvB"], writes=[("qn", s2)], inc=False)
                T.op(DVE, lambda: nc.vector.scalar_tensor_tensor(out=qn[:, s2, 384:512], in0=ps[:, b, 384:512], scalar=rs2[:, s2, 1:2],
                                                                 in1=gqkvB[:, 384:512], op0=ALU.mult, op1=ALU.mult),
                     reads=BK(b) + [("rs2", s2), "gqkvB"], writes=[("qn", s2)])

            def qa_s3(i):
                bt = 3 + (i % 2)
                s2 = i % 4
                for c in range(4):
                    T.op(PE, lambda c=c: nc.tensor.transpose(out=psb[:, bt, c * P:(c + 1) * P], in_=qn[:, s2, c * P:(c + 1) * P],
                                                             identity=ident_b[:]),
                         reads=[("qn", s2), "ident_b"], writes=BK(bt), inc=(c == 3))

            def qa_s4(i):
                bt = 3 + (i % 2)
                T.op(DVE, lambda: nc.vector.tensor_copy(out=qkT[:, :, i * P:(i + 1) * P],
                                                        in_=psb[:, bt, 0:512].rearrange("p (c t) -> p c t", t=P)),
                     reads=BK(bt), writes=[("qkT", i)])

            pipeline([kr_trickle, qa_s0, qa_s1, qa_s2, qa_s3, qa_s4], NT)
            while kr_steps:
                kr_steps.pop(0)()

            if stage == 2:
                dbg_dump([(qkT[:, c, 0:1024], 1024) for c in range(4)] + [(kpeT[0:64, 0:1024], 1024), (cosT[:, 0:1024], 1024), (ssinT[:, 0:1024], 1024)])
                return nc

            for i in range(NT):
                b = i % 2
                T.op(PE, lambda: nc.tensor.matmul(ps[:, b, :], lhsT=qkT[:, 3, i * P:(i + 1) * P], rhs=wkvv[:, :], start=True, stop=True),
                     reads=["wkvv", ("qkT", i)], writes=BK(b))
                T.op(DVE, lambda: nc.vector.tensor_copy(out=vext[:, i, :, 0:128], in_=ps[:, b, :].rearrange("p (h d) -> p h d", d=P)),
                     reads=BK(b), writes=["vext"])

            qTn_s = [qTn, qTn2[:, :]]
            qTr_s = [qTr, qTr2[:, :]]
            kTn_s = [kTn2[:, 0, :], kTn2[:, 1, :]]

            def proj_steps(h, blk, banks):
                st = h % 2
                bq, br, bs_, bk = banks
                sl = slice(blk * 512, (blk + 1) * 512)
                qk_keys = hkeys("qkT", blk * 512, blk * 512 + 512)

                def s_q():
                    for c in range(3):
                        T.op(PE, lambda c=c: nc.tensor.matmul(ps[:, bq, :], lhsT=wqb[:, c, h * 192:h * 192 + 128], rhs=qkT[:, c, sl],
                                                              start=(c == 0), stop=(c == 2)),
                             reads=["wqb"] + qk_keys, writes=BK(bq), inc=(c == 2))
                    T.op(DVE, lambda: nc.vector.tensor_copy(out=qTn_s[st][:, sl], in_=ps[:, bq, :]), reads=BK(bq), writes=[("qTn", st, blk)])

                def s_r():
                    for c in range(3):
                        T.op(PE, lambda c=c: nc.tensor.matmul(ps[0:64, br, :], lhsT=wqb[:, c, h * 192 + 128:h * 192 + 192], rhs=qkT[:, c, sl],
                                                              start=(c == 0), stop=(c == 2)),
                             reads=["wqb"] + qk_keys, writes=BK(br), inc=(c == 2))
                    T.op(DVE, lambda: nc.vector.tensor_tensor(out=rt1[:], in0=ps[0:64, br, :], in1=cosT[:, sl], op=ALU.mult),
                         reads=BK(br), writes=["rt1"])

                def s_s():
                    for c in range(3):
                        T.op(PE, lambda c=c: nc.tensor.matmul(ps[0:64, bs_, :], lhsT=wqbsw[:, c, h * 64:(h + 1) * 64], rhs=qkT[:, c, sl],
                                                              start=(c == 0), stop=(c == 2)),
                             reads=["wqbsw"] + qk_keys, writes=BK(bs_), inc=(c == 2))
                    T.op(DVE, lambda: nc.vector.tensor_tensor(out=rt2[:], in0=ps[0:64, bs_, :], in1=ssinT[:, sl], op=ALU.mult),
                         reads=BK(bs_), writes=["rt2"])
                    T.op(POOL, lambda: nc.gpsimd.tensor_tensor(out=qTr_s[st][0:64, sl], in0=rt1[:], in1=rt2[:], op=ALU.add),
                         reads=["rt1", "rt2"], writes=[("qTr", st, blk)])

                def s_k():
                    T.op(PE, lambda: nc.tensor.matmul(ps[:, bk, :], lhsT=wkvk[:, h * P:(h + 1) * P], rhs=qkT[:, 3, sl], start=True, stop=True),
                         reads=["wkvk"] + qk_keys, writes=BK(bk))
                    T.op(DVE, lambda: nc.vector.tensor_copy(out=kTn_s[st][:, sl], in_=ps[:, bk, :]), reads=BK(bk), writes=[("kTn", st, blk)])
                return [s_q, s_r, s_s, s_k]

            def emit_proj(h, blk, banks):
                for f in proj_steps(h, blk, banks):
                    f()

            for blk in range(4):
                emit_proj(0, blk, (0, 2, 3, 1) if blk % 2 == 0 else (4, 6, 7, 5))
            if stage == 3:
                dbg_dump([(qTn[:, 0:1024], 1024), (qTr[0:64, 0:1024], 1024), (kTn2[:, 0, 0:1024], 1024),
                          (vext[:, 0, :, :].rearrange("p h d -> p (h d)"), 520)])
                return nc

            deferred = []
            for h in range(4):
                st = h % 2
                qTn_h, qTr_h, kTn_h = qTn_s[st], qTr_s[st], kTn_s[st]

                def emit_qk(n):
                    qb, kt = divmod(n, NT)
                    bs = n % 3
                    ksl = slice(kt * P, (kt + 1) * P)
                    qsl = slice(qb * 512, (qb + 1) * 512)
                    T.op(PE, lambda: nc.tensor.matmul(ps[:, bs, :], lhsT=kTn_h[:, ksl], rhs=qTn_h[:, qsl], start=True, stop=False),
                         reads=[("kTn", st, kt // 4), ("qTn", st, qb)], writes=BK(bs), inc=False)
                    T.op(PE, lambda: nc.tensor.matmul(ps[:, bs, :], lhsT=kpeT[:, ksl], rhs=qTr_h[:, qsl], start=False, stop=True),
                         reads=[("kpeT", kt // 4), ("qTr", st, qb), "pe_pad"], writes=BK(bs))

                def fin_parts(qb, h=h):
                    ob = 4 + 2 * (qb % 2)
                    qsl = slice(qb * 512, (qb + 1) * 512)
                    kf_ = ("fin", qb % 2)
                    fv = fin[:, qb % 2, :, :]

                    def part_a():
                        for qi in range(4):
                            bo = ob + qi // 2
                            o0 = (qi % 2) * 130
                            T.op(ACT, lambda qi=qi, bo=bo, o0=o0: nc.scalar.activation(out=scr[:, 0:128], in_=ps[:, bo, o0:o0 + 128], func=AF.Square,
                                                                                       scale=inv128, accum_out=fv[:, 0, qi:qi + 1]),
                                 reads=BK(bo), writes=["scr", kf_])
                        for qi in range(4):
                            bo = ob + qi // 2
                            o0 = (qi % 2) * 130
                            T.op(DVE, lambda qi=qi, bo=bo, o0=o0: nc.vector.tensor_copy(out=fv[:, 1, qi:qi + 1], in_=ps[:, bo, o0 + 128:o0 + 129]),
                                 reads=BK(bo), writes=[kf_])
                        T.op(DVE, lambda: nc.vector.tensor_tensor(out=fv[:, 2, :], in0=fv[:, 1, :], in1=fv[:, 1, :], op=ALU.mult), reads=[kf_], writes=[kf_])
                        T.op(DVE, lambda: nc.vector.scalar_tensor_tensor(out=fv[:, 3, :], in0=fv[:, 2, :], scalar=EPS, in1=fv[:, 0, :],
                                                                         op0=ALU.mult, op1=ALU.add), reads=[kf_], writes=[kf_])

                    def part_b():
                        T.op(ACT, lambda: nc.scalar.activation(out=fv[:, 4, :], in_=fv[:, 3, :], func=AF.Ln), reads=[kf_], writes=[kf_])
                        T.op(ACT, lambda: nc.scalar.activation(out=fv[:, 5, :], in_=fv[:, 4, :], func=AF.Exp, scale=-0.5), reads=[kf_], writes=[kf_])
                        for qi in range(4):
                            bo = ob + qi // 2
                            o0 = (qi % 2) * 130
                            T.op(DVE, lambda qi=qi, bo=bo, o0=o0: nc.vector.tensor_scalar(out=onb[:, qi, :], in0=ps[:, bo, o0:o0 + 128],
                                                                                          scalar1=fv[:, 5, qi:qi + 1], scalar2=None, op0=ALU.mult),
                                 reads=BK(bo) + [kf_], writes=[("onb", qi)])

                    def part_c():
                        for qi in range(4):
                            T.op(PE, lambda qi=qi: nc.tensor.transpose(out=psb[:, 3, qi * P:(qi + 1) * P], in_=onb[:, qi, :], identity=ident_b[:]),
                                 reads=[("onb", qi), "ident_b"], writes=BK(3), inc=(qi == 3))
                        T.op(DVE, lambda: nc.vector.tensor_scalar(out=catT[:, 4 + h, qsl], in0=psb[:, 3, 0:512], scalar1=cols[:, 4:5], scalar2=None,
                                                                  op0=ALU.mult),
                             reads=BK(3) + [C_ALL], writes=[("catT", 4 + h, qb)])
                    return part_a, part_b, part_c

                NIT = 4 * NT
                emit_qk(0)
                emit_qk(1)
                for n in range(NIT):
                    qb, kt = divmod(n, NT)
                    if n + 2 < NIT:
                        emit_qk(n + 2)
                    bs = n % 3
                    pslot = n % 3
                    T.op(ACT, lambda: nc.scalar.activation(out=PT[:, pslot, :], in_=ps[:, bs, :], func=AF.Exp, scale=SCALE),
                         reads=BK(bs), writes=[("PT", pslot)])
                    ob = 4 + 2 * (qb % 2)
                    for qi in range(4):
                        bo = ob + qi // 2
                        o0 = (qi % 2) * 130
                        T.op(PE, lambda qi=qi, bo=bo, o0=o0: nc.tensor.matmul(ps[:, bo, o0:o0 + 130], lhsT=PT[:, pslot, qi * P:(qi + 1) * P],
                                                                              rhs=vext[:, kt, h, :], start=(kt == 0 and qi % 2 == 0),
                                                                              stop=(kt == NT - 1), skip_group_check=True),
                             reads=[("PT", pslot), "vext"], writes=BK(bo), inc=(qi % 2 == 1))
                    gn = h * NIT + n
                    for item in [d_ for d_ in deferred if d_[0] <= gn]:
                        item[1]()
                        deferred.remove(item)
                    if kt == NT - 1:
                        pa, pb_, pc_ = fin_parts(qb)
                        deferred.append((gn + 2, pa))
                        deferred.append((gn + 4, pb_))
                        deferred.append((gn + 7, pc_))
                    if h + 1 < 4 and kt in (3, 6, 9, 12):
                        proj_steps(h + 1, qb, (3, 3, 3, 3))[kt // 3 - 1]()
            while deferred:
                item = deferred.pop(0)
                item[1]()
            T.barrier(ALL)
        pw.close()
        if stage == 4:
            dbg_dump([(catT[:, 4 + h, 0:1024], 1024) for h in range(4)])
            return nc

        for rnd in range(2):
            with ExitStack() as p3:
                qsT = carve_f(0, 2 * S).rearrange("p (h s) -> p h s", s=S)
                opart = carve_f(4096, NT * 256).rearrange("p (i v) -> p i v", v=256)
                qtT = carve_b(8192, 4 * S).rearrange("p (c s) -> p c s", s=S)
                ktT = carve_b(12288, 4 * S).rearrange("p (c s) -> p c s", s=S)
                vtok = sb("vtok", [P, NT, 256], BF16, p3)
                sg = sb("sg", [P, NT, 256], BF16, p3)
                dA = sb("dA", [P, 4, NT], F32, p3)
                dB = sb("dB", [P, 4, NT], F32, p3)
                dM = sb("dM", [P, 4, NT], F32, p3)
                pg = p3.enter_context(ExitStack())
                wrf = sb("wrf", [P, 2, KC, 256], BF16, pg)
                pq = pg.enter_context(ExitStack())
                wrq = sb("wrq", [P, 3, KC, 256], BF16, pq)
                for gi, c0 in ((1, 512), (2, 2048), (0, 0)):
                    T.dma(POOL, d_w[gi], wrq[:, gi, :, :], w_in_d[:, :, c0 + rnd * 256:c0 + rnd * 256 + 256], writes=[("wrq", gi)])
                for gi, c0 in enumerate((1024, 1536)):
                    T.dma(POOL, d_w[3 + gi], wrf[:, gi, :, :], w_in_d[:, :, c0 + rnd * 256:c0 + rnd * 256 + 256], writes=[("wrf", gi)])

                def vg_mm(i):
                    for half, gi in ((0, 1), (1, 2)):
                        b = 2 * half + i % 2
                        for kc in range(KC):
                            T.op(PE, lambda kc=kc, b=b, gi=gi: nc.tensor.matmul(ps[:, b, 0:256], lhsT=hT[:, kc, i * P:(i + 1) * P], rhs=wrq[:, gi, kc, :],
                                                                              start=(kc == 0), stop=(kc == KC - 1)),
                                 reads=[("wrq", gi), ("hT", i)], writes=BK(b), inc=(kc == KC - 1))

                def vg_ev(i):
                    T.op(DVE, lambda: nc.vector.tensor_copy(out=vtok[:, i, :], in_=ps[:, i % 2, 0:256]), reads=BK(i % 2), writes=[("vtok", i)])
                    T.op(ACT, lambda: nc.scalar.activation(out=sg[:, i, :], in_=ps[:, 2 + i % 2, 0:256], func=AF.Silu), reads=BK(2 + i % 2), writes=[("sg", i)])

                pipeline([vg_mm, vg_ev], NT)

                def q_mm(n):
                    hh, blk = divmod(n, 4)
                    sl = slice(blk * 512, (blk + 1) * 512)
                    b = 6 + n % 2
                    for kc in range(KC):
                        T.op(PE, lambda kc=kc: nc.tensor.matmul(ps[:, b, :], lhsT=wrq[:, 0, kc, hh * P:(hh + 1) * P], rhs=hT[:, kc, sl],
                                                                start=(kc == 0), stop=(kc == KC - 1)),
                             reads=[("wrq", 0)] + hkeys("hT", blk * 512, blk * 512 + 512), writes=BK(b), inc=(kc == KC - 1))

                def q_ev(n):
                    hh, blk = divmod(n, 4)
                    sl = slice(blk * 512, (blk + 1) * 512)
                    b = 6 + n % 2
                    T.op(ACT, lambda: nc.scalar.activation(out=qsT[:, hh, sl], in_=ps[:, b, :], func=AF.Silu), reads=BK(b), writes=[("qsT", hh, blk)])

                pipeline([q_mm, q_ev], 8)
                T.barrier(ALL)
                pq.close()

                ge = sb("ge", [P, 2, 512], F32, pg)
                gl2 = sb("gl2", [P, 2, 512], F32, pg)
                gl1 = sb("gl1", [P, 2, 512], F32, pg)
                gsp = sb("gsp", [P, 2, 512], F32, pg)
                gsn = sb("gsn", [P, 3, 512], F32, pg)
                gG = sb("gG", [P, 512], F32, pg)
                grel = sb("grel", [P, 2, 512], F32, pg)
                grl2 = sb("grl2", [P, 2, 512], F32, pg)
                gE1 = sb("gE1", [P, 2, 512], F32, pg)
                gE2 = sb("gE2", [P, 2, 512], F32, pg)
                gsm = sb("gsm", [P, 2, 16], F32, pg)

                def piece(n):
                    d, r = divmod(n, 8)
                    hh, blk = divmod(r, 4)
                    return d, hh, blk, d * 2 + hh, d * 4 + rnd * 2 + hh

                def g_s0(n):
                    d, hh, blk, ci, lcol = piece(n)
                    sl = slice(blk * 512, (blk + 1) * 512)
                    b = 4 + n % 2
                    for kc in range(KC):
                        T.op(PE, lambda kc=kc: nc.tensor.matmul(ps[:, b, :], lhsT=wrf[:, d, kc, hh * P:(hh + 1) * P], rhs=hT[:, kc, sl],
                                                                start=(kc == 0), stop=(kc == KC - 1)),
                             reads=[("wrf", d)] + hkeys("hT", blk * 512, blk * 512 + 512), writes=BK(b), inc=(kc == KC - 1))

                def g_s1_ops(n):
                    d, hh, blk, ci, lcol = piece(n)
                    b = 4 + n % 2
                    s2 = n % 2
                    return [
                        lambda: T.op(ACT, lambda: nc.scalar.activation(out=ge[:, s2, :], in_=ps[:, b, :], func=AF.Exp, scale=-1.0), reads=BK(b), writes=[("ge", s2)]),
                        lambda: T.op(ACT, lambda: nc.scalar.activation(out=gl2[:, s2, :], in_=ge[:, s2, :], func=AF.Ln, bias=cst[:, 3:4]),
                                     reads=[("ge", s2), "cst"], writes=[("gl2", s2)]),
                        lambda: T.op(ACT, lambda: nc.scalar.activation(out=gsp[:, s2, :], in_=gl2[:, s2, :], func=AF.Exp, scale=-1.0),
                                     reads=[("gl2", s2)], writes=[("gsp", s2)]),
                        lambda: T.op(ACT, lambda: nc.scalar.activation(out=gl1[:, s2, :], in_=gsp[:, s2, :], func=AF.Ln, scale=omlb[:, lcol:lcol + 1],
                                                                       bias=lbT[:, lcol:lcol + 1]), reads=[("gsp", s2), "lbT", "omlb"], writes=[("gl1", s2)]),
                    ]

                def g_s2(n):
                    d, hh, blk, ci, lcol = piece(n)
                    s2 = n % 2
                    s3 = n % 3
                    g_ap = gl1[:, s2, :]
                    rel = grel[:, s2, :]
                    R3 = rel.rearrange("p (c t) -> p c t", t=P)
                    rkey = ("grel", s2)
                    sm = gsm[:, s2, :]
                    if d == 0:
                        T.op(DVE, lambda: nc.vector.tensor_tensor_scan(out=rel, data0=m0[:], data1=g_ap, initial=0.0, op0=ALU.mult, op1=ALU.add),
                             reads=[("gl1", s2), "m0"], writes=[rkey])
                        G3 = R3
                        gkey = rkey
                    else:
                        T.op(DVE, lambda: nc.vector.tensor_tensor_scan(out=gG[:], data0=m0[:], data1=g_ap, initial=0.0, op0=ALU.mult, op1=ALU.add),
                             reads=[("gl1", s2), "m0"], writes=["gG"])
                        G3 = gG[:].rearrange("p (c t) -> p c t", t=P)
                        gkey = "gG"
                    T.op(DVE, lambda: nc.vector.tensor_scalar(out=gsn[:, s3, :], in0=gsp[:, s2, :], scalar1=-1.0, scalar2=1.0, op0=ALU.mult, op1=ALU.add),
                         reads=[("gsp", s2)], writes=[("gsn", s3)])
                    if d == 1:
                        T.op(DVE, lambda: nc.vector.tensor_tensor(out=rel, in0=gG[:], in1=g_ap, op=ALU.subtract), reads=["gG", ("gl1", s2)], writes=[rkey])
                    T.op(DVE, lambda: nc.vector.tensor_copy(out=sm[:, 0:4].unsqueeze(2), in_=G3[:, :, 127:128]), reads=[gkey], writes=[("gsm", s2)])
                    T.op(DVE, lambda: nc.vector.tensor_copy(out=sm[:, 4:8].unsqueeze(2), in_=R3[:, :, 64:65]), reads=[rkey], writes=[("gsm", s2)])
                    T.op(DVE, lambda: nc.vector.tensor_tensor(out=grl2[:, s2, :].rearrange("p (c t) -> p c t", t=P), in0=R3,
                                                              in1=R3[:, :, 64:65].broadcast_to([P, 4, P]), op=ALU.subtract),
                         reads=[rkey], writes=[("grl2", s2)])
                    T.op(DVE, lambda: nc.vector.tensor_tensor(out=sm[:, 8:12], in0=sm[:, 0:4], in1=sm[:, 4:8], op=ALU.subtract),
                         reads=[("gsm", s2)], writes=[("gsm", s2)])

                def g_s3_ops(n):
                    d, hh, blk, ci, lcol = piece(n)
                    s2 = n % 2
                    cs = slice(blk * 4, blk * 4 + 4)
                    sm = gsm[:, s2, :]
                    sgn = 1.0 if d == 0 else -1.0
                    return [
                        lambda: T.op(ACT, lambda: nc.scalar.activation(out=gE1[:, s2, :], in_=grl2[:, s2, :], func=AF.Exp, scale=sgn),
                                     reads=[("grl2", s2)], writes=[("gE1", s2)]),
                        lambda: T.op(ACT, lambda: nc.scalar.activation(out=dA[:, ci, cs], in_=sm[:, 0:4], func=AF.Exp), reads=[("gsm", s2)], writes=[("dA", ci, blk)]),
                        lambda: T.op(ACT, lambda: nc.scalar.activation(out=gE2[:, s2, :], in_=grl2[:, s2, :], func=AF.Exp, scale=-sgn),
                                     reads=[("grl2", s2)], writes=[("gE2", s2)]),
                        lambda: T.op(ACT, lambda: nc.scalar.activation(out=(dM if d == 0 else dB)[:, ci, cs], in_=sm[:, 4:8], func=AF.Exp),
                                     reads=[("gsm", s2)], writes=[("dMB", ci, blk)]),
                        lambda: T.op(ACT, lambda: nc.scalar.activation(out=(dB if d == 0 else dM)[:, ci, cs], in_=sm[:, 8:12], func=AF.Exp),
                                     reads=[("gsm", s2)], writes=[("dBM", ci, blk)]),
                    ]

                def g_s13(step_n):
                    o1 = g_s1_ops(step_n) if step_n < 16 else []
                    o3 = g_s3_ops(step_n - 2) if 0 <= step_n - 2 < 16 else []
                    while o1 or o3:
                        if o1:
                            o1.pop(0)()
                        if o3:
                            o3.pop(0)()

                def g_s4(n):
                    d, hh, blk, ci, lcol = piece(n)
                    s2 = n % 2
                    s3 = n % 3
                    sl = slice(blk * 512, (blk + 1) * 512)
                    T.op(DVE, lambda: nc.vector.tensor_tensor(out=qtT[:, ci, sl], in0=qsT[:, hh, sl], in1=gE1[:, s2, :], op=ALU.mult),
                         reads=[("qsT", hh, blk), ("gE1", s2)], writes=[("qtT", ci, blk)])
                    T.op(DVE, lambda: nc.vector.scalar_tensor_tensor(out=ktT[:, ci, sl], in0=gsn[:, s3, :], scalar=omlb[:, lcol:lcol + 1],
                                                                     in1=gE2[:, s2, :], op0=ALU.mult, op1=ALU.mult),
                         reads=[("gsn", s3), ("gE2", s2), "omlb"], writes=[("ktT", ci, blk)])

                for gstep in range(16 + 5):
                    if gstep < 16:
                        g_s0(gstep)
                    if 0 <= gstep - 1 < 18:
                        g_s13(gstep - 1)
                    if 0 <= gstep - 2 < 16:
                        g_s2(gstep - 2)
                    if 0 <= gstep - 4 < 16:
                        g_s4(gstep - 4)
                if stage == 5 and rnd == 0:
                    dbg_dump([(qtT[:, 0, 0:512], 512), (ktT[:, 0, 0:512], 512), (qtT[:, 2, 0:512], 512), (ktT[:, 2, 0:512], 512),
                              (dA[:, :, :].rearrange("p a b -> p (a b)"), 64), (dB[:, :, :].rearrange("p a b -> p (a b)"), 64),
                              (dM[:, :, :].rearrange("p a b -> p (a b)"), 64), (qsT[:, 0, 0:512], 512), (vtok[:, 0, :], 256), (sg[:, 0, :], 256)])
                    return nc

                T.barrier(ALL)
                pg.close()
                if rnd == 1:
                    for c4 in range(4):
                        hc_, kh = divmod(c4, 2)
                        T.dma(POOL, d_w[12 + hc_], wout_v[:, 4 * kh:4 * kh + 4, hc_ * 512:(hc_ + 1) * 512],
                              w_out_d[:, 4 * kh:4 * kh + 4, hc_ * 512:(hc_ + 1) * 512], writes=[("wout", hc_)])
                St = sb("St", [P, 4, P], F32, p3)
                Stmp = sb("Stmp", [P, 4, P], F32, p3)
                Sb = sb("Sb", [P, 2, 4, P], BF16, p3)
                osum = sb("osum", [P, 4, 4, P], F32, p3)
                onh = sb("onh", [P, 2, 4, P], BF16, p3)
                fh = sb("fh", [P, 4, 4, 4], F32, p3)
                scr3 = sb("scr3", [P, P], BF16, p3)
                Am_all = sb("Am_all", [P, 4, NT, P], BF16, p3)
                ktok_all = sb("ktok_all", [P, 4, NT, P], BF16, p3)

                def chunk_of(ci, step):
                    return step if ci < 2 else NT - 1 - step

                nb1 = 0
                for ci in range(4):
                    mask = maskF if ci < 2 else maskB
                    mkey = "maskF" if ci < 2 else "maskB"
                    for cg in range(4):
                        ba = nb1 % 2
                        bt = 2 + nb1 % 2
                        nb1 += 1
                        for j in range(4):
                            c = cg * 4 + j
                            csl = slice(c * P, (c + 1) * P)
                            T.op(PE, lambda j=j, csl=csl: nc.tensor.matmul(ps[:, ba, j * P:(j + 1) * P], lhsT=ktT[:, ci, csl], rhs=qtT[:, ci, csl],
                                                                           start=True, stop=True), writes=BK(ba), inc=(j == 3))
                        for j in range(4):
                            c = cg * 4 + j
                            csl = slice(c * P, (c + 1) * P)
                            T.op(PE, lambda j=j, csl=csl: nc.tensor.transpose(out=psb[:, bt, j * P:(j + 1) * P], in_=ktT[:, ci, csl], identity=ident_b[:]),
                                 reads=["ident_b"], writes=BK(bt), inc=(j == 3))
                        T.op(DVE, lambda: nc.vector.tensor_tensor(out=Am_all[:, ci, cg * 4:(cg + 1) * 4, :],
                                                                  in0=ps[:, ba, :].rearrange("p (j t) -> p j t", t=P),
                                                                  in1=mask[:].unsqueeze(1).broadcast_to([P, 4, P]), op=ALU.mult),
                             reads=BK(ba) + [mkey], writes=[("Am", ci, cg)])
                        T.op(ACT, lambda: nc.scalar.activation(out=ktok_all[:, ci, cg * 4:(cg + 1) * 4, :],
                                                               in_=psb[:, bt, 0:512].rearrange("p (j t) -> p j t", t=P), func=AF.Copy),
                             reads=BK(bt), writes=[("ktok", ci, cg)])

                def emit_U(step):
                    bu = 4 + step % 2
                    for ci in range(4):
                        c = chunk_of(ci, step)
                        vsl = slice((ci % 2) * P, (ci % 2 + 1) * P)
                        T.op(PE, lambda ci=ci, c=c, vsl=vsl: nc.tensor.matmul(ps[:, bu, ci * P:(ci + 1) * P], lhsT=ktok_all[:, ci, c, :], rhs=vtok[:, c, vsl],
                                                                              start=True, stop=True),
                             reads=[("ktok", ci, c // 4)], writes=BK(bu), inc=(ci == 3))

                def emit_rec(step):
                    bu = 4 + step % 2
                    if step > 0:
                        for ci in range(4):
                            c = chunk_of(ci, step)
                            T.op(DVE, lambda ci=ci, c=c: nc.vector.tensor_scalar(out=Stmp[:, ci, :], in0=St[:, ci, :], scalar1=dA[:, ci, c:c + 1], scalar2=None,
                                                                                 op0=ALU.mult), reads=[("St", ci)], writes=[("Stmp", ci)])
                    for ci in range(4):
                        c = chunk_of(ci, step)
                        usl = slice(ci * P, (ci + 1) * P)
                        if step == 0:
                            T.op(DVE, lambda ci=ci, c=c, usl=usl: nc.vector.tensor_scalar(out=St[:, ci, :], in0=ps[:, bu, usl], scalar1=dB[:, ci, c:c + 1],
                                                                                          scalar2=None, op0=ALU.mult), reads=BK(bu), writes=[("St", ci)])
                        else:
                            T.op(DVE, lambda ci=ci, c=c, usl=usl: nc.vector.scalar_tensor_tensor(out=St[:, ci, :], in0=ps[:, bu, usl], scalar=dB[:, ci, c:c + 1],
                                                                                                 in1=Stmp[:, ci, :], op0=ALU.mult, op1=ALU.add),
                                 reads=BK(bu) + [("Stmp", ci)], writes=[("St", ci)])

                def emit_Sb(step):
                    if step >= NT - 1:
                        return
                    for ci in range(4):
                        cn = chunk_of(ci, step + 1)
                        T.op(ACT, lambda ci=ci, cn=cn: nc.scalar.activation(out=Sb[:, step % 2, ci, :], in_=St[:, ci, :], func=AF.Identity,
                                                                            scale=dM[:, ci, cn:cn + 1]),
                             reads=[("St", ci)], writes=[("Sb", step % 2, ci)])

                def emit_O(step):
                    bo = 6 + step % 2
                    for ci in range(4):
                        c = chunk_of(ci, step)
                        csl = slice(c * P, (c + 1) * P)
                        vsl = slice((ci % 2) * P, (ci % 2 + 1) * P)
                        osl = slice(ci * P, (ci + 1) * P)
                        if step > 0:
                            T.op(PE, lambda ci=ci, csl=csl, osl=osl: nc.tensor.matmul(ps[:, bo, osl], lhsT=qtT[:, ci, csl], rhs=Sb[:, (step - 1) % 2, ci, :],
                                                                                     start=True, stop=False),
                                 reads=[("Sb", (step - 1) % 2, ci)], writes=BK(bo), inc=False)
                        T.op(PE, lambda ci=ci, c=c, vsl=vsl, osl=osl: nc.tensor.matmul(ps[:, bo, osl], lhsT=Am_all[:, ci, c, :], rhs=vtok[:, c, vsl],
                                                                                      start=(step == 0), stop=True),
                             reads=[("Am", ci, c // 4)], writes=BK(bo), inc=(ci == 3))

                def fin_A(step):
                    bo = 6 + step % 2
                    sl4 = step % 4
                    for pr in range(2):
                        c = chunk_of(2 * pr, step)
                        src = ps[:, bo, pr * 256:(pr + 1) * 256]
                        if step < NT // 2:
                            T.op(ACT, lambda c=c, src=src: nc.scalar.activation(out=opart[:, c, :], in_=src, func=AF.Copy), reads=BK(bo), writes=[("opart", c)])
                        else:
                            T.op(DVE, lambda c=c, src=src, pr=pr: nc.vector.tensor_tensor(out=osum[:, sl4, 2 * pr:2 * pr + 2, :].rearrange("p c v -> p (c v)"),
                                                                                          in0=src, in1=opart[:, c, :], op=ALU.add),
                                 reads=BK(bo) + [("opart", c)], writes=[("osum", sl4)])

                def fin_B(step):
                    if step < NT // 2:
                        return
                    sl4 = step % 4
                    for ci in range(4):
                        T.op(ACT, lambda ci=ci: nc.scalar.activation(out=scr3[:], in_=osum[:, sl4, ci, :], func=AF.Square, scale=inv128,
                                                                     accum_out=fh[:, sl4, 0, ci:ci + 1]), reads=[("osum", sl4)], writes=["scr3", ("fh", sl4)])
                    T.op(ACT, lambda: nc.scalar.activation(out=fh[:, sl4, 1, :], in_=fh[:, sl4, 0, :], func=AF.Ln, bias=cst[:, 0:1]),
                         reads=[("fh", sl4), "cst"], writes=[("fh", sl4)])
                    T.op(ACT, lambda: nc.scalar.activation(out=fh[:, sl4, 2, :], in_=fh[:, sl4, 1, :], func=AF.Exp, scale=-0.5),
                         reads=[("fh", sl4)], writes=[("fh", sl4)])

                def fin_C(step):
                    if step < NT // 2:
                        return
                    sl4 = step % 4
                    for ci in range(4):
                        c = chunk_of(ci, step)
                        vsl = slice((ci % 2) * P, (ci % 2 + 1) * P)
                        T.op(DVE, lambda ci=ci, c=c, vsl=vsl: nc.vector.scalar_tensor_tensor(out=onh[:, step % 2, ci, :], in0=osum[:, sl4, ci, :],
                                                                                             scalar=fh[:, sl4, 2, ci:ci + 1], in1=sg[:, c, vsl],
                                                                                             op0=ALU.mult, op1=ALU.mult),
                             reads=[("osum", sl4), ("fh", sl4), ("sg", c)], writes=[("onh", step % 2)])

                def fin_D(step):
                    if step < NT // 2:
                        return
                    bt = step % 2
                    for ci in range(4):
                        T.op(PE, lambda ci=ci: nc.tensor.transpose(out=psb[:, bt, ci * P:(ci + 1) * P], in_=onh[:, step % 2, ci, :], identity=ident_b[:]),
                             reads=[("onh", step % 2), "ident_b"], writes=BK(bt), inc=(ci == 3))

                def fin_E(step):
                    if step < NT // 2:
                        return
                    bt = step % 2
                    for pr in range(2):
                        c = chunk_of(2 * pr, step)
                        T.op(ACT, lambda pr=pr, c=c: nc.scalar.activation(out=catT[:, rnd * 2:rnd * 2 + 2, c * P:(c + 1) * P],
                                                                          in_=psb[:, bt, pr * 256:(pr + 1) * 256].rearrange("p (h t) -> p h t", t=P),
                                                                          func=AF.Identity, scale=cols[:, 5:6]),
                             reads=BK(bt) + [C_ALL], writes=[("catT", rnd, c)])

                emit_U(0)
                for step in range(NT + 5):
                    if step + 1 < NT:
                        emit_U(step + 1)
                    if step < NT:
                        emit_rec(step)
                        emit_Sb(step)
                        emit_O(step)
                    for lagk, fn in ((1, fin_A), (2, fin_B), (3, fin_C), (4, fin_D), (5, fin_E)):
                        if 0 <= step - lagk < NT:
                            fn(step - lagk)
                T.barrier(ALL)
        if stage == 6:
            dbg_dump([(catT[:, h, 0:1024], 1024) for h in range(4)])
            return nc

        yres = ybig[:].rearrange("p (i d) -> p i d", d=D)
        with ExitStack() as p6:
            wout = wout_v
            lnb = sb("lnb", [P, 2, D], F32, p6)
            for r6 in range(2):
                T.dma(SP, d_c2[3], lnb[:, r6, :], lnv_d[2 * r6:2 * r6 + 1, :].partition_broadcast(P), writes=["lnb"])
            T.op(DVE, lambda: nc.vector.tensor_scalar(out=lnb[:].rearrange("p r d -> p (r d)"), in0=lnb[:].rearrange("p r d -> p (r d)"),
                                                      scalar1=ALPHA, scalar2=None, op0=ALU.mult), reads=["lnb"], writes=["lnb"])

            NX6 = 10
            order6 = list(range(4, NT)) + [0, 1, 2, 3]
            til = lambda p_: order6[p_]
            xt6 = sb("xtb", [P, NX6, D], F32, p6)
            xk = lambda i: ("xt", i % NX6)
            xa = lambda i: xt6[:, i % NX6, :]
            bk6 = lambda i: (2 * (i % 2), 2 * (i % 2) + 1)

            def p6_s0(i):
                T.dma(SP, d_x[i % NX6], xa(i), x_d[til(i) * P:(til(i) + 1) * P, :], writes=[xk(i)])

            def p6_s2(i):
                ln_apply(xa(i), xk(i), i % NX6)
                b0 = 4 + 2 * (i % 2)
                for hc in range(2):
                    b = b0 + hc
                    T.op(PE, lambda hc=hc, b=b: nc.tensor.matmul(ps[:, b, :], lhsT=ones_b[:], rhs=brow[:, 0, hc * 512:(hc + 1) * 512], start=True, stop=False),
                         reads=["ones_b", "brow"], writes=BK(b), inc=False)
                    for kc in range(KC):
                        T.op(PE, lambda kc=kc, hc=hc, b=b: nc.tensor.matmul(ps[:, b, :], lhsT=catT[:, kc, til(i) * P:(til(i) + 1) * P],
                                                                            rhs=wout[:, kc, hc * 512:(hc + 1) * 512], start=False, stop=(kc == KC - 1)),
                             reads=[("wout", hc)], writes=BK(b), inc=(kc == KC - 1))

            def p6_s3(i):
                xn_ap, key = xa(i), xk(i)
                b0 = 4 + 2 * (i % 2)
                T.op(DVE, lambda: nc.vector.tensor_tensor(out=xn_ap, in0=xn_ap, in1=lnb[:, 0, :], op=ALU.mult), reads=[key, "lnb"], writes=[key])
                T.op(DVE, lambda: nc.vector.tensor_tensor(out=xn_ap, in0=xn_ap, in1=ps[:, b0:b0 + 2, :].rearrange("p b n -> p (b n)"), op=ALU.add),
                     reads=[key] + BK(b0) + BK(b0 + 1), writes=[key])

            def p6_s6(i):
                ti = til(i)
                transpose_evac(hT, "hT", ti, bk6(i), 2, act_extra=2)
                extra = [("wout", 0), ("wout", 1)] if ti < 4 else []
                T.op(DVE, lambda: nc.vector.tensor_tensor(out=yres[:, ti, :], in0=xa(i), in1=lnb[:, 1, :], op=ALU.mult), reads=[xk(i), "lnb"],
                     writes=[("yres", ti)] + extra)

            def p6_mm(i):
                b0 = 4 + 2 * (i % 2)
                for hc in range(2):
                    b = b0 + hc
                    T.op(PE, lambda hc=hc, b=b: nc.tensor.matmul(ps[:, b, :], lhsT=ones_b[:], rhs=brow[:, 0, hc * 512:(hc + 1) * 512], start=True, stop=False),
                         reads=["ones_b", "brow"], writes=BK(b), inc=False)
                    for kc in range(KC):
                        T.op(PE, lambda kc=kc, hc=hc, b=b: nc.tensor.matmul(ps[:, b, :], lhsT=catT[:, kc, til(i) * P:(til(i) + 1) * P],
                                                                            rhs=wout[:, kc, hc * 512:(hc + 1) * 512], start=False, stop=(kc == KC - 1)),
                             reads=[("wout", hc)], writes=BK(b), inc=(kc == KC - 1))

            def p6_apply_pair(i):
                o1 = [lambda: T.op(ACT, lambda: nc.scalar.activation(out=xa(i), in_=xa(i), func=AF.Identity, scale=lnst[:, til(i), 0:1], bias=lnst[:, til(i), 1:2]),
                                   reads=[xk(i), ("lnst", til(i))], writes=[xk(i)])] if 0 <= i < NT else []
                i2 = i - 4
                o2 = ln_apply_ops(xa(i2), xk(i2), 12 + i2 % NX6) if 0 <= i2 < NT else []
                while o1 or o2:
                    if o1:
                        o1.pop(0)()
                    if o2:
                        o2.pop(0)()
                if 0 <= i < NT:
                    p6_mm(i)

            inr = lambda i: 0 <= i < NT
            for step in range(NT + 10):
                if inr(step):
                    p6_s0(step)
                p6_apply_pair(step - 3)
                if inr(step - 4):
                    p6_s3(step - 4)
                if inr(step - 5):
                    ln_stats_a(xa(step - 5), xk(step - 5), 12 + (step - 5) % NX6)
                if inr(step - 6):
                    ln_stats_b(12 + (step - 6) % NX6)
                if inr(step - 8):
                    transpose_pe(xa(step - 8), [xk(step - 8)], bk6(step - 8))
                if inr(step - 9):
                    p6_s6(step - 9)
            T.barrier(ALL)
        if stage == 7:
            dbg_dump([(yres[:, i, :], 1024) for i in range(8)])
            return nc
        T.barrier(ALL)
        pc.close()

        with ExitStack() as p7:
            NUP = 3
            GS = 3
            NRA = 2 * GS + 1
            NRW = 2 * GS + 3
            wup = sb("wup", [P, NUP, KC, 256], BF16, p7)
            wdn = sb("wdn", [P, NRW, D], BF16, p7)
            actT = sb("actT", [P, NRA, S], BF16, p7)
            ubuf = sb("ubuf", [P, 2, S + 2], F32, p7)
            cbuf = sb("cbuf", [P, 3, S], F32, p7)
            T.op(POOL, lambda: nc.gpsimd.memset(ubuf[:, :, 0:1], 0.0), writes=["ubuf_pad"], inc=False)
            T.op(POOL, lambda: nc.gpsimd.memset(ubuf[:, :, S + 1:S + 2], 0.0), writes=["ubuf_pad"])
            groups = []
            j0 = 0
            while j0 < NJ:
                j1 = min(NJ, j0 + GS)
                if NJ - j1 == 1:
                    j1 = NJ
                groups.append((j0, j1))
                j0 = j1
            gend = {g[1] - 1: g for g in groups}
            dw_up = d_w[0:3]
            dw_dn = d_w[3:13]
            assert NRW <= 10

            def load_j(j):
                T.dma(POOL, dw_up[j % NUP], wup[:, j % NUP, :, :], w_up_d[j, :, :, :], writes=[("wup", j % NUP)])
                T.dma(POOL, dw_dn[j % NRW], wdn[:, j % NRW, :], w_dn_d[:, j, :], writes=[("wdn", j % NRW)])

            pending = []
            hold = [0]

            def down_unit(i, hc, g0, g1, n):
                def emit():
                    b = 4 + n % 4
                    if g0 == 0:
                        T.op(PE, lambda: nc.tensor.matmul(ps[:, b, :], lhsT=ones_b[:], rhs=brow[:, 1, hc * 512:(hc + 1) * 512], start=True, stop=False),
                             reads=["ones_b", "brow"], writes=BK(b), inc=False)
                    for jj in range(g0, g1):
                        T.op(PE, lambda jj=jj: nc.tensor.matmul(ps[:, b, :], lhsT=actT[:, jj % NRA, i * P:(i + 1) * P],
                                                                rhs=wdn[:, jj % NRW, hc * 512:(hc + 1) * 512],
                                                                start=(jj == g0 and g0 != 0), stop=(jj == g1 - 1)),
                             reads=[("actT", jj % NRA), ("wdn", jj % NRW)], writes=BK(b), inc=(jj == g1 - 1))
                    T.op(DVE, lambda: nc.vector.tensor_tensor(out=yres[:, i, hc * 512:(hc + 1) * 512],
                                                              in0=yres[:, i, hc * 512:(hc + 1) * 512], in1=ps[:, b, :], op=ALU.add),
                         reads=[("yres", i)] + BK(b), writes=[("yres", i)])
                return emit

            load_j(0)
            load_j(1)
            nb = 0
            nd = 0
            for j in range(NJ):
                if j + 2 < NJ:
                    load_j(j + 2)
                us = j % NUP
                rsl = j % NRA
                ca = j % 2
                for ab in range(2):
                    cb_i = ca if ab == 0 else 2
                    for half in range(2):
                        b0 = (nb % 2) * 2
                        nb += 1
                        for tb in range(2):
                            t0 = half * 1024 + tb * 512
                            for kc in range(KC):
                                T.op(PE, lambda kc=kc, tb=tb, t0=t0: nc.tensor.matmul(ps[:, b0 + tb, :], lhsT=wup[:, us, kc, ab * P:(ab + 1) * P],
                                                                                     rhs=hT[:, kc, t0:t0 + 512], start=(kc == 0), stop=(kc == KC - 1)),
                                     reads=[("wup", us)] + hkeys("hT", t0, t0 + 512), writes=BK(b0 + tb), inc=(kc == KC - 1))
                        src2 = ps[:, b0:b0 + 2, :].rearrange("p b n -> p (b n)")
                        T.op(ACT, lambda: nc.scalar.activation(out=ubuf[:, ab, 1 + half * 1024:1 + (half + 1) * 1024], in_=src2, func=AF.Copy),
                             reads=BK(b0) + BK(b0 + 1), writes=[("ubuf", ab, half)])
                        T.op(ACT, lambda: nc.scalar.activation(out=cbuf[:, cb_i, half * 1024:(half + 1) * 1024], in_=src2, func=AF.Identity,
                                                               scale=convp[:, j, ab, 1:2], bias=convp[:, j, ab, 3:4]),
                             reads=BK(b0) + BK(b0 + 1) + [C_ALL], writes=[("cbuf", cb_i, half)])
                        if hold[0] > 0:
                            hold[0] -= 1
                        else:
                            for _ in range(4):
                                if pending:
                                    pending.pop(0)()
                    ck = [("cbuf", cb_i, 0), ("cbuf", cb_i, 1)]
                    uk = [("ubuf", ab, 0), ("ubuf", ab, 1), "ubuf_pad"]
                    T.op(DVE, lambda: nc.vector.scalar_tensor_tensor(out=cbuf[:, cb_i, :], in0=ubuf[:, ab, 0:S], scalar=convp[:, j, ab, 0:1],
                                                                     in1=cbuf[:, cb_i, :], op0=ALU.mult, op1=ALU.add),
                         reads=ck + uk + [C_ALL], writes=ck)
                    T.op(DVE, lambda: nc.vector.scalar_tensor_tensor(out=cbuf[:, cb_i, :], in0=ubuf[:, ab, 2:S + 2], scalar=convp[:, j, ab, 2:3],
                                                                     in1=cbuf[:, cb_i, :], op0=ALU.mult, op1=ALU.add),
                         reads=ck + uk + [C_ALL], writes=ck)
                ck0 = [("cbuf", ca, 0), ("cbuf", ca, 1)]
                T.op(ACT, lambda: nc.scalar.activation(out=cbuf[:, ca, :], in_=cbuf[:, ca, :], func=AF.Gelu_apprx_tanh), reads=ck0, writes=ck0)
                T.op(POOL, lambda: nc.gpsimd.tensor_tensor(out=actT[:, rsl, :], in0=cbuf[:, ca, :], in1=cbuf[:, 2, :], op=ALU.mult),
                     reads=ck0 + [("cbuf", 2, 0), ("cbuf", 2, 1)], writes=[("actT", rsl)])
                if j in gend:
                    g0, g1 = gend[j]
                    while pending:
                        pending.pop(0)()
                    for i in range(NT):
                        for hc in range(2):
                            pending.append(down_unit(i, hc, g0, g1, nd))
                            nd += 1
                    hold[0] = 2
            lnv2 = ubuf[:, 1, 0:2 * D].rearrange("p (r d) -> p r d", d=D)
            for r6 in range(2):
                T.dma(SP, d_c2[4], lnv2[:, r6, :], lnv_d[4 + r6:5 + r6, :].partition_broadcast(P), reads=[], writes=["lnx", ("ubuf", 1, 0), ("ubuf", 1, 1)])
            assert len(pending) == 2 * NT
            otb = cbuf[:].rearrange("p a s -> p (a s)")
            ota = lambda i: otb[:, (i % 6) * D:(i % 6 + 1) * D]
            otk = lambda i: ("ot", i % 6)

            def tail_units(i):
                pending.pop(0)()
                pending.pop(0)()

            pipeline([
                tail_units,
                lambda i: (ln_stats_act(yres[:, i, :], ("yres", i), i % 8, ubuf[:, 0, 0:D], "ujunk") if i % 2 == 0
                           else ln_stats(yres[:, i, :], ("yres", i), i % 8)),
                lambda i: ln_apply(yres[:, i, :], ("yres", i), i % 8),
                lambda i: T.op(DVE, lambda: nc.vector.tensor_tensor(out=ota(i), in0=yres[:, i, :], in1=lnv2[:, 0, :], op=ALU.mult),
                               reads=[("yres", i), "lnx"], writes=[otk(i)]),
                lambda i: T.op(POOL, lambda: nc.gpsimd.tensor_tensor(out=ota(i), in0=ota(i), in1=lnv2[:, 1, :], op=ALU.add),
                               reads=[otk(i), "lnx"], writes=[otk(i)]),
                lambda i: T.dma(SP, d_out[i % 6], y_d[i * P:(i + 1) * P, :], ota(i), reads=[otk(i)]),
            ], NT)
            T.barrier(ALL)
    return nc


def _prep_shared(inp):
    f = np.float32
    c = lambda a: np.ascontiguousarray(a, dtype=f)
    w_in = np.asarray(inp["w_in"])[0]
    w_in_r = c(w_in.reshape(KC, P, 3136).transpose(1, 0, 2))
    kr = w_in_r[:, :, 3072:3136]
    w_krsw = c(np.concatenate([kr[:, :, 32:64], kr[:, :, 0:32]], axis=2))
    w_qb = np.asarray(inp["w_q_b"])[0]
    w_qb_r = c(w_qb.reshape(3, P, 768).transpose(1, 0, 2))
    sw = []
    for h in range(4):
        r = w_qb_r[:, :, h * 192 + 128:h * 192 + 192]
        sw.append(np.concatenate([r[:, :, 32:64], r[:, :, 0:32]], axis=2))
    w_qbsw = c(np.concatenate(sw, axis=2))
    w_kvb = np.asarray(inp["w_kv_b"])[0].reshape(P, 4, 256)
    w_kvk = c(w_kvb[:, :, 0:128].reshape(P, 512))
    w_kvv = c(w_kvb[:, :, 128:256].reshape(P, 512))
    w_out_r = c(np.asarray(inp["w_out"])[0].reshape(KC, P, D).transpose(1, 0, 2))
    w_up = np.asarray(inp["w_up"])[0].reshape(KC, P, 2, NJ, P)
    w_up_r = c(w_up.transpose(3, 1, 0, 2, 4).reshape(NJ, P, KC, 256))
    w_dn_r = c(np.asarray(inp["w_down"])[0].reshape(NJ, P, D).transpose(1, 0, 2))
    cols = np.zeros((P, NCOL), f)
    cols[:, 0:3] = np.asarray(inp["q_a_norm_g"])[0].reshape(3, P).T
    cols[:, 3] = np.asarray(inp["kv_a_norm_g"])[0]
    cols[:, 4] = np.asarray(inp["attn_norm_g"])[0]
    cols[:, 5] = np.asarray(inp["hg_norm_g"])[0]
    half = 32
    inv_freq = (1.0 / (10000.0 ** (np.arange(half, dtype=np.float32) / np.float32(half)))).astype(f)
    cols[0:64, 6] = np.concatenate([inv_freq, inv_freq])
    cols[0:32, 7] = -1.0
    cols[32:64, 7] = 1.0
    lbr = np.zeros((P, 16), f)
    for d, nm in enumerate(("lb_fwd", "lb_bwd")):
        a = np.asarray(inp[nm]).reshape(2, 4, P)
        lbr[:, d * 8:(d + 1) * 8] = a.transpose(2, 0, 1).reshape(P, 8)
    lnv = c(np.stack([np.asarray(inp["ln_in_g"]), np.asarray(inp["ln_in_b"]), np.asarray(inp["ln1_g"])[0], np.asarray(inp["ln1_b"])[0],
                      np.asarray(inp["ln2_g"])[0], np.asarray(inp["ln2_b"])[0]]))
    lnc = c(lnv.reshape(6, KC, P).transpose(2, 0, 1).reshape(P, 48))
    gqkv = c(np.concatenate([np.asarray(inp["q_a_norm_g"])[0], np.asarray(inp["kv_a_norm_g"])[0]])[None])
    cw = np.asarray(inp["conv_w"])[0].reshape(3, 2, NJ, P)
    cb = np.asarray(inp["conv_b"])[0].reshape(1, 2, NJ, P)
    convp = c(np.concatenate([cw, cb], axis=0).transpose(3, 2, 1, 0).reshape(P, NJ * 8))
    return {"cols": cols, "lbr": lbr, "lnv": lnv, "lnc": lnc, "gqkv": gqkv, "convp": convp, "w_in_r": w_in_r, "w_krsw": w_krsw, "w_qb_r": w_qb_r,
            "w_qbsw": w_qbsw, "w_kvk": w_kvk, "w_kvv": w_kvv, "w_out_r": w_out_r, "w_up_r": w_up_r, "w_dn_r": w_dn_r}


def make_in_maps(inp, n=8):
    shared = _prep_shared(inp)
    x = np.asarray(inp["x"], dtype=np.float32)
    pos = np.asarray(inp["positions"], dtype=np.int32)
    maps = []
    for b in range(n):
        m = dict(shared)
        m["x"] = np.ascontiguousarray(x[b])
        m["pos"] = np.ascontiguousarray(pos[b][None])
        maps.append(m)
    return maps


def kernel(**inputs):
    nc = build_nc()
    in_maps = make_in_maps(inputs, 8)
    res = run_bass_kernel_spmd(nc, in_maps, core_ids=list(range(8)))
    return np.stack([np.asarray(r["y"], dtype=np.float32) for r in res.results], axis=0)
```
